# Optimizing a Trainium2 kernel written in Bass

```python
import math
import jax, jax.numpy as jnp
from jax import lax
import numpy as np

D_MODEL = 1024
BATCH = 4
SEQ = 4096
DEPTH = 2

GRID_W = 64
CTX_LEN = 256
Q_BLOCK = 128
ROPE_BASE = 10000.0
NORM_EPS = 1e-6
SUBLN_EPS = 1e-5

A_HEADS = 4
A_DIM = 64
A_WIDTH = A_HEADS * 2 * A_DIM
A_COLS = 3 * A_WIDTH
B_HEADS = 4
B_NOPE = 64
B_ROPE = 32
B_VDIM = 64
B_Q_RANK = 192
B_KV_RANK = 128
B_WIDTH = B_HEADS * B_VDIM
B_COLS = B_Q_RANK + B_KV_RANK + B_ROPE
C_HEADS = 4
C_DIM = 64
C_WIDTH = C_HEADS * C_DIM
C_DECAY_LORA = 64
C_ICLR_LORA = 64
C_GATE_LORA = 128
C_GN_EPS = 64e-5
C_COLS = 3 * C_WIDTH + 2 * C_DECAY_LORA + 2 * C_ICLR_LORA + C_GATE_LORA

D_MIX = A_WIDTH + B_WIDTH + C_WIDTH
N_IN = A_COLS + B_COLS + C_COLS
IN_SPLITS = [A_COLS, A_COLS + B_COLS]
C_SPLITS = [int(s) for s in np.cumsum([C_WIDTH, C_WIDTH, C_WIDTH, C_DECAY_LORA, C_DECAY_LORA, C_ICLR_LORA, C_ICLR_LORA])]

D_FF = 256 * ((8 * D_MODEL // 3 + 255) // 256)

kernel_name = "hybrid_dit_diffattn_mla_rwkv7_block"

F32 = jnp.float32


def rmsnorm(x, gain, eps=NORM_EPS):
    x32 = x.astype(F32)
    y = x32 * lax.rsqrt(jnp.mean(x32 * x32, axis=-1, keepdims=True) + eps)
    return (y * gain.astype(F32)).astype(x.dtype)


def shift_prev(x):
    return jnp.pad(x, ((0, 0), (1, 0), (0, 0)))[:, :-1]


def shift_next(x):
    return jnp.pad(x, ((0, 0), (0, 1), (0, 0)))[:, 1:]


def dwconv3(x, w, b):
    return shift_prev(x) * w[0] + x * w[1] + shift_next(x) * w[2] + b


def rope_angles(rows, cols, dim):
    nf = dim // 4
    inv = ROPE_BASE ** (-jnp.arange(nf, dtype=F32) / nf)
    return jnp.concatenate([rows.astype(F32)[:, None] * inv, cols.astype(F32)[:, None] * inv], axis=-1)


def axial_rope(x, ang):
    T, d = x.shape[1], x.shape[-1]
    nf = d // 4
    xs = x.astype(F32).reshape(*x.shape[:-1], 2, 2, nf)
    x1, x2 = xs[..., 0, :], xs[..., 1, :]
    a = ang.reshape(T, 1, 2, nf)
    cos, sin = jnp.cos(a), jnp.sin(a)
    out = jnp.stack([x1 * cos - x2 * sin, x2 * cos + x1 * sin], axis=-2)
    return out.reshape(x.shape).astype(x.dtype)


def over_query_blocks(fn, *qs):
    Bt, T = qs[0].shape[:2]
    nb = T // Q_BLOCK
    split = lambda a: jnp.moveaxis(a.reshape(Bt, nb, Q_BLOCK, *a.shape[2:]), 1, 0)
    out = lax.map(lambda qb: fn(*qb), tuple(split(q) for q in qs))
    return jnp.moveaxis(out, 0, 1).reshape(Bt, T, *out.shape[3:])


def softmax32(s):
    return jax.nn.softmax(s.astype(F32), axis=-1)


def diff_qkv(pa, ang):
    Bt, T = pa.shape[:2]
    q, k, v = jnp.split(pa, 3, axis=-1)
    q = q.reshape(Bt, T, 2 * A_HEADS, A_DIM)
    k = k.reshape(Bt, T, 2 * A_HEADS, A_DIM)
    if ang is not None:
        q = axial_rope(q, ang)
        k = axial_rope(k, ang)
    q = q.reshape(Bt, T, A_HEADS, 2, A_DIM)
    k = k.reshape(Bt, T, A_HEADS, 2, A_DIM)
    v = v.reshape(Bt, T, A_HEADS, 2 * A_DIM)
    return q[..., 0, :], q[..., 1, :], k[..., 0, :], k[..., 1, :], v


def diff_attn(q1, q2, k1, k2, v, lam):
    scale = A_DIM ** -0.5
    p1 = softmax32(jnp.einsum('bqhd,bkhd->bhqk', q1, k1) * scale)
    p2 = softmax32(jnp.einsum('bqhd,bkhd->bhqk', q2, k2) * scale)
    a = (p1 - lam * p2).astype(v.dtype)
    return jnp.einsum('bhqk,bkhe->bqhe', a, v)


def mla_qkv(pb, P, ang):
    Bt, T = pb.shape[:2]
    cq, ckv, kr = jnp.split(pb, [B_Q_RANK, B_Q_RANK + B_KV_RANK], axis=-1)
    q = (rmsnorm(cq, P['b_q_norm_g']) @ P['b_w_q_up']).reshape(Bt, T, B_HEADS, B_NOPE + B_ROPE)
    kv = (rmsnorm(ckv, P['b_kv_norm_g']) @ P['b_w_kv_up']).reshape(Bt, T, B_HEADS, B_NOPE + B_VDIM)
    q_nope, q_rope = q[..., :B_NOPE], q[..., B_NOPE:]
    k_nope, v = kv[..., :B_NOPE], kv[..., B_NOPE:]
    kr = kr[:, :, None, :]
    if ang is not None:
        q_rope = axial_rope(q_rope, ang)
        kr = axial_rope(kr, ang)
    k = jnp.concatenate([k_nope, jnp.broadcast_to(kr, (Bt, T, B_HEADS, B_ROPE))], axis=-1)
    q = jnp.concatenate([q_nope, q_rope], axis=-1)
    return q, k, v


def softmax_attn(q, k, v):
    scale = (B_NOPE + B_ROPE) ** -0.5
    p = softmax32(jnp.einsum('bqhd,bkhd->bhqk', q, k) * scale).astype(v.dtype)
    return jnp.einsum('bhqk,bkhe->bqhe', p, v)


def rwkv7_scan(S0, r, w, k, v, a, b, reverse):
    def step(S, inp):
        r_t, w_t, k_t, v_t, a_t, b_t = inp
        sa = jnp.einsum('bhvk,bhk->bhv', S, a_t)
        S = S * w_t[:, :, None, :] + sa[..., None] * b_t[:, :, None, :] + v_t[..., None] * k_t[:, :, None, :]
        return S, jnp.einsum('bhvk,bhk->bhv', S, r_t)
    xs = tuple(jnp.moveaxis(t.astype(F32), 1, 0) for t in (r, w, k, v, a, b))
    S, ys = lax.scan(step, S0, xs, reverse=reverse)
    return S, jnp.moveaxis(ys, 0, 1)


def rwkv7_mix(pc, S0_f, S0_b, P, want_out):
    Bt, T, _ = pc.shape
    pc = pc + P['c_mu_prev'] * (shift_prev(pc) - pc) + P['c_mu_next'] * (shift_next(pc) - pc)
    r, k, v, wl_f, wl_b, al_f, al_b, gl = jnp.split(pc, C_SPLITS, axis=-1)
    heads = lambda t: t.reshape(Bt, T, C_HEADS, C_DIM)
    kk = heads(k * P['c_k_k']).astype(F32)
    kk = kk / jnp.maximum(jnp.linalg.norm(kk, axis=-1, keepdims=True), 1e-12)
    r_h, v_h, k_h = heads(r), heads(v), heads(k)
    k_a = P['c_k_a'].reshape(C_HEADS, C_DIM)
    states, ys, bonuses = [], [], []
    for d, (wl, al, S0, rev) in enumerate(((wl_f, al_f, S0_f, False), (wl_b, al_b, S0_b, True))):
        w = -jax.nn.softplus(-(P['c_w0'][d] + jnp.tanh(wl) @ P['c_w2'][d])) - 0.5
        decay = jnp.exp(-jnp.exp(w.astype(F32)))
        a = heads(jax.nn.sigmoid(P['c_a0'][d] + al @ P['c_a2'][d]))
        kd = k_h * (1 + (a - 1) * k_a)
        S, y = rwkv7_scan(S0, r_h, heads(decay), kd, v_h, -kk, kk * a, rev)
        states.append(S)
        ys.append(y)
        bonuses.append(jnp.sum(r_h * kd * P['c_r_k'], axis=-1, keepdims=True) * v_h)
    if not want_out:
        return None, states[0], states[1]
    y = ys[0] + ys[1]
    mu = jnp.mean(y, axis=-1, keepdims=True)
    var = jnp.mean(jnp.square(y - mu), axis=-1, keepdims=True)
    yn = ((y - mu) * lax.rsqrt(var + C_GN_EPS)).reshape(Bt, T, C_WIDTH) * P['c_gn_g'] + P['c_gn_b']
    yn = yn + (bonuses[0] + bonuses[1]).reshape(Bt, T, C_WIDTH)
    out = yn * (jax.nn.sigmoid(gl) @ P['c_g2'])
    return out.astype(pc.dtype), states[0], states[1]


def merge_groups(oa, ob, oc, P, lam_init):
    Bt, T = oa.shape[:2]
    oa = (rmsnorm(oa, P['a_subln_g'], SUBLN_EPS) * (1 - lam_init)).reshape(Bt, T, A_WIDTH)
    ob = ob.reshape(Bt, T, B_WIDTH)
    return jnp.concatenate([oa, ob, oc], axis=-1) @ P['w_out']


def token_mix(h, hc, P, lam_init, ang_a, ang_b, ctx_out):
    p = h @ P['w_in']
    pc = hc @ P['w_in']
    pa, pb, pr = jnp.split(p, IN_SPLITS, axis=-1)
    pca, pcb, pcr = jnp.split(pc, IN_SPLITS, axis=-1)
    q1, q2, k1, k2, v = diff_qkv(pa, ang_a)
    cq1, cq2, ck1, ck2, cv = diff_qkv(pca, None)
    lam = (jnp.exp(jnp.sum(P['lam_q1'] * P['lam_k1']).astype(F32))
           - jnp.exp(jnp.sum(P['lam_q2'] * P['lam_k2']).astype(F32)) + lam_init)
    K1 = jnp.concatenate([ck1, k1], axis=1)
    K2 = jnp.concatenate([ck2, k2], axis=1)
    V = jnp.concatenate([cv, v], axis=1)
    oa = over_query_blocks(lambda a_, b_: diff_attn(a_, b_, K1, K2, V, lam), q1, q2)
    qb, kb, vb = mla_qkv(pb, P, ang_b)
    cqb, ckb, cvb = mla_qkv(pcb, P, None)
    Kb = jnp.concatenate([ckb, kb], axis=1)
    Vb = jnp.concatenate([cvb, vb], axis=1)
    ob = over_query_blocks(lambda q_: softmax_attn(q_, Kb, Vb), qb)
    zero = jnp.zeros((h.shape[0], C_HEADS, C_DIM, C_DIM), F32)
    ocr, S_f, S_b = rwkv7_mix(pcr, zero, zero, P, ctx_out)
    orr, _, _ = rwkv7_mix(pr, S_f, S_b, P, True)
    o = merge_groups(oa, ob, orr, P, lam_init)
    if not ctx_out:
        return o, None
    oc = merge_groups(diff_attn(cq1, cq2, ck1, ck2, cv, lam), softmax_attn(cqb, ckb, cvb), ocr, P, lam_init)
    return o, oc


def conv_ffn(h, P):
    u = dwconv3(h @ P['ffn_w_up'], P['ffn_conv_w'], P['ffn_conv_b'])
    g, v = jnp.split(u, 2, axis=-1)
    return (jax.nn.silu(g) * v) @ P['ffn_w_down']


def trunk_layer(x, xc, c_act, cc_act, P, lam_init, ang_a, ang_b, ctx_out):
    mod = (c_act @ P['ada_w'] + P['ada_b'])[:, None, :]
    modc = (cc_act @ P['ada_w'] + P['ada_b'])[None, None, :]
    sh1, sc1, gt1, sh2, sc2, gt2 = jnp.split(mod, 6, axis=-1)
    csh1, csc1, cgt1, csh2, csc2, cgt2 = jnp.split(modc, 6, axis=-1)
    h = rmsnorm(x, P['mix_pre_g']) * (1 + sc1) + sh1
    hc = rmsnorm(xc, P['mix_pre_g']) * (1 + csc1) + csh1
    o, oc = token_mix(h, hc, P, lam_init, ang_a, ang_b, ctx_out)
    x = x + gt1 * rmsnorm(o, P['mix_post_g'])
    h = rmsnorm(x, P['ffn_pre_g']) * (1 + sc2) + sh2
    x = x + gt2 * rmsnorm(conv_ffn(h, P), P['ffn_post_g'])
    if ctx_out:
        xc = xc + cgt1 * rmsnorm(oc, P['mix_post_g'])
        hc = rmsnorm(xc, P['ffn_pre_g']) * (1 + csc2) + csh2
        xc = xc + cgt2 * rmsnorm(conv_ffn(hc, P), P['ffn_post_g'])
    return x, xc


def setup_inputs(seed: int = 0) -> dict:
    key = jax.random.key(seed)
    ks = iter(jax.random.split(key, 48))
    nrm = lambda shape, s: jax.random.normal(next(ks), shape, F32) * s
    uni = lambda shape, lo, hi: jax.random.uniform(next(ks), shape, F32, lo, hi)
    L = DEPTH
    return {
        "x": nrm((BATCH, SEQ, D_MODEL), 1.0),
        "c": nrm((BATCH, D_MODEL), 1.0),
        "ctx": nrm((BATCH, CTX_LEN, D_MODEL), 1.0),
        "c_ctx": nrm((D_MODEL,), 1.0),
        "ada_w": nrm((L, D_MODEL, 6 * D_MODEL), 0.5 * D_MODEL ** -0.5),
        "ada_b": nrm((L, 6 * D_MODEL), 0.01),
        "mix_pre_g": 1.0 + nrm((L, D_MODEL), 0.02),
        "mix_post_g": 1.0 + nrm((L, D_MODEL), 0.02),
        "ffn_pre_g": 1.0 + nrm((L, D_MODEL), 0.02),
        "ffn_post_g": 1.0 + nrm((L, D_MODEL), 0.02),
        "w_in": nrm((L, D_MODEL, N_IN), D_MODEL ** -0.5),
        "w_out": nrm((L, D_MIX, D_MODEL), D_MIX ** -0.5),
        "lam_q1": nrm((L, A_DIM), 0.1),
        "lam_k1": nrm((L, A_DIM), 0.1),
        "lam_q2": nrm((L, A_DIM), 0.1),
        "lam_k2": nrm((L, A_DIM), 0.1),
        "a_subln_g": 1.0 + nrm((L, 2 * A_DIM), 0.02),
        "b_q_norm_g": 1.0 + nrm((L, B_Q_RANK), 0.02),
        "b_w_q_up": nrm((L, B_Q_RANK, B_HEADS * (B_NOPE + B_ROPE)), B_Q_RANK ** -0.5),
        "b_kv_norm_g": 1.0 + nrm((L, B_KV_RANK), 0.02),
        "b_w_kv_up": nrm((L, B_KV_RANK, B_HEADS * (B_NOPE + B_VDIM)), B_KV_RANK ** -0.5),
        "c_mu_prev": uni((L, C_COLS), 0.0, 0.5),
        "c_mu_next": uni((L, C_COLS), 0.0, 0.5),
        "c_w0": uni((L, 2, C_WIDTH), -5.0, -1.0),
        "c_w2": nrm((L, 2, C_DECAY_LORA, C_WIDTH), 0.5 * C_DECAY_LORA ** -0.5),
        "c_a0": nrm((L, 2, C_WIDTH), 0.5),
        "c_a2": nrm((L, 2, C_ICLR_LORA, C_WIDTH), 0.5 * C_ICLR_LORA ** -0.5),
        "c_g2": nrm((L, C_GATE_LORA, C_WIDTH), C_GATE_LORA ** -0.5),
        "c_k_k": 0.85 + nrm((L, C_WIDTH), 0.05),
        "c_k_a": 1.0 + nrm((L, C_WIDTH), 0.05),
        "c_r_k": nrm((L, C_HEADS, C_DIM), 0.1),
        "c_gn_g": 1.0 + nrm((L, C_WIDTH), 0.02),
        "c_gn_b": nrm((L, C_WIDTH), 0.01),
        "ffn_w_up": nrm((L, D_MODEL, 2 * D_FF), D_MODEL ** -0.5),
        "ffn_conv_w": nrm((L, 3, 2 * D_FF), 0.5),
        "ffn_conv_b": nrm((L, 2 * D_FF), 0.01),
        "ffn_w_down": nrm((L, D_FF, D_MODEL), D_FF ** -0.5),
    }


def reference(x, c, ctx, c_ctx, ada_w, ada_b, mix_pre_g, mix_post_g, ffn_pre_g, ffn_post_g,
              w_in, w_out, lam_q1, lam_k1, lam_q2, lam_k2, a_subln_g,
              b_q_norm_g, b_w_q_up, b_kv_norm_g, b_w_kv_up,
              c_mu_prev, c_mu_next, c_w0, c_w2, c_a0, c_a2, c_g2, c_k_k, c_k_a, c_r_k, c_gn_g, c_gn_b,
              ffn_w_up, ffn_conv_w, ffn_conv_b, ffn_w_down):
    T = x.shape[1]
    ROWS = T // GRID_W
    rows = jnp.repeat(jnp.arange(ROWS, dtype=jnp.int32), GRID_W)
    cols = jnp.tile(jnp.arange(GRID_W, dtype=jnp.int32), ROWS)
    ang_a = rope_angles(rows, cols, A_DIM)
    ang_b = rope_angles(rows, cols, B_ROPE)
    c_act = jax.nn.silu(c)
    cc_act = jax.nn.silu(c_ctx)
    stacked = dict(
        ada_w=ada_w, ada_b=ada_b, mix_pre_g=mix_pre_g, mix_post_g=mix_post_g,
        ffn_pre_g=ffn_pre_g, ffn_post_g=ffn_post_g, w_in=w_in, w_out=w_out,
        lam_q1=lam_q1, lam_k1=lam_k1, lam_q2=lam_q2, lam_k2=lam_k2, a_subln_g=a_subln_g,
        b_q_norm_g=b_q_norm_g, b_w_q_up=b_w_q_up, b_kv_norm_g=b_kv_norm_g, b_w_kv_up=b_w_kv_up,
        c_mu_prev=c_mu_prev, c_mu_next=c_mu_next, c_w0=c_w0, c_w2=c_w2, c_a0=c_a0, c_a2=c_a2,
        c_g2=c_g2, c_k_k=c_k_k, c_k_a=c_k_a, c_r_k=c_r_k, c_gn_g=c_gn_g, c_gn_b=c_gn_b,
        ffn_w_up=ffn_w_up, ffn_conv_w=ffn_conv_w, ffn_conv_b=ffn_conv_b, ffn_w_down=ffn_w_down)
    xc = ctx
    for i in range(DEPTH):
        P = {name: w[i] for name, w in stacked.items()}
        lam_init = 0.8 - 0.6 * math.exp(-0.3 * i)
        x, xc = trunk_layer(x, xc, c_act, cc_act, P, lam_init, ang_a, ang_b, i < DEPTH - 1)
    return x
```

```python
import math
from contextlib import ExitStack
import numpy as np
import concourse.bass as bass
import concourse.mybir as mybir
from concourse.bass_utils import run_bass_kernel_spmd

F32 = mybir.dt.float32
BF16 = mybir.dt.bfloat16
AF = mybir.ActivationFunctionType
ALU = mybir.AluOpType

D = 1024
L = 2
NLAT = 2048
NCTX = 256
NTOK = NLAT + NCTX
NT = NTOK // 128
NCOL = NTOK + 1
HALO = NLAT
CTX0 = NLAT + 1
NKEY = 4096 + NCTX
NKT = NKEY // 128
N_IN = 3040
DFF = 2816
NCH = DFF // 128
GROUPS = [[0, 1], [2, 3], [4, 5], [6, 7]]
NORM_EPS = 1e-6
SUBLN_EPS = 1e-5
GN_EPS = 64e-5
DECAY_C = -math.exp(-0.5)
RB = 256


def col(t):
    return t * 128 if t < 16 else CTX0 + (t - 16) * 128


def _mk(lst):
    d, o = {}, 0
    for n, w in lst:
        d[n] = (o, w)
        o += w
    return d, o


PC_OFF, NPC = _mk([("adab", 32), ("preg1", 8), ("preg2", 8), ("qng", 2), ("kvng", 1), ("mup", 9), ("mun", 9),
                   ("a0", 4), ("kk", 2), ("ka", 2), ("rk", 2), ("cw0", 44), ("cw1", 44), ("cw2", 44), ("cb", 44),
                   ("sel", 2)])
ROW_OFF, NROW = _mk([("adab_g1", 1024), ("adab_g2", 1024), ("postg1", 1024), ("postg2", 1024), ("lam", 256),
                     ("subln", 128), ("w0", 512), ("gng", 256), ("gnb", 256)])
CF_OFF, NCF = _mk([("ident", 128), ("ones", 128), ("blk", 128), ("RA", 128), ("RB", 128), ("RK", 32), ("hsel", 2)])
CR_OFF, NCR = _mk([("Us", 128), ("Ui", 128), ("Ls", 128), ("Li", 128), ("trif", 384), ("trib", 384)])


class Res:
    __slots__ = ("name", "w", "rs", "excl")

    def __init__(self, name, excl=False):
        self.name = name
        self.w = None
        self.rs = {}
        self.excl = excl


class Eng:
    def __init__(self, key, sem):
        self.key = key
        self.sem = sem
        self.count = 0
        self.items = []
        self.waited = {}


class FW:
    NDS = 6

    def __init__(self, nc):
        self.nc = nc
        self.engs = {k: Eng(k, nc.alloc_semaphore("es_" + k)) for k in ("pe", "act", "dve", "pool", "sp")}
        self.dsems = {q: [[nc.alloc_semaphore("ds_%s%d" % (q, i)), 0] for i in range(self.NDS)] for q in ("sp", "pool")}
        self.drr = {"sp": 0, "pool": 0}
        self.cctoks = []
        self.nres = 0
        self.same_engine_sync = True
        self.ninst = 0
        self.muted = False

    def res(self, name=None):
        self.nres += 1
        return Res(name or ("r%d" % self.nres))

    def pres(self):
        self.nres += 1
        return Res("p%d" % self.nres, excl=True)

    def _wait(self, e, tok):
        sem, val = tok
        if e.waited.get(sem, 0) >= val:
            return
        e.waited[sem] = val
        e.items.append(("w", sem, val))

    def _deps(self, r, w, own=None):
        deps = []
        for x in r:
            if x.w is not None:
                deps.append(x.w)
            if x.excl:
                for s, v in x.rs.items():
                    if s is not own:
                        deps.append((s, v))
        for x in w:
            if x.w is not None:
                deps.append(x.w)
            for s, v in x.rs.items():
                deps.append((s, v))
        return deps

    def _update(self, tok, r, w):
        for x in w:
            x.w = tok
            x.rs = {}
        for x in r:
            if x in w:
                continue
            if x.rs.get(tok[0], 0) < tok[1]:
                x.rs[tok[0]] = tok[1]

    def op(self, ek, fn, r=(), w=()):
        if self.muted:
            return None
        e = self.engs[ek]
        for tok in self._deps(r, w, e.sem):
            if tok[0] is e.sem and (ek == "pe" or not self.same_engine_sync):
                continue
            self._wait(e, tok)
        e.count += 1
        self.ninst += 1
        tok = (e.sem, e.count)
        e.items.append(("o", fn))
        self._update(tok, r, w)
        return tok

    def dma(self, q, out, in_, r=(), w=(), **kw):
        if self.muted:
            return None
        e = self.engs[q]
        for tok in self._deps(r, w):
            self._wait(e, tok)
        slot = self.dsems[q][self.drr[q]]
        self.drr[q] = (self.drr[q] + 1) % self.NDS
        sem, val = slot
        if val > 0:
            self._wait(e, (sem, val))
        slot[1] = val + 16
        tok = (sem, val + 16)
        self.ninst += 1
        e.items.append(("d", out, in_, sem, kw))
        self._update(tok, r, w)
        return tok

    def collective(self, in_ap, out_ap, r=(), w=()):
        if self.muted:
            return None
        e = self.engs["pool"]
        for tok in self._deps(r, w):
            self._wait(e, tok)
        sem = self.nc.alloc_semaphore("cc%d" % len(self.cctoks))
        tok = (sem, 1)
        e.items.append(("c", in_ap, out_ap, sem))
        self.cctoks.append(tok)
        self._update(tok, r, w)
        return tok

    def barrier(self):
        toks = [(e.sem, e.count) for e in self.engs.values() if e.count > 0]
        for q in self.dsems:
            toks += [(s, v) for s, v in self.dsems[q] if v > 0]
        toks += self.cctoks
        for e in self.engs.values():
            for tok in toks:
                self._wait(e, tok)

    def flush(self):
        nc = self.nc
        hmap = {"pe": "tensor", "act": "scalar", "dve": "vector", "pool": "gpsimd", "sp": "sync"}
        with nc.Block() as block:
            for k, e in self.engs.items():
                items = e.items
                e.items = []

                def f(h, items=items, e=e):
                    for it in items:
                        if it[0] == "w":
                            h.wait_ge(it[1], it[2])
                        elif it[0] == "o":
                            it[1](h).then_inc(e.sem, 1)
                        elif it[0] == "d":
                            h.dma_start(out=it[1], in_=it[2], **it[4]).then_inc(it[3], 16)
                        else:
                            h.collective_compute("AllGather", ALU.bypass, replica_groups=GROUPS,
                                                 ins=[it[1]], outs=[it[2]]).then_inc(it[3], 1)
                getattr(block, hmap[k])(f)


def build_program(nlayers=L, dbg=None, do_rwkv=True, do_attn=True, do_ffn=True, do_cc=True, do_p=True, do_merge=True):
    dbg = dbg or {}
    dbgl = dbg.get("_layer", 0)
    pstop = dbg.get("_pstop", 0)

    def STAGE(k):
        if pstop == k:
            fw.muted = True

    nc = bass.Bass("TRN2", target_bir_lowering=False)
    fw = FW(nc)
    op, dma = fw.op, fw.dma

    def MM(out, lhsT, rhs, start, stop, r, w):
        op("pe", lambda e: e.matmul(out, lhsT=lhsT, rhs=rhs, start=start, stop=stop), r, w)

    def TR(out, in_, ident, r, w):
        op("pe", lambda e: e.transpose(out=out, in_=in_, identity=ident), r, w)

    def ACT(out, in_, func, r, w, bias=None, scale=None, accum_out=None, eng="act"):
        kw = {}
        if bias is not None:
            kw["bias"] = bias
        if scale is not None:
            kw["scale"] = scale
        if accum_out is not None:
            kw["accum_out"] = accum_out
        op(eng, lambda e: e.activation(out=out, in_=in_, func=func, **kw), r, w)

    def TT(out, in0, in1, alu, r, w, eng="dve"):
        op(eng, lambda e: e.tensor_tensor(out=out, in0=in0, in1=in1, op=alu), r, w)

    def TS(out, in0, s1, op0, r, w, s2=None, op1=None, eng="dve"):
        if op1 is None:
            op(eng, lambda e: e.tensor_scalar(out=out, in0=in0, scalar1=s1, scalar2=None, op0=op0), r, w)
        else:
            op(eng, lambda e: e.tensor_scalar(out=out, in0=in0, scalar1=s1, scalar2=s2, op0=op0, op1=op1), r, w)

    def STT(out, in0, scalar, in1, op0, op1, r, w, eng="dve"):
        eng = "dve"
        op(eng, lambda e: e.scalar_tensor_tensor(out=out, in0=in0, scalar=scalar, in1=in1, op0=op0, op1=op1), r, w)

    def CP(out, in_, r, w, eng="dve"):
        if eng == "act":
            op("act", lambda e: e.copy(out=out, in_=in_), r, w)
        else:
            op(eng, lambda e: e.tensor_copy(out=out, in_=in_), r, w)

    def RECIP(out, in_, r, w):
        op("dve", lambda e: e.reciprocal(out=out, in_=in_), r, w)

    def MSET(ap, v, w, eng="pool"):
        op(eng, lambda e: e.memset(ap, v), (), w)

    def CC(snd, rcv, R_s, R_r):
        if do_cc:
            for q_ in fw.dsems:
                for s_, v_ in fw.dsems[q_]:
                    if v_ > 0:
                        fw._wait(fw.engs["pool"], (s_, v_))
            fw.collective(snd.opt(), rcv.opt(), r=[R_s], w=[R_r])
        else:
            nr = snd.shape[0]
            dma("sp", rcv[0:nr, :], snd[:, :], r=[R_s], w=[R_r])
            dma("sp", rcv[nr:2 * nr, :], snd[:, :], r=[R_s], w=[R_r])

    def din(name, shape, dt=F32):
        return nc.dram_tensor(name, list(shape), dt, kind="ExternalInput").ap()

    def dscr(name, shape, dt=F32):
        return nc.dram_tensor(name, list(shape), dt).ap()

    xin = din("xin", [NTOK, D])
    cvec = din("cvec", [128, 16])
    ada_w = din("ada_w", [L, D, 6 * D])
    w_in = din("w_in", [L, D, N_IN])
    w_out = din("w_out", [L, D, D])
    ffn_up = din("ffn_up", [L, D, 2 * DFF])
    ffn_dn = din("ffn_dn", [L, DFF, D])
    wq_up = din("wq_up", [L, 192, 384])
    wkv_up = din("wkv_up", [L, 128, 512])
    cw2 = din("cw2", [L, 128, 256])
    ca2 = din("ca2", [L, 128, 256])
    cg2 = din("cg2", [L, 128, 256])
    pcd = din("pc", [L, 128, NPC])
    rowd = din("row", [L, NROW])
    cfd = din("cf", [128, NCF])
    crd = din("cr", [128, NCR])
    ropeA = din("ropeA", [2, 128, NCOL])
    ropeB = din("ropeB", [2, 96, NCOL])
    yout = nc.dram_tensor("yout", [NLAT, D], F32, kind="ExternalOutput").ap()
    dbg_out = {}
    for k, shp in dbg.items():
        if k.startswith("_"):
            continue
        dbg_out[k] = nc.dram_tensor("dbg_" + k, list(shp), F32, kind="ExternalOutput").ap()

    grow = dscr("grow", [L * 4, D])
    qA_d = dscr("qA_d", [4 * 128, NTOK], BF16)
    qB_d = dscr("qB_d", [4 * 96, NTOK], BF16)
    kctx_d = dscr("kctx_d", [896, NCTX], BF16)
    snd1A = [dscr("snd1A%d" % i, [256, NLAT], BF16) for i in range(2)]
    rcv1A = [dscr("rcv1A%d" % i, [512, NLAT], BF16) for i in range(2)]
    snd1B = [dscr("snd1B%d" % i, [192, NLAT], BF16) for i in range(2)]
    rcv1B = [dscr("rcv1B%d" % i, [384, NLAT], BF16) for i in range(2)]
    snd2 = [dscr("snd2_%d" % i, [512, 768], BF16) for i in range(4)]
    rcv2 = [dscr("rcv2_%d" % i, [1024, 768], BF16) for i in range(4)]
    vctx_d = dscr("vctx_d", [NCTX, 768], BF16)
    pcT_d = dscr("pcT_d", [1152, NCOL])
    snd3 = dscr("snd3", [128, 128])
    rcv3 = dscr("rcv3", [256, 128])
    snd4 = dscr("snd4", [128, 8])
    rcv4 = dscr("rcv4", [256, 8])
    snd5 = dscr("snd5", [128, 8])
    rcv5 = dscr("rcv5", [256, 8])
    wup_bf = dscr("wup_bf", [D, 2 * DFF], BF16)
    wdn_bf = dscr("wdn_bf", [DFF, D], BF16)
    R_grow, R_qA, R_qB, R_kctx, R_snd1, R_rcv1 = [fw.res() for _ in range(6)]
    R_snd2, R_rcv2, R_vctx, R_pcT = [fw.res() for _ in range(4)]
    R_snd3, R_rcv3, R_snd4, R_rcv4, R_snd5, R_rcv5 = [fw.res() for _ in range(6)]
    R_wup, R_wdn = fw.res(), fw.res()

    glob = ExitStack()

    uniq = [0]

    def sb(st, name, shape, dt=F32):
        uniq[0] += 1
        return st.enter_context(nc.sbuf_tensor("%s_u%d" % (name, uniq[0]), list(shape), dt))

    def pst(st, name, shape, dt=F32):
        uniq[0] += 1
        return st.enter_context(nc.psum_tensor("%s_u%d" % (name, uniq[0]), list(shape), dt))

    X = sb(glob, "X", [128, NT, D])
    CF = sb(glob, "CF", [128, NCF])
    IDB = sb(glob, "IDB", [128, 128], BF16)
    ONB = sb(glob, "ONB", [128, 128], BF16)
    MOD = sb(glob, "MOD", [128, L * 4 * 2 * 8])
    PCS = sb(glob, "PCS", [128, L * NPC])
    EPSB = sb(glob, "EPSB", [128, 4])
    R_X = [fw.res("X%d" % t) for t in range(NT)]
    R_CF, R_IDB, R_MOD, R_PCS = fw.res(), fw.res(), fw.res(), fw.res()

    def cf(name, p0=0, p1=128, c0=0, c1=None):
        o, w = CF_OFF[name]
        c1 = w if c1 is None else c1
        return CF[p0:p1, o + c0:o + c1]

    def pc(l, name, c0=0, c1=None, p0=0, p1=128):
        o, w = PC_OFF[name]
        c1 = w if c1 is None else c1
        return PCS[p0:p1, l * NPC + o + c0:l * NPC + o + c1]

    def modv(l, which, s):
        o = ((l * 4 + which) * 2 + s) * 8
        return MOD[:, o:o + 8]

    def row_b(l, name, c0=0, c1=None, parts=128):
        o, w = ROW_OFF[name]
        c1 = w if c1 is None else c1
        return rowd[l, o + c0:o + c1].partition_broadcast(parts)

    EPS_T = {NORM_EPS: EPSB[:, 0:1], SUBLN_EPS: EPSB[:, 1:2], GN_EPS: EPSB[:, 2:3], 1e-24: EPSB[:, 3:4]}

    dma("sp", CF[:, :], cfd[:, :], w=[R_CF])
    for l in range(L):
        dma("sp", PCS[:, l * NPC:(l + 1) * NPC], pcd[l, :, :], w=[R_PCS])
    xv = xin.rearrange("(t p) f -> p t f", p=128)
    for t in range(NT):
        dma("sp", X[:, t, :], xv[:, t, :], w=[R_X[t]])
    CP(IDB[:, :], cf("ident"), [R_CF], [R_IDB])
    CP(ONB[:, :], cf("ones"), [R_CF], [R_IDB])
    for i, v in enumerate((NORM_EPS, SUBLN_EPS, GN_EPS, 1e-24)):
        MSET(EPSB[:, i:i + 1], v, [R_CF])

    def rstd_from_ss(RS, R_RS, n, eps):
        ACT(RS, RS, AF.Sqrt, [R_RS, R_CF], [R_RS], bias=EPS_T[eps], scale=1.0 / n)
        RECIP(RS, RS, [R_RS], [R_RS])

    with ExitStack() as st:
        ACTF = sb(st, "ACTF", [128, 16])
        ACTT = sb(st, "ACTT", [128, 8, 2], BF16)
        ACTB = sb(st, "ACTB", [128, 2, 8, 128], BF16)
        ADW = [sb(st, "ADW%d" % i, [128, 8, 1024], BF16) for i in range(2)]
        MRAW = sb(st, "MRAW", [128, 4, 2, 8])
        GB = sb(st, "GB", [128, 1024])
        GP = sb(st, "GP", [128, 1024])
        GO = [sb(st, "GO%d" % i, [128, 512]) for i in range(2)]
        psA = [pst(st, "psA%d" % i, [128, 512]) for i in range(4)]
        R_ACTF, R_ACTT, R_ACTB, R_MRAW, R_GB, R_GP = [fw.res() for _ in range(6)]
        R_ADW = [fw.res(), fw.res()]
        R_GO = [fw.res(), fw.res()]
        R_ps = [fw.pres() for _ in range(4)]
        dma("sp", ACTF[:, :], cvec[:, :], w=[R_ACTF])
        ACT(ACTF[:, :], ACTF[:, :], AF.Silu, [R_ACTF], [R_ACTF])
        for s in range(2):
            CP(ACTT[:, :, s], ACTF[:, s * 8:(s + 1) * 8], [R_ACTF], [R_ACTT])
            for kc in range(8):
                TS(ACTB[:, s, kc, :], cf("ones"), ACTF[:, s * 8 + kc:s * 8 + kc + 1], ALU.mult, [R_ACTF, R_CF], [R_ACTB])
        nload = 0
        for l in range(nlayers):
            awv = ada_w[l].rearrange("(kc p) n -> p kc n", p=128)
            for j in range(6):
                bi = nload % 2
                nload += 1
                for kc in range(8):
                    dma("pool", ADW[bi][:, kc, :], awv[:, kc, j * 1024:(j + 1) * 1024], w=[R_ADW[bi]])
                if j in (0, 1, 3, 4):
                    blk = {0: 0, 1: 1, 3: 2, 4: 3}[j]
                    pb = psA[blk % 2]
                    for fc in range(8):
                        for kc in range(8):
                            MM(pb[:, 2 * fc:2 * fc + 2], ADW[bi][:, kc, fc * 128:(fc + 1) * 128], ACTT[:, kc, :],
                               kc == 0, kc == 7, [R_ADW[bi], R_ACTT], [R_ps[blk % 2]])
                    for s in range(2):
                        TT(MRAW[:, blk, s, :], pb[:, s:16:2], pc(l, "adab", blk * 8, blk * 8 + 8), ALU.add,
                           [R_ps[blk % 2], R_PCS], [R_MRAW])
                else:
                    gate = 0 if j == 2 else 1
                    dma("sp", GB[:, :], row_b(l, "adab_g1" if gate == 0 else "adab_g2"), w=[R_GB])
                    dma("sp", GP[:, :], row_b(l, "postg1" if gate == 0 else "postg2"), w=[R_GP])
                    for s in range(2):
                        for hf in range(2):
                            pi = 2 + hf
                            for kc in range(8):
                                MM(psA[pi][:, :], ACTB[:, s, kc, :], ADW[bi][:, kc, hf * 512:(hf + 1) * 512],
                                   kc == 0, kc == 7, [R_ADW[bi], R_ACTB], [R_ps[pi]])
                            TT(GO[hf][:, :], psA[pi][:, :], GB[:, hf * 512:(hf + 1) * 512], ALU.add,
                               [R_ps[pi], R_GB], [R_GO[hf]])
                            TT(GO[hf][:, :], GO[hf][:, :], GP[:, hf * 512:(hf + 1) * 512], ALU.mult, [R_GP], [R_GO[hf]])
                            gr = (l * 2 + s) * 2 + gate
                            dma("sp", grow[gr:gr + 1, hf * 512:(hf + 1) * 512], GO[hf][0:1, :], r=[R_GO[hf]], w=[R_grow])
            for s in range(2):
                for (which, sc_blk, sh_blk, gname) in ((0, 1, 0, "preg1"), (2, 3, 2, "preg2")):
                    STT(modv(l, which, s), MRAW[:, sc_blk, s, :], 1.0, pc(l, gname), ALU.add, ALU.mult,
                        [R_MRAW, R_PCS], [R_MOD])
                    CP(modv(l, which + 1, s), MRAW[:, sh_blk, s, :], [R_MRAW], [R_MOD])
        fw.barrier()
        fw.flush()
    if "mod" in dbg_out:
        dma("sp", dbg_out["mod"], MOD[:, :], r=[R_MOD])

    def norm_transpose(l, which, tiles, HT, R_HT, psb, R_psb, XS, R_XS, RSTD, R_RSTD, JUNK, R_JUNK):
        for t in tiles:
            ACT(JUNK[:, :], X[:, t, :], AF.Square, [R_X[t]], [R_JUNK, R_RSTD], accum_out=RSTD[:, t:t + 1])
        rstd_from_ss(RSTD[:, tiles[0]:tiles[-1] + 1], R_RSTD, D, NORM_EPS)
        for n_, t in enumerate(tiles):
            s = 0 if t < 16 else 1
            xb = n_ % 2
            TS(XS[xb][:, :], X[:, t, :], RSTD[:, t:t + 1], ALU.mult, [R_X[t], R_RSTD], [R_XS[xb]])
            for half in range(2):
                pi = (n_ * 2 + half) % len(psb)
                for j in range(4):
                    fc = half * 4 + j
                    TR(psb[pi][:, j * 128:(j + 1) * 128], XS[xb][:, fc * 128:(fc + 1) * 128], cf("ident"),
                       [R_XS[xb], R_CF], [R_psb[pi]])
                for j in range(4):
                    fc = half * 4 + j
                    ACT(HT[:, fc, col(t):col(t) + 128], psb[pi][:, j * 128:(j + 1) * 128], AF.Identity,
                        [R_psb[pi], R_MOD], [R_HT], scale=modv(l, which, s)[:, fc:fc + 1],
                        bias=modv(l, which + 1, s)[:, fc:fc + 1])

    def halo_exchange(st, HT, R_HT, R_HTh, snd, rcv, R_snd, R_rcv, l, tag):
        HS = sb(st, "HS" + tag, [128, 8])
        HR = sb(st, "HR" + tag, [128, 2, 8])
        R_HS, R_HR = fw.res(), fw.res()
        CP(HS[:, :], HT[:, :, NLAT - 1], [R_HT], [R_HS])
        dma("sp", snd[:, :], HS[:, :], r=[R_HS], w=[R_snd])
        CC(snd, rcv, R_snd, R_rcv)
        dma("sp", HR[:, :, :], rcv.rearrange("(r p) c -> p r c", p=128), r=[R_rcv], w=[R_HR])
        TS(HS[:, :], HR[:, 0, :], pc(l, "sel", 0, 1), ALU.mult, [R_HR, R_PCS], [R_HS])
        STT(HS[:, :], HR[:, 1, :], pc(l, "sel", 1, 2), HS[:, :], ALU.mult, ALU.add, [R_HR, R_PCS], [R_HS])
        CP(HT[:, :, HALO], HS[:, :], [R_HS], [R_HTh])

    oall_d = dscr("oall_d", [NTOK, D], BF16)
    R_oall = fw.res()
    ys_d = dscr("ys_d", [NTOK, 256])
    R_ysd_t = [fw.res() for _ in range(NT)]
    AXX = mybir.AxisListType.X

    def RWKV_PHASE(l, ctx_out, tiles_out):
        with ExitStack() as st:
            CR = sb(st, "CR", [128, NCR])
            R_CR = fw.res()
            dma("sp", CR[:, :], crd[:, :], w=[R_CR])

            def cr(name, c0=0, c1=None):
                o, w = CR_OFF[name]
                c1 = w if c1 is None else c1
                return CR[:, o + c0:o + c1]

            BS = sb(st, "BS", [128, NT, 4])
            R_BS = fw.res()
            W2 = sb(st, "W2", [128, 256])
            A2 = sb(st, "A2", [128, 256])
            G2w = sb(st, "G2w", [128, 256], BF16)
            W0R = sb(st, "W0R", [128, 256])
            R_W0R = fw.res()
            GNG = sb(st, "GNG", [128, 256])
            GNB = sb(st, "GNB", [128, 256])
            MUC = sb(st, "MUC", [128, 9])
            OMKA = sb(st, "OMKA", [128, 2])
            DI2 = sb(st, "DI2", [128, 64])
            R_W = fw.res()
            dma("sp", W2[:, :], cw2[l, :, :], w=[R_W])
            dma("sp", A2[:, :], ca2[l, :, :], w=[R_W])
            dma("pool", G2w[:, :], cg2[l, :, :], w=[R_W])
            dma("sp", GNG[:, :], row_b(l, "gng"), w=[R_W])
            dma("sp", GNB[:, :], row_b(l, "gnb"), w=[R_W])
            TT(MUC[:, :], pc(l, "mup"), pc(l, "mun"), ALU.add, [R_PCS], [R_W])
            TS(MUC[:, :], MUC[:, :], -1.0, ALU.mult, [], [R_W], s2=1.0, op1=ALU.add)
            TS(OMKA[:, :], pc(l, "ka"), -1.0, ALU.mult, [R_PCS], [R_W], s2=1.0, op1=ALU.add)
            TT(DI2[:, :], cf("ident", c0=0, c1=64), cf("ident", c0=64, c1=128), ALU.add, [R_CF], [R_W])
            Hb = [sb(st, "Hb%d" % i, [128, 2, 64]) for i in range(2)]
            R_H = [fw.res(), fw.res()]
            HR = sb(st, "HRr", [128, 2, 128])
            R_HR = fw.res()
            PCB = sb(st, "PCB", [128, 9, 130])
            SH = sb(st, "SH", [128, 9, 128])
            TW = sb(st, "TW", [128, 128])
            AA = sb(st, "AA", [128, 2, 128])
            LW = sb(st, "LW", [128, 256])
            KS = sb(st, "KS", [128, 2, 128])
            KK = sb(st, "KK", [128, 2, 128])
            TMP = sb(st, "TMPr", [128, 2, 128])
            SQf = TMP
            RN = TMP
            KD = sb(st, "KD", [128, 2, 128])
            BB = sb(st, "BB", [128, 2, 128])
            BT = sb(st, "BT", [128, 2, 128])
            KTt = sb(st, "KTt", [128, 2, 128])
            BH = sb(st, "BH", [128, 2, 128])
            KH = sb(st, "KH", [128, 2, 128])
            BTm = sb(st, "BTm", [128, 2, 2, 128])
            KTm = sb(st, "KTm", [128, 2, 2, 128])
            SGs = [sb(st, "SG%d" % i, [128, 128], BF16) for i in range(2)]
            EEs = [sb(st, "EE%d" % i, [128, 2, 4, 128]) for i in range(2)]
            ARs = [sb(st, "AR%d" % i, [128, 2, 2, 128]) for i in range(2)]
            TOKs = [sb(st, "TOK%d" % i, [128, 2, 4, 128]) for i in range(2)]
            NMSs = [sb(st, "NMS%d" % i, [128, 4, 3, 128]) for i in range(2)]
            X0s = [sb(st, "X0_%d" % i, [128, 4, 128]) for i in range(2)]
            XT0s = [sb(st, "XT0_%d" % i, [128, 4, 128]) for i in range(2)]
            Z0s = [sb(st, "Z0_%d" % i, [128, 4, 128]) for i in range(2)]
            YLs = [sb(st, "YL%d" % i, [128, 256]) for i in range(2)]
            R_SGs, R_EEs, R_ARs, R_TOKs, R_NMSs, R_X0s, R_XT0s, R_Z0s, R_YLs = [[fw.res(), fw.res()] for _ in range(9)]
            TOKm = sb(st, "TOKm", [128, 2, 2, 2, 128])
            XB = [sb(st, "XB%d" % i, [128, 4, 128]) for i in range(2)]
            XTB = [sb(st, "XTB%d" % i, [128, 4, 128]) for i in range(2)]
            ZBf = [sb(st, "ZBf%d" % i, [128, 4, 128]) for i in range(2)]
            AN = sb(st, "AN", [128, 4, 128])
            WU = sb(st, "WUr", [128, 4, 128])
            WUm = sb(st, "WUm", [128, 2, 4, 128])
            WRT = sb(st, "WRT", [128, 2, 128])
            PTS = sb(st, "PTS", [128, 2, 2, 64])
            WRTms = [sb(st, "WRTm%d" % i, [128, 2, 2, 128]) for i in range(2)]
            YVs = [sb(st, "YV%d" % i, [128, 4, 64]) for i in range(2)]
            PTSms = [sb(st, "PTSm%d" % i, [128, 2, 2, 2, 64]) for i in range(2)]
            QSs = [sb(st, "QS%d" % i, [128, 2, 2, 64]) for i in range(2)]
            V2s = [sb(st, "V2_%d" % i, [128, 2, 128]) for i in range(2)]
            GTs = [sb(st, "GT%d" % i, [128, 256], BF16) for i in range(2)]
            R_WRTms, R_YVs, R_PTSms, R_QSs, R_V2s, R_GTs = [[fw.res(), fw.res()] for _ in range(6)]
            YSQ = sb(st, "YSQ", [128, 256])
            YN = sb(st, "YN", [128, 256])
            OC = [sb(st, "OC%d" % i, [128, 256], BF16) for i in range(2)]
            STt = sb(st, "STt", [128, 12])
            (R_PCB, R_SH, R_TW, R_AA, R_LW, R_KS, R_KK, R_TMP, R_KD, R_BB, R_BT, R_KTt,
             R_BH, R_KH, R_BTm, R_KTm, R_TOKm, R_AN, R_WU, R_WUm, R_WRT, R_PTS,
             R_YSQ, R_YN, R_ST) = [fw.res() for _ in range(25)]
            R_SQf = R_TMP
            R_RN = R_TMP
            R_XB, R_XTB, R_ZBf, R_OC = [[fw.res(), fw.res()] for _ in range(4)]
            bk = [pst(st, "psR%d" % i, [128, 512]) for i in range(8)]
            R_bk = [fw.pres() for _ in range(8)]
            b0, bCH, bA, bB, bI1, bI2, bI3, bPQ = bk
            R_b0, R_bCH, R_bA, R_bB, R_bI1, R_bI2, R_bI3, R_bPQ = R_bk
            b1, R_b1 = b0, R_b0
            pcv = pcT_d.rearrange("(c p) n -> p c n", p=128)
            ident = cf("ident")
            hsel = [cf("hsel", c0=0, c1=1), cf("hsel", c0=1, c1=2)]

            def v3(ap, a):
                return ap.rearrange("p (a b) -> p a b", a=a)

            def MASK(out, in_, hcol, r, w):
                ACT(out, in_, AF.Identity, r + [R_CF], w, scale=hcol)

            def stage_a(d, t, final, pp):
                EE, AR, TOK, NMS, SG = EEs[pp], ARs[pp], TOKs[pp], NMSs[pp], SGs[pp]
                R_EE, R_AR, R_TOK, R_NMS, R_SG = R_EEs[pp], R_ARs[pp], R_TOKs[pp], R_NMSs[pp], R_SGs[pp]
                X0, XT0, Z0 = X0s[pp], XT0s[pp], Z0s[pp]
                R_X0, R_XT0, R_Z0 = R_X0s[pp], R_XT0s[pp], R_Z0s[pp]
                c0 = col(t)
                has_prev = t not in (0, 16)
                has_next = t != 17
                a_ = c0 - 1 if has_prev else c0
                b_ = c0 + 129 if has_next else c0 + 128
                dma("sp", PCB[:, :, a_ - (c0 - 1):b_ - (c0 - 1)], pcv[:, :, a_:b_], r=[R_pcT], w=[R_PCB])
                if not has_prev:
                    MSET(PCB[:, :, 0:1], 0.0, [R_PCB])
                if not has_next:
                    MSET(PCB[:, :, 129:130], 0.0, [R_PCB])
                nch = 9 if final else 8
                for c in range(nch):
                    TS(SH[:, c, :], PCB[:, c, 1:129], MUC[:, c:c + 1], ALU.mult, [R_PCB, R_W], [R_SH], eng="pool")
                    STT(SH[:, c, :], PCB[:, c, 0:128], pc(l, "mup", c, c + 1), SH[:, c, :], ALU.mult, ALU.add,
                        [R_PCB, R_PCS], [R_SH])
                    STT(SH[:, c, :], PCB[:, c, 2:130], pc(l, "mun", c, c + 1), SH[:, c, :], ALU.mult, ALU.add,
                        [R_PCB, R_PCS], [R_SH])
                ACT(TW[:, :], SH[:, 6, :], AF.Tanh, [R_SH], [R_TW])
                if final:
                    ACT(SG[:, :], SH[:, 8, :], AF.Sigmoid, [R_SH], [R_SG])
                r0 = 64 * d
                for hp in range(2):
                    MM(b0[:, hp * 128:(hp + 1) * 128], A2[r0:r0 + 64, hp * 128:(hp + 1) * 128], SH[r0:r0 + 64, 7, :], True, True,
                       [R_W, R_SH], [R_b0])
                MM(b0[:, 256:512], TW[r0:r0 + 64, :], W2[r0:r0 + 64, :], True, True, [R_TW, R_W], [R_b0])
                for hp in range(2):
                    ACT(AA[:, hp, :], b0[:, hp * 128:(hp + 1) * 128], AF.Sigmoid, [R_b0, R_PCS], [R_AA],
                        bias=pc(l, "a0", d * 2 + hp, d * 2 + hp + 1))
                TT(LW[:, :], b0[:, 256:512], W0R[:, :], ALU.add, [R_b0, R_W0R], [R_LW])
                ACT(LW[:, :], LW[:, :], AF.Sigmoid, [], [R_LW])
                tri = cr("trif") if d == 0 else cr("trib")
                for hp in range(2):
                    MM(b1[:, 0:384], LW[:, hp * 128:(hp + 1) * 128], tri, True, True, [R_LW, R_CR], [R_b1])
                    ACT(EE[:, hp, 0:2, :], v3(b1[:, 0:256], 2), AF.Exp, [R_b1], [R_EE])
                    ACT(EE[:, hp, 2, :], b1[:, 0:128], AF.Exp, [R_b1], [R_EE], scale=-1.0)
                    ACT(EE[:, hp, 3, :], b1[:, 256:384], AF.Exp, [R_b1], [R_EE])
                for hp in range(2):
                    TS(KS[:, hp, :], SH[:, 2 + hp, :], pc(l, "kk", hp, hp + 1), ALU.mult, [R_SH, R_PCS], [R_KS])
                ACT(SQf[:, :, :], KS[:, :, :], AF.Square, [R_KS], [R_SQf])
                MM(bA[:, 0:256], cf("blk"), SQf[:, :, :], True, True, [R_CF, R_SQf], [R_bA])
                TS(RN[:, :, :], v3(bA[:, 0:256], 2), 1e-24, ALU.max, [R_bA], [R_RN])
                ACT(RN[:, :, :], RN[:, :, :], AF.Sqrt, [], [R_RN])
                RECIP(RN[:, :, :], RN[:, :, :], [], [R_RN])
                TT(KK[:, :, :], KS[:, :, :], RN[:, :, :], ALU.mult, [R_KS, R_RN], [R_KK])
                for hp in range(2):
                    TS(TMP[:, hp, :], AA[:, hp, :], pc(l, "ka", hp, hp + 1), ALU.mult, [R_AA, R_PCS, R_W], [R_TMP],
                       s2=OMKA[:, hp:hp + 1], op1=ALU.add)
                TT(KD[:, :, :], TMP[:, :, :], SH[:, 2:4, :], ALU.mult, [R_TMP, R_SH], [R_KD])
                TT(BB[:, :, :], KK[:, :, :], AA[:, :, :], ALU.mult, [R_KK, R_AA], [R_BB], eng="pool")
                TT(TMP[:, :, :], SH[:, 0:2, :], KD[:, :, :], ALU.mult, [R_SH, R_KD], [R_TMP])
                for hp in range(2):
                    TS(TMP[:, hp, :], TMP[:, hp, :], pc(l, "rk", hp, hp + 1), ALU.mult, [R_PCS], [R_TMP])
                for hp in range(2):
                    MM(bB[:, 256 + hp * 2:256 + hp * 2 + 2], TMP[:, hp, :], cf("hsel"), True, True, [R_TMP, R_CF], [R_bB])
                if d == 0:
                    CP(BS[:, t, :], bB[:, 256:260], [R_bB], [R_BS])
                else:
                    TT(BS[:, t, :], bB[:, 256:260], BS[:, t, :], ALU.add, [R_bB], [R_BS])
                STT(AR[:, :, 0, :], KK[:, :, :], -1.0, EE[:, :, 1, :], ALU.mult, ALU.mult, [R_KK, R_EE], [R_AR])
                TT(AR[:, :, 1, :], SH[:, 0:2, :], EE[:, :, 0, :], ALU.mult, [R_SH, R_EE], [R_AR], eng="pool")
                TT(BT[:, :, :], BB[:, :, :], EE[:, :, 2, :], ALU.mult, [R_BB, R_EE], [R_BT])
                TT(KTt[:, :, :], KD[:, :, :], EE[:, :, 2, :], ALU.mult, [R_KD, R_EE], [R_KTt], eng="pool")
                TT(BH[:, :, :], BB[:, :, :], EE[:, :, 3, :], ALU.mult, [R_BB, R_EE], [R_BH])
                TT(KH[:, :, :], KD[:, :, :], EE[:, :, 3, :], ALU.mult, [R_KD, R_EE], [R_KH], eng="pool")
                for hh in range(2):
                    MASK(BTm[:, hh, :, :], BT[:, :, :], hsel[hh], [R_BT], [R_BTm])
                    MASK(KTm[:, hh, :, :], KTt[:, :, :], hsel[hh], [R_KTt], [R_KTm])
                for hp in range(2):
                    srcs = [AR[:, hp, 0, :], BH[:, hp, :], KH[:, hp, :], SH[:, 4 + hp, :]]
                    for q, s_ in enumerate(srcs):
                        TR(b0[:, q * 128:(q + 1) * 128], s_, ident, [R_AR, R_BH, R_KH, R_SH, R_CF], [R_b0])
                    CP(TOK[:, hp, :, :], v3(b0[:, :], 4), [R_b0], [R_TOK])
                if d == 0:
                    m_si, m_s, m_i, m_x = cr("Us", 0, 256), cr("Us"), cr("Ui"), cr("Ls")
                else:
                    m_si, m_s, m_i, m_x = cr("Ls", 0, 256), cr("Ls"), cr("Li"), cr("Us")
                for h in range(4):
                    hp, hh = h // 2, h % 2
                    MM(bA[:, 0:256], BTm[:, hh, hp, :], AR[:, hp, :, :], True, True, [R_BTm, R_AR], [R_bA])
                    MM(bB[:, 0:256], KTm[:, hh, hp, :], AR[:, hp, :, :], True, True, [R_KTm, R_AR], [R_bB])
                    TT(XT0[:, h, :], bA[:, 0:128], m_s, ALU.mult, [R_bA, R_CR], [R_XT0])
                    TT(NMS[:, h, 0, :], bA[:, 128:256], m_i, ALU.mult, [R_bA, R_CR], [R_NMS])
                    TR(bA[:, 256:384], XT0[:, h, :], ident, [R_XT0, R_CF], [R_bA])
                    CP(X0[:, h, :], bA[:, 256:384], [R_bA], [R_X0], eng="act")
                    TT(NMS[:, h, 1:3, :], v3(bB[:, 0:256], 2), v3(m_si, 2), ALU.mult, [R_bB, R_CR], [R_NMS])
                    TT(Z0[:, h, :], XT0[:, h, :], ident, ALU.add, [R_XT0, R_CF], [R_Z0], eng="pool")

            def stage_b(d, t, pp, final):
                EE, AR, TOK, NMS = EEs[pp], ARs[pp], TOKs[pp], NMSs[pp]
                R_EE, R_AR, R_TOK, R_NMS = R_EEs[pp], R_ARs[pp], R_TOKs[pp], R_NMSs[pp]
                WRTm, YV, PTSm, QS = WRTms[pp], YVs[pp], PTSms[pp], QSs[pp]
                R_WRTm, R_YV, R_PTSm, R_QS = R_WRTms[pp], R_YVs[pp], R_PTSms[pp], R_QSs[pp]
                if d == 1:
                    dma("sp", YLs[pp][:, :], ys_d[t * 128:(t + 1) * 128, :], r=[R_ysd_t[t]], w=[R_YLs[pp]])
                if final:
                    for hp in range(2):
                        CP(V2s[pp][:, hp, :], TOK[:, hp, 3, :], [R_TOK], [R_V2s[pp]], eng="pool")
                    MM(bPQ[:, 0:256], SGs[pp][:, :], G2w[:, :], True, True, [R_SGs[pp], R_W], [R_bPQ])
                    CP(GTs[pp][:, :], bPQ[:, 0:256], [R_bPQ], [R_GTs[pp]], eng="act")
                for c in range(2):
                    MASK(TOKm[:, c, :, :, :], TOK[:, :, 1:3, :], hsel[c], [R_TOK], [R_TOKm])
                Xc, XTc, Zc = X0s[pp], XT0s[pp], Z0s[pp]
                R_Xc, R_XTc, R_Zc = R_X0s[pp], R_XT0s[pp], R_Z0s[pp]
                for m in range(5):
                    nx = m % 2
                    for h in range(4):
                        MM(bI1[:, h * 128:(h + 1) * 128], XTc[:, h, :], Xc[:, h, :], True, True, [R_XTc, R_Xc], [R_bI1])
                    if m < 4:
                        for h in range(4):
                            MM(bI2[:, h * 128:(h + 1) * 128], Xc[:, h, :], XTc[:, h, :], True, True, [R_XTc, R_Xc], [R_bI2])
                    CP(XB[nx][:, :, :], v3(bI1[:, :], 4), [R_bI1], [R_XB[nx]])
                    if m < 4:
                        CP(XTB[nx][:, :, :], v3(bI2[:, :], 4), [R_bI2], [R_XTB[nx]], eng="act")
                    for h in range(4):
                        MM(bI3[:, h * 128:(h + 1) * 128], XB[nx][:, h, :], Zc[:, h, :], True, True, [R_XB[nx], R_Zc], [R_bI3])
                    TT(ZBf[nx][:, :, :], v3(bI3[:, :], 4), Zc[:, :, :], ALU.add, [R_bI3, R_Zc], [R_ZBf[nx]])
                    Xc, XTc, Zc = XB[nx], XTB[nx], ZBf[nx]
                    R_Xc, R_XTc, R_Zc = R_XB[nx], R_XTB[nx], R_ZBf[nx]
                ZF, R_ZF = Zc, R_Zc
                for h in range(4):
                    hp, hh = h // 2, h % 2
                    MM(bI1[:, h * 64:(h + 1) * 64], NMS[:, h, 1, :], TOK[:, hp, 3, hh * 64:(hh + 1) * 64], True, True,
                       [R_NMS, R_TOK], [R_bI1])
                CP(AN[:, :, 64:128], v3(bI1[:, 0:256], 4), [R_bI1], [R_AN])
                for hp in range(2):
                    CP(AN[:, 2 * hp:2 * hp + 2, 0:64], v3(TOK[:, hp, 0, :], 2), [R_TOK], [R_AN], eng="act")
                for h in range(4):
                    MM(bI2[:, h * 128:(h + 1) * 128], ZF[:, h, :], AN[:, h, :], True, True, [R_ZF, R_AN], [R_bI2])
                CP(WU[:, :, :], v3(bI2[:, :], 4), [R_bI2], [R_WU])
                for c in range(2):
                    MASK(WUm[:, c, :, :], WU[:, :, :], hsel[c], [R_WU], [R_WUm])
                for h in range(4):
                    hp, hh = h // 2, h % 2
                    pb = 64 * hh
                    MM(bI3[pb:pb + 64, hp * 128:(hp + 1) * 128], WU[:, h, 0:64], NMS[:, h, 0, :], True, True,
                       [R_WU, R_NMS], [R_bI3])
                TT(WRT[:, :, :], v3(bI3[:, 0:256], 2), AR[:, :, 1, :], ALU.add, [R_bI3, R_AR], [R_WRT])
                for hh in range(2):
                    MASK(WRTm[:, hh, :, :], WRT[:, :, :], hsel[hh], [R_WRT], [R_WRTm])
                for h in range(4):
                    hp, hh = h // 2, h % 2
                    MM(bI1[:, 256 + h * 64:256 + (h + 1) * 64], NMS[:, h, 0, :], WU[:, h, 64:128], True, False,
                       [R_NMS, R_WU], [R_bI1])
                    MM(bI1[:, 256 + h * 64:256 + (h + 1) * 64], NMS[:, h, 2, :], TOK[:, hp, 3, hh * 64:(hh + 1) * 64], False, True,
                       [R_NMS, R_TOK], [R_bI1])
                CP(YV[:, :, :], v3(bI1[:, 256:512], 4), [R_bI1], [R_YV])
                for c in range(2):
                    for h in range(4):
                        hp, hh = h // 2, h % 2
                        pb = 64 * hh
                        o_ = (c * 2 + hp) * 64
                        MM(bPQ[pb:pb + 64, o_:o_ + 64], WUm[:, c, h, 0:64], TOK[:, hp, 1, hh * 64:(hh + 1) * 64], True, True,
                           [R_WUm, R_TOK], [R_bPQ])
                for c in range(2):
                    colc = c * 64 + (63 if d == 0 else 0)
                    for hp in range(2):
                        o_ = (c * 2 + hp) * 64
                        STT(PTS[:, c, hp, :], DI2[:, :], EE[:, hp, 0, colc:colc + 1], bPQ[:, o_:o_ + 64], ALU.mult, ALU.add,
                            [R_W, R_EE, R_bPQ], [R_PTS])
                for hh in range(2):
                    MASK(PTSm[:, hh, :, :, :], PTS[:, :, :, :], hsel[hh], [R_PTS], [R_PTSm])
                for c in range(2):
                    for h in range(4):
                        hp, hh = h // 2, h % 2
                        pb = 64 * hh
                        o_ = 256 + (c * 2 + hp) * 64
                        MM(bPQ[pb:pb + 64, o_:o_ + 64], TOKm[:, c, hp, 0, hh * 64:(hh + 1) * 64], WU[:, h, 64:128], True, False,
                           [R_TOKm, R_WU], [R_bPQ])
                        MM(bPQ[pb:pb + 64, o_:o_ + 64], TOKm[:, c, hp, 1, hh * 64:(hh + 1) * 64],
                           TOK[:, hp, 3, hh * 64:(hh + 1) * 64], False, True, [R_TOKm, R_TOK], [R_bPQ])
                CP(QS[:, :, :, :], bPQ[:, 256:512].rearrange("p (a b c) -> p a b c", a=2, b=2), [R_bPQ], [R_QS], eng="act")

            def tile_chain(d, t, cur, pp):
                WRTm, YV, PTSm, QS = WRTms[pp], YVs[pp], PTSms[pp], QSs[pp]
                R_WRTm, R_YV, R_PTSm, R_QS = R_WRTms[pp], R_YVs[pp], R_PTSms[pp], R_QSs[pp]
                order = [0, 1] if d == 0 else [1, 0]
                for c in order:
                    cb = 64 * c
                    nx = 1 - cur
                    for h in range(4):
                        hp, hh = h // 2, h % 2
                        MM(bCH[cb:cb + 64, h * 64:(h + 1) * 64], WRTm[:, hh, hp, cb:cb + 64], Hb[cur][:, hp, :], True, True,
                           [R_WRTm, R_H[cur]], [R_bCH])
                    for h in range(4):
                        hp, hh = h // 2, h % 2
                        pb = 64 * hh
                        MM(bCH[pb:pb + 64, 256 + hp * 64:256 + (hp + 1) * 64], PTSm[:, hh, c, hp, :], Hb[cur][:, hp, :], True, True,
                           [R_PTSm, R_H[cur]], [R_bCH])
                    TT(Hb[nx][:, :, :], v3(bCH[:, 256:384], 2), QS[:, c, :, :], ALU.add, [R_bCH, R_QS], [R_H[nx]])
                    cur = nx
                YL, R_YL = YLs[pp], R_YLs[pp]
                if d == 0:
                    TT(v3(YL[:, :], 4), v3(bCH[:, 0:256], 4), YV[:, :, :], ALU.add, [R_bCH, R_YV], [R_YL])
                    dma("sp", ys_d[t * 128:(t + 1) * 128, :], YL[:, :], r=[R_YL], w=[R_ysd_t[t]])
                else:
                    TT(v3(YSQ[:, :], 4), v3(bCH[:, 0:256], 4), YV[:, :, :], ALU.add, [R_bCH, R_YV], [R_YSQ])
                    TT(YL[:, :], YL[:, :], YSQ[:, :], ALU.add, [R_YSQ], [R_YL])
                return cur

            def tile_final(t, n_, pp):
                ob = n_ % 2
                YL, R_YL = YLs[pp], R_YLs[pp]
                op("dve", lambda e: e.tensor_reduce(out=STt[:, 0:4], in_=v3(YL[:, :], 4), axis=AXX, op=ALU.add),
                   [R_YL], [R_ST])
                ACT(YSQ[:, :], YL[:, :], AF.Square, [R_YL], [R_YSQ])
                op("dve", lambda e: e.tensor_reduce(out=STt[:, 4:8], in_=v3(YSQ[:, :], 4), axis=AXX, op=ALU.add),
                   [R_YSQ], [R_ST])
                TS(STt[:, 0:4], STt[:, 0:4], 1.0 / 64, ALU.mult, [], [R_ST])
                TT(STt[:, 8:12], STt[:, 0:4], STt[:, 0:4], ALU.mult, [], [R_ST])
                STT(STt[:, 4:8], STt[:, 4:8], 1.0 / 64, STt[:, 8:12], ALU.mult, ALU.subtract, [], [R_ST])
                ACT(STt[:, 4:8], STt[:, 4:8], AF.Sqrt, [R_CF], [R_ST], bias=EPS_T[GN_EPS], scale=1.0)
                RECIP(STt[:, 4:8], STt[:, 4:8], [], [R_ST])
                for h in range(4):
                    TS(YN[:, h * 64:(h + 1) * 64], YL[:, h * 64:(h + 1) * 64], STt[:, h:h + 1], ALU.subtract, [R_YL, R_ST],
                       [R_YN], s2=STt[:, 4 + h:5 + h], op1=ALU.mult)
                TT(YN[:, :], YN[:, :], GNG[:, :], ALU.mult, [R_W], [R_YN])
                TT(YN[:, :], YN[:, :], GNB[:, :], ALU.add, [R_W], [R_YN])
                for h in range(4):
                    hp, hh = h // 2, h % 2
                    STT(YN[:, h * 64:(h + 1) * 64], V2s[pp][:, hp, hh * 64:(hh + 1) * 64], BS[:, t, h:h + 1],
                        YN[:, h * 64:(h + 1) * 64], ALU.mult, ALU.add, [R_V2s[pp], R_BS], [R_YN])
                TT(OC[ob][:, :], YN[:, :], GTs[pp][:, :], ALU.mult, [R_YN, R_GTs[pp]], [R_OC[ob]])
                dma("sp", oall_d[t * 128:(t + 1) * 128, 768:1024], OC[ob][:, :], r=[R_OC[ob]], w=[R_oall])

            def run_seq(d, tiles, cur, final, n0):
                n_ = n0
                if not tiles:
                    return cur, n_
                dma("sp", W0R[:, :], row_b(l, "w0", d * 256, (d + 1) * 256), w=[R_W0R])
                stage_a(d, tiles[0], final, n_ % 2)
                prev = None
                for i, t in enumerate(tiles):
                    pp = n_ % 2
                    if i + 1 < len(tiles):
                        stage_a(d, tiles[i + 1], final, (n_ + 1) % 2)
                    stage_b(d, t, pp, final)
                    if prev is not None:
                        cur = tile_chain(d, prev[0], cur, prev[1])
                        if final:
                            tile_final(prev[0], prev[2], prev[1])
                    prev = (t, pp, n_)
                    n_ += 1
                cur = tile_chain(d, prev[0], cur, prev[1])
                if final:
                    tile_final(prev[0], prev[2], prev[1])
                return cur, n_

            MSET(Hb[0][:, :, :], 0.0, [R_H[0]])
            cur, n_ = run_seq(0, [16, 17] + list(range(16)), 0, False, 0)
            Hflat = lambda i: Hb[i][:, :, :].rearrange("p a b -> p (a b)")
            dma("sp", snd3[:, :], Hflat(cur), r=[R_H[cur]], w=[R_snd3])
            CC(snd3, rcv3, R_snd3, R_rcv3)
            dma("sp", HR[:, :, :], rcv3.rearrange("(r p) c -> p r c", p=128), r=[R_rcv3], w=[R_HR])
            TS(Hflat(0), HR[:, 0, :], pc(l, "sel", 0, 1), ALU.mult, [R_HR, R_PCS], [R_H[0]])
            STT(Hflat(0), HR[:, 1, :], pc(l, "sel", 1, 2), Hflat(0), ALU.mult, ALU.add, [R_HR, R_PCS], [R_H[0]])
            cur, n_ = run_seq(1, list(range(15, -1, -1)), 0, True, n_)
            if ctx_out:
                MSET(Hb[0][:, :, :], 0.0, [R_H[0]])
                cur, n_ = run_seq(1, [17, 16], 0, True, n_)
            fw.barrier()
            fw.flush()

    TBS = [(i * 512, 512, i * 512) for i in range(4)] + [(CTX0, 256, NLAT)]

    for l in range(nlayers):
        ctx_out = l < L - 1
        lam_init = 0.8 - 0.6 * math.exp(-0.3 * l)
        tiles_all = list(range(NT))
        tiles_out = tiles_all if ctx_out else list(range(16))
        if do_ffn:
            for rblk in range(8):
                for cb_ in range(4):
                    dma("pool", wup_bf[rblk * 128:(rblk + 1) * 128, cb_ * 1408:(cb_ + 1) * 1408],
                        ffn_up[l, rblk * 128:(rblk + 1) * 128, cb_ * 1408:(cb_ + 1) * 1408], w=[R_wup])
            for rblk in range(22):
                dma("pool", wdn_bf[rblk * 128:(rblk + 1) * 128, :], ffn_dn[l, rblk * 128:(rblk + 1) * 128, :], w=[R_wdn])

        with ExitStack() as st:
            if not do_p:
                break
            HT = sb(st, "HT", [128, 8, NCOL], BF16)
            R_HT = fw.res()
            XS = [sb(st, "XS%d" % i, [128, D]) for i in range(2)]
            R_XS = [fw.res(), fw.res()]
            RSTD = sb(st, "RSTD", [128, NT])
            R_RSTD = fw.res()
            WB = [sb(st, "WB%d" % i, [128, 8, 512], BF16) for i in range(2)]
            R_WB = [fw.res(), fw.res()]
            ps = [pst(st, "psP%d" % i, [128, 512]) for i in range(8)]
            R_ps = [fw.pres() for _ in range(8)]
            norm_transpose(l, 0, tiles_all, HT, R_HT, ps[0:4], R_ps[0:4], XS, R_XS, RSTD, R_RSTD, XS[1], R_XS[1])
            STAGE(1)
            R_HTh = fw.res()
            halo_exchange(st, HT, R_HT, R_HTh, snd4, rcv4, R_snd4, R_rcv4, l, "a")
            STAGE(2)
            if "hT" in dbg_out and l == dbgl:
                for fc in range(8):
                    for c0_ in range(0, NCOL, 1024):
                        n_ = min(1024, NCOL - c0_)
                        CP(XS[0][:, 0:n_], HT[:, fc, c0_:c0_ + n_], [R_HT], [R_XS[0]])
                        dma("sp", dbg_out["hT"][fc * 128:(fc + 1) * 128, c0_:c0_ + n_], XS[0][:, 0:n_], r=[R_XS[0]])
            wv = w_in[l].rearrange("(kc p) n -> p kc n", p=128)
            nwl = [0]

            def load_w(c0, c1):
                bi = nwl[0] % 2
                nwl[0] += 1
                for kc in range(8):
                    dma("pool", WB[bi][:, kc, 0:c1 - c0], wv[:, kc, c0:c1], w=[R_WB[bi]])
                return bi

            ROPE = [sb(st, "ROPE%d" % i, [128, 2, 512]) for i in range(2)]
            R_ROPE = [fw.res(), fw.res()]
            QF = [sb(st, "QF%d" % i, [128, 512], BF16) for i in range(2)]
            R_QF = [fw.res(), fw.res()]
            T1 = [sb(st, "T1_%d" % i, [128, 512]) for i in range(2)]
            R_T1 = [fw.res(), fw.res()]
            T2 = [sb(st, "T2_%d" % i, [128, 512]) for i in range(2)]
            R_T2 = [fw.res(), fw.res()]
            QO = [sb(st, "QO%d" % i, [128, NTOK], BF16) for i in range(2)]
            R_QO = [fw.res(), fw.res()]
            RAB = sb(st, "RAB", [128, 288], BF16)
            R_RAB = fw.res()
            CP(RAB[:, 0:128], cf("RA"), [R_CF], [R_RAB])
            CP(RAB[:, 128:256], cf("RB"), [R_CF], [R_RAB])
            CP(RAB[:, 256:288], cf("RK"), [R_CF], [R_RAB])
            cnt = [0]

            def proj_fm(bi, wc0, M, c0, n, pbank, extra=()):
                for kc in range(8):
                    MM(ps[pbank][0:M, 0:n], WB[bi][:, kc, wc0:wc0 + M], HT[:, kc, c0:c0 + n], kc == 0, kc == 7,
                       [R_WB[bi], R_HT] + list(extra), [R_ps[pbank]])

            def rope_block(M, n, pbank, p2, rcol, cos_ap, sin_ap, r_tab, dst_ap, rdst, k):
                CP(QF[k][0:M, 0:n], ps[pbank][0:M, 0:n], [R_ps[pbank]], [R_QF[k]], eng="act")
                STAGE(31)
                MM(ps[p2][0:M, 0:n], RAB[0:M, rcol:rcol + M], QF[k][0:M, 0:n], True, True, [R_RAB, R_QF[k]], [R_ps[p2]])
                STAGE(32)
                import os as _os
                _v = _os.environ.get("TTV", "")
                if _v == "a":
                    TT(T1[k][0:M, 0:n], T2[k][0:M, 0:n], cos_ap, ALU.mult, [R_ps[pbank], r_tab], [R_T1[k]])
                elif _v == "c":
                    TT(T1[k][0:M, 0:n], ps[pbank][0:M, 0:n], T2[k][0:M, 0:n], ALU.mult, [R_ps[pbank], r_tab], [R_T1[k]])
                elif _v == "d":
                    TT(T1[k][0:M, 0:n], ps[pbank][0:M, 0:n], cos_ap, ALU.mult, [R_ps[pbank], r_tab, R_QF[k]], [R_T1[k]])
                else:
                    TT(T1[k][0:M, 0:n], ps[pbank][0:M, 0:n], cos_ap, ALU.mult, [R_ps[pbank], r_tab], [R_T1[k]])
                STAGE(33)
                TT(T2[k][0:M, 0:n], ps[p2][0:M, 0:n], sin_ap, ALU.mult, [R_ps[p2], r_tab], [R_T2[k]])
                STAGE(34)
                TT(dst_ap, T1[k][0:M, 0:n], T2[k][0:M, 0:n], ALU.add, [R_T1[k], R_T2[k]], [rdst], eng="pool")

            for grp in range(2):
                bi = load_w(grp * 512, grp * 512 + 512)
                STAGE(21)
                for hh in range(4):
                    ci = grp * 4 + hh
                    qo = ci % 2
                    for (c0, n, t0) in TBS:
                        k = cnt[0] % 2
                        cnt[0] += 1
                        for i in range(2):
                            dma("sp", ROPE[k][:, i, 0:n], ropeA[i, :, c0:c0 + n], w=[R_ROPE[k]])
                        STAGE(22)
                        proj_fm(bi, hh * 128, 128, c0, n, k)
                        STAGE(23)
                        rope_block(128, n, k, 2 + k, 0, ROPE[k][:, 0, 0:n], ROPE[k][:, 1, 0:n], R_ROPE[k],
                                   QO[qo][:, t0:t0 + n], R_QO[qo], k)
                        STAGE(24)
                    STAGE(25)
                    if grp == 0:
                        dma("sp", qA_d[hh * 128:(hh + 1) * 128, :], QO[qo][:, :], r=[R_QO[qo]], w=[R_qA])
                    else:
                        dma("sp", snd1A[hh // 2][(hh % 2) * 128:(hh % 2 + 1) * 128, :], QO[qo][:, 0:NLAT], r=[R_QO[qo]], w=[R_snd1])
                        dma("sp", kctx_d[hh * 128:(hh + 1) * 128, :], QO[qo][:, NLAT:NTOK], r=[R_QO[qo]], w=[R_kctx])
                    if ("qkA" in dbg_out) and l == dbgl:
                        for (c0, n, t0) in TBS:
                            CP(T1[0][:, 0:n], QO[qo][:, t0:t0 + n], [R_QO[qo]], [R_T1[0]])
                            dma("sp", dbg_out["qkA"][ci * 128:(ci + 1) * 128, t0:t0 + n], T1[0][:, 0:n], r=[R_T1[0]])
            STAGE(3)
            VT = [sb(st, "VT%d" % i, [128, 768], BF16) for i in range(2)]
            R_VT = [fw.res(), fw.res()]
            WQ = sb(st, "WQ", [128, 2, 384], BF16)
            WQF = sb(st, "WQF", [128, 2, 384])
            WKV = sb(st, "WKV", [128, 512], BF16)
            WKVF = sb(st, "WKVF", [128, 512])
            R_WQ, R_WKV, R_WQF, R_WKVF = fw.res(), fw.res(), fw.res(), fw.res()
            MSET(WQF[:, :, :], 0.0, [R_WQF])
            dma("sp", WQF[:, 0, :], wq_up[l, 0:128, :], w=[R_WQF])
            dma("sp", WQF[0:64, 1, :], wq_up[l, 128:192, :], w=[R_WQF])
            dma("sp", WKVF[:, :], wkv_up[l, :, :], w=[R_WKVF])
            for c in range(2):
                TS(WQ[:, c, :], WQF[:, c, :], pc(l, "qng", c, c + 1), ALU.mult, [R_WQF, R_PCS], [R_WQ])
            TS(WKV[:, :], WKVF[:, :], pc(l, "kvng", 0, 1), ALU.mult, [R_WKVF, R_PCS], [R_WKV])
            STAGE(4)
            bV = load_w(1024, 1536)
            bB = load_w(1536, 1888)
            CQ = [sb(st, "CQ0", [128, 2, 512], BF16)] * 2
            SQ = [sb(st, "SQ0", [128, 3, 512], BF16)] * 2
            CKV = [sb(st, "CKV0", [128, 512], BF16)] * 2
            RQ = [sb(st, "RQ0", [128, 2, 512])] * 2
            RPB = [sb(st, "RPB0", [128, 2, 512])] * 2
            RPK = [sb(st, "RPK0", [32, 2, 512])] * 2
            RKT = [sb(st, "RKT%d" % i, [128, 4]) for i in range(2)]
            KR = [sb(st, "KR%d" % i, [32, 512], BF16) for i in range(2)]
            KN = [sb(st, "KN%d" % i, [64, 512], BF16) for i in range(2)]
            QBO = [sb(st, "QBO%d" % i, [128, 512], BF16) for i in range(2)]
            R_CQ, R_SQ, R_CKV, R_RQ, R_RPB, R_RPK = [[fw.res()] * 2 for _ in range(6)]
            R_RKT, R_KR, R_KN, R_QBO = [[fw.res(), fw.res()] for _ in range(4)]
            for bix, (c0, n, t0) in enumerate(TBS):
                k = bix % 2
                nt_ = n // 128
                proj_fm(bB, 0, 128, c0, n, 0)
                proj_fm(bB, 128, 64, c0, n, 1)
                proj_fm(bB, 192, 128, c0, n, 2)
                proj_fm(bB, 320, 32, c0, n, 3)
                CP(CQ[k][:, 0, 0:n], ps[0][:, 0:n], [R_ps[0]], [R_CQ[k]], eng="act")
                CP(CQ[k][0:64, 1, 0:n], ps[1][0:64, 0:n], [R_ps[1]], [R_CQ[k]], eng="act")
                CP(CKV[k][:, 0:n], ps[2][:, 0:n], [R_ps[2]], [R_CKV[k]], eng="act")
                ACT(SQ[k][:, 0, 0:n], ps[0][:, 0:n], AF.Square, [R_ps[0]], [R_SQ[k]])
                ACT(SQ[k][0:64, 1, 0:n], ps[1][0:64, 0:n], AF.Square, [R_ps[1]], [R_SQ[k]])
                ACT(SQ[k][:, 2, 0:n], ps[2][:, 0:n], AF.Square, [R_ps[2]], [R_SQ[k]])
                MM(ps[4][:, 0:n], ONB[:, :], SQ[k][:, 0, 0:n], True, False, [R_IDB, R_SQ[k]], [R_ps[4]])
                MM(ps[4][:, 0:n], ONB[0:64, :], SQ[k][0:64, 1, 0:n], False, True, [R_IDB, R_SQ[k]], [R_ps[4]])
                MM(ps[5][:, 0:n], ONB[:, :], SQ[k][:, 2, 0:n], True, True, [R_IDB, R_SQ[k]], [R_ps[5]])
                ACT(RQ[k][:, 0, 0:n], ps[4][:, 0:n], AF.Sqrt, [R_ps[4], R_CF], [R_RQ[k]], bias=EPS_T[NORM_EPS], scale=1.0 / 192)
                ACT(RQ[k][:, 1, 0:n], ps[5][:, 0:n], AF.Sqrt, [R_ps[5], R_CF], [R_RQ[k]], bias=EPS_T[NORM_EPS], scale=1.0 / 128)
                RECIP(RQ[k][:, :, 0:n], RQ[k][:, :, 0:n], [R_RQ[k]], [R_RQ[k]])
                for tt in range(nt_):
                    MM(ps[6][:, tt:tt + 1], SQ[k][:, 2, tt * 128:(tt + 1) * 128], ONB[:, 0:1], True, True,
                       [R_IDB, R_SQ[k]], [R_ps[6]])
                ACT(RKT[k][:, 0:nt_], ps[6][:, 0:nt_], AF.Sqrt, [R_ps[6], R_CF], [R_RKT[k]], bias=EPS_T[NORM_EPS], scale=1.0 / 128)
                RECIP(RKT[k][:, 0:nt_], RKT[k][:, 0:nt_], [R_RKT[k]], [R_RKT[k]])
                for i in range(2):
                    dma("sp", RPB[k][0:96, i, 0:n], ropeB[i, :, c0:c0 + n], w=[R_RPB[k]])
                    dma("sp", RPK[k][0:32, i, 0:n], ropeB[i, 64:96, c0:c0 + n], w=[R_RPK[k]])
                for i in range(2):
                    TT(RPB[k][0:96, i, 0:n], RPB[k][0:96, i, 0:n], RQ[k][0:96, 0, 0:n], ALU.mult, [R_RQ[k]], [R_RPB[k]])
                rope_block(32, n, 3, 7, 256, RPK[k][0:32, 0, 0:n], RPK[k][0:32, 1, 0:n], R_RPK[k],
                           KR[k][0:32, 0:n], R_KR[k], k)
                for hh in range(4):
                    r0 = 512 + hh * 96 + 64
                    rb = (hh % 2) * 96 + 64
                    if t0 < NLAT:
                        dma("sp", snd1B[hh // 2][rb:rb + 32, t0:t0 + n], KR[k][0:32, 0:n], r=[R_KR[k]], w=[R_snd1])
                    else:
                        dma("sp", kctx_d[r0:r0 + 32, :], KR[k][0:32, 0:n], r=[R_KR[k]], w=[R_kctx])
                for hh in range(4):
                    pb_ = hh % 2
                    MM(ps[pb_][0:96, 0:n], WQ[:, 0, hh * 96:(hh + 1) * 96], CQ[k][:, 0, 0:n], True, False,
                       [R_WQ, R_CQ[k]], [R_ps[pb_]])
                    MM(ps[pb_][0:96, 0:n], WQ[0:64, 1, hh * 96:(hh + 1) * 96], CQ[k][0:64, 1, 0:n], False, True,
                       [R_WQ, R_CQ[k]], [R_ps[pb_]])
                    rope_block(96, n, pb_, 2 + pb_, 128, RPB[k][0:96, 0, 0:n], RPB[k][0:96, 1, 0:n], R_RPB[k],
                               QBO[pb_][0:96, 0:n], R_QBO[pb_], pb_)
                    dma("sp", qB_d[hh * 96:(hh + 1) * 96, t0:t0 + n], QBO[pb_][0:96, 0:n], r=[R_QBO[pb_]], w=[R_qB])
                    MM(ps[4 + pb_][0:64, 0:n], WKV[:, hh * 64:(hh + 1) * 64], CKV[k][:, 0:n], True, True,
                       [R_WKV, R_CKV[k]], [R_ps[4 + pb_]])
                    TT(KN[pb_][0:64, 0:n], ps[4 + pb_][0:64, 0:n], RQ[k][0:64, 1, 0:n], ALU.mult,
                       [R_ps[4 + pb_], R_RQ[k]], [R_KN[pb_]])
                    r0 = 512 + hh * 96
                    rb = (hh % 2) * 96
                    if t0 < NLAT:
                        dma("sp", snd1B[hh // 2][rb:rb + 64, t0:t0 + n], KN[pb_][0:64, 0:n], r=[R_KN[pb_]], w=[R_snd1])
                    else:
                        dma("sp", kctx_d[r0:r0 + 64, :], KN[pb_][0:64, 0:n], r=[R_KN[pb_]], w=[R_kctx])
                for tt in range(nt_):
                    vb = tt % 2
                    cc0 = c0 + tt * 128
                    pv = 4 + vb
                    for kc in range(8):
                        MM(ps[pv][:, :], HT[:, kc, cc0:cc0 + 128], WB[bV][:, kc, 0:512], kc == 0, kc == 7,
                           [R_HT, R_WB[bV]], [R_ps[pv]])
                    CP(VT[vb][:, 0:512], ps[pv][:, :], [R_ps[pv]], [R_VT[vb]], eng="act")
                    MM(ps[7][:, 0:256], CKV[k][:, tt * 128:(tt + 1) * 128], WKV[:, 256:512], True, True,
                       [R_CKV[k], R_WKV], [R_ps[7]])
                    TS(VT[vb][:, 512:768], ps[7][:, 0:256], RKT[k][:, tt:tt + 1], ALU.mult, [R_ps[7], R_RKT[k]], [R_VT[vb]])
                    tk = t0 + tt * 128
                    if tk < NLAT:
                        dma("sp", snd2[tk // 512][tk % 512:tk % 512 + 128, :], VT[vb][:, :], r=[R_VT[vb]], w=[R_snd2])
                    else:
                        dma("sp", vctx_d[tk - NLAT:tk - NLAT + 128, :], VT[vb][:, :], r=[R_VT[vb]], w=[R_vctx])
            STAGE(5)
            for i in range(2):
                CC(snd1A[i], rcv1A[i], R_snd1, R_rcv1)
                CC(snd1B[i], rcv1B[i], R_snd1, R_rcv1)
            for i in range(4):
                CC(snd2[i], rcv2[i], R_snd2, R_rcv2)
            if do_rwkv:
                PCO = [sb(st, "PCO%d" % i, [128, 512]) for i in range(2)]
                R_PCO = [fw.res(), fw.res()]
                TBC = [(i * 512, 512) for i in range(4)] + [(HALO, 1), (CTX0, 256)]
                n_ = 0
                for (w0_, w1_) in ((1888, 2400), (2400, 2912), (2912, 3040)):
                    bi = load_w(w0_, w1_)
                    for cc in range((w1_ - w0_) // 128):
                        crow = (w0_ - 1888) + cc * 128
                        for (c0, n) in TBC:
                            k = n_ % 2
                            n_ += 1
                            proj_fm(bi, cc * 128, 128, c0, n, k, extra=[R_HTh] if n == 1 else [])
                            CP(PCO[k][:, 0:n], ps[k][:, 0:n], [R_ps[k]], [R_PCO[k]], eng="act")
                            dma("sp", pcT_d[crow:crow + 128, c0:c0 + n], PCO[k][:, 0:n], r=[R_PCO[k]], w=[R_pcT],
                                **({"allow_slow_non_contiguous": True} if n == 1 else {}))
            fw.muted = False
            fw.barrier()
            fw.flush()

        if do_attn:
            with ExitStack() as st:
                KT = [sb(st, "KT%d" % i, [128, NKEY], BF16) for i in range(2)]
                V1 = [sb(st, "V1_%d" % i, [128, NKT, 129], BF16) for i in range(2)]
                V1B = [sb(st, "V1B%d" % i, [128, NKT, 65], BF16) for i in range(2)]
                QT = [sb(st, "QT%d" % i, [128, NTOK], BF16) for i in range(2)]
                PT = [sb(st, "PT%d" % i, [128, 512], BF16) for i in range(4)]
                ZB = sb(st, "ZB", [128, 512], BF16)
                OAH = [sb(st, "OAH%d" % i, [128, NT, 128], BF16) for i in range(2)]
                O1 = [sb(st, "O1_%d" % i, [128, 128]) for i in range(4)]
                OCP = sb(st, "OCP", [128, 3, 512])
                R_OCP = fw.res()
                JK = sb(st, "JK", [128, 128])
                RCP = [sb(st, "RCP%d" % i, [128, 4]) for i in range(4)]
                LQ = sb(st, "LQ", [128, 256])
                LS = sb(st, "LS", [128, 4])
                GSUB = sb(st, "GSUB", [128, 128])
                psS = [pst(st, "psS%d" % i, [128, 512]) for i in range(4)]
                psO = [pst(st, "psO%d" % i, [128, 512]) for i in range(3)]
                R_KT, R_V1, R_V1B, R_QT, R_OAH = [[fw.res(), fw.res()] for _ in range(5)]
                R_O1, R_RCP = [[fw.res() for _ in range(4)] for _ in range(2)]
                R_PT = [fw.res() for _ in range(4)]
                R_psS = [fw.pres() for _ in range(4)]
                R_psO = [fw.pres() for _ in range(3)]
                R_ZB, R_JK, R_LQ, R_LS, R_GSUB = [fw.res() for _ in range(5)]
                MSET(ZB[:, :], 0.0, [R_ZB])
                for i in range(2):
                    MSET(V1[i][:, :, 128:129], 1.0, [R_V1[i]])
                    MSET(V1B[i][:, :, 64:65], 1.0, [R_V1B[i]])
                dma("sp", LQ[:, :], row_b(l, "lam"), w=[R_LQ])
                dma("sp", GSUB[:, :], row_b(l, "subln"), w=[R_GSUB])
                TT(LQ[:, 0:64], LQ[:, 0:64], LQ[:, 64:128], ALU.mult, [], [R_LQ])
                TT(LQ[:, 128:192], LQ[:, 128:192], LQ[:, 192:256], ALU.mult, [], [R_LQ])
                ACT(JK[:, 0:64], LQ[:, 0:64], AF.Identity, [R_LQ], [R_JK, R_LS], accum_out=LS[:, 0:1])
                ACT(JK[:, 0:64], LQ[:, 128:192], AF.Identity, [R_LQ], [R_JK, R_LS], accum_out=LS[:, 1:2])
                ACT(LS[:, 0:2], LS[:, 0:2], AF.Exp, [], [R_LS])
                TS(LS[:, 2:3], LS[:, 0:1], LS[:, 1:2], ALU.subtract, [], [R_LS], s2=lam_init, op1=ALU.add)
                TS(LS[:, 3:4], LS[:, 2:3], -1.0, ALU.mult, [], [R_LS])
                TS(GSUB[:, :], GSUB[:, :], 1.0 - lam_init, ALU.mult, [], [R_GSUB])
                if "lam" in dbg_out and l == dbgl:
                    dma("sp", dbg_out["lam"], LS[:, :], r=[R_LS])
                vctx_v = vctx_d.rearrange("(t p) c -> p t c", p=128)
                rcv2_v = [r_.rearrange("(t p) c -> p t c", p=128) for r_ in rcv2]
                qgroups = [(i * 512, 512, 0, NKT) for i in range(4)]
                if ctx_out:
                    qgroups.append((NLAT, 256, 0, 2))
                hsets = [("A", h) for h in range(4)] + [("B", h) for h in range(4)]
                for hi, (kind, h) in enumerate(hsets):
                    b = hi % 2
                    if kind == "A":
                        nm, dk, e, scale = 2, 64, 128, 64 ** -0.5
                        kr0, qsrc, Vb, R_Vb, vc0 = h * 128, qA_d[h * 128:(h + 1) * 128, :], V1[b], R_V1[b], h * 128
                        dkt = 128
                    else:
                        nm, dk, e, scale = 1, 96, 64, 96 ** -0.5
                        kr0, qsrc, Vb, R_Vb, vc0 = 512 + h * 96, qB_d[h * 96:(h + 1) * 96, :], V1B[b], R_V1B[b], 512 + h * 64
                        dkt = 96
                    dma("sp", KT[b][0:dkt, 0:NCTX], kctx_d[kr0:kr0 + dkt, :], r=[R_kctx], w=[R_KT[b]])
                    rsrc = rcv1A[h // 2] if kind == "A" else rcv1B[h // 2]
                    nrw = 256 if kind == "A" else 192
                    ro = (h % 2) * dkt
                    dma("sp", KT[b][0:dkt, NCTX:NCTX + NLAT], rsrc[ro:ro + dkt, :], r=[R_rcv1], w=[R_KT[b]])
                    dma("sp", KT[b][0:dkt, NCTX + NLAT:NKEY], rsrc[nrw + ro:nrw + ro + dkt, :], r=[R_rcv1], w=[R_KT[b]])
                    dma("sp", Vb[:, 0:2, 0:e], vctx_v[:, :, vc0:vc0 + e], r=[R_vctx], w=[R_Vb])
                    for rk_ in range(2):
                        for q4 in range(4):
                            kt_ = 2 + rk_ * 16 + q4 * 4
                            dma("sp", Vb[:, kt_:kt_ + 4, 0:e], rcv2_v[q4][:, rk_ * 4:(rk_ + 1) * 4, vc0:vc0 + e],
                                r=[R_rcv2], w=[R_Vb])
                    dma("sp", QT[b][0:dkt, :], qsrc, r=[R_qA, R_qB], w=[R_QT[b]])
                    wacc = e + 1
                    per_bank = 512 // wacc if kind == "A" else 4

                    def acc(qb, m):
                        i = qb * nm + m
                        return i // per_bank, (i % per_bank) * wacc

                    for (q0, nq, kt0, kt1) in qgroups:
                        nqb = nq // 128
                        nbanks = (nqb * nm + per_bank - 1) // per_bank
                        for bk in range(nbanks):
                            MM(psO[bk][:, :], ZB[:, 0:128], ZB[:, :], True, False, [R_ZB], [R_psO[bk]])
                        sidx = [0]

                        def emit_S(kt):
                            lst = []
                            for m in range(nm):
                                si = sidx[0] % 4
                                sidx[0] += 1
                                MM(psS[si][:, 0:nq], KT[b][64 * m:64 * m + dk, kt * 128:(kt + 1) * 128],
                                   QT[b][64 * m:64 * m + dk, q0:q0 + nq], True, True, [R_KT[b], R_QT[b]], [R_psS[si]])
                                lst.append(si)
                            return lst

                        cur = emit_S(kt0)
                        for kt in range(kt0, kt1):
                            nxt = emit_S(kt + 1) if kt + 1 < kt1 else None
                            for m in range(nm):
                                si = cur[m]
                                ACT(PT[si][:, 0:nq], psS[si][:, 0:nq], AF.Exp, [R_psS[si]], [R_PT[si]], scale=scale)
                            for m in range(nm):
                                si = cur[m]
                                for qb in range(nqb):
                                    bk, off = acc(qb, m)
                                    MM(psO[bk][:, off:off + wacc], PT[si][:, qb * 128:(qb + 1) * 128], Vb[:, kt, :],
                                       False, kt == kt1 - 1, [R_PT[si], R_Vb], [R_psO[bk]])
                            cur = nxt
                        for bk in range(nbanks):
                            CP(OCP[:, bk, :], psO[bk][:, :], [R_psO[bk]], [R_OCP], eng="act" if bk % 2 else "dve")
                        tl = [(q0 + qb * 128) // 128 for qb in range(nqb)]
                        if kind == "A":
                            A0 = [acc(qb, 0) for qb in range(nqb)]
                            A1 = [acc(qb, 1) for qb in range(nqb)]
                            for qb in range(nqb):
                                b0, o0 = A0[qb]
                                RECIP(RCP[qb][:, 0:1], OCP[:, b0, o0 + 128:o0 + 129], [R_OCP], [R_RCP[qb]])
                            for qb in range(nqb):
                                b1, o1 = A1[qb]
                                RECIP(RCP[qb][:, 1:2], OCP[:, b1, o1 + 128:o1 + 129], [R_OCP], [R_RCP[qb]])
                            for qb in range(nqb):
                                TT(RCP[qb][:, 1:2], RCP[qb][:, 1:2], LS[:, 3:4], ALU.mult, [R_LS], [R_RCP[qb]])
                            for qb in range(nqb):
                                b0, o0 = A0[qb]
                                TS(O1[qb][:, :], OCP[:, b0, o0:o0 + 128], RCP[qb][:, 0:1], ALU.mult, [R_OCP, R_RCP[qb]], [R_O1[qb]])
                            for qb in range(nqb):
                                b1, o1 = A1[qb]
                                STT(O1[qb][:, :], OCP[:, b1, o1:o1 + 128], RCP[qb][:, 1:2], O1[qb][:, :], ALU.mult, ALU.add,
                                    [R_OCP, R_RCP[qb]], [R_O1[qb]])
                            for qb in range(nqb):
                                ACT(JK[:, :], O1[qb][:, :], AF.Square, [R_O1[qb]], [R_JK, R_RCP[qb]], accum_out=RCP[qb][:, 2:3])
                            for qb in range(nqb):
                                ACT(RCP[qb][:, 2:3], RCP[qb][:, 2:3], AF.Sqrt, [R_CF], [R_RCP[qb]], bias=EPS_T[SUBLN_EPS], scale=1.0 / 128)
                            for qb in range(nqb):
                                RECIP(RCP[qb][:, 2:3], RCP[qb][:, 2:3], [], [R_RCP[qb]])
                            for qb in range(nqb):
                                STT(OAH[b][:, tl[qb], :], O1[qb][:, :], RCP[qb][:, 2:3], GSUB[:, :], ALU.mult, ALU.mult,
                                    [R_O1[qb], R_RCP[qb], R_GSUB], [R_OAH[b]])
                        else:
                            for qb in range(nqb):
                                b0, o0 = acc(qb, 0)
                                RECIP(RCP[qb][:, 0:1], OCP[:, b0, o0 + 64:o0 + 65], [R_OCP], [R_RCP[qb]])
                            for qb in range(nqb):
                                b0, o0 = acc(qb, 0)
                                TS(OAH[b][:, tl[qb], 0:64], OCP[:, b0, o0:o0 + 64], RCP[qb][:, 0:1], ALU.mult,
                                   [R_OCP, R_RCP[qb]], [R_OAH[b]])
                    nto = len(tiles_out)
                    dst = oall_d.rearrange("(t p) c -> p t c", p=128)
                    if kind == "A":
                        dma("sp", dst[:, 0:nto, h * 128:(h + 1) * 128], OAH[b][:, 0:nto, :], r=[R_OAH[b]], w=[R_oall])
                    else:
                        dma("sp", dst[:, 0:nto, 512 + h * 64:512 + (h + 1) * 64], OAH[b][:, 0:nto, 0:64], r=[R_OAH[b]], w=[R_oall])
                fw.barrier()
                fw.flush()

        if do_rwkv:
            RWKV_PHASE(l, ctx_out, tiles_out)

        with ExitStack() as st:
            if not do_merge:
                break
            OT = sb(st, "OT", [128, 8, NTOK], BF16)
            WO = sb(st, "WO", [128, 8, D], BF16)
            OTI = [sb(st, "OTI%d" % i, [128, D], BF16) for i in range(2)]
            G1 = [sb(st, "G1_%d" % i, [128, D]) for i in range(2)]
            TM = [sb(st, "TM%d" % i, [128, D]) for i in range(2)]
            JM = sb(st, "JM", [128, 512])
            SS = [sb(st, "SSm%d" % i, [128, 2]) for i in range(2)]
            psT = [pst(st, "psTb%d" % i, [128, 1024], BF16) for i in range(2)]
            psM = [pst(st, "psM%d" % i, [128, 512]) for i in range(4)]
            R_OT, R_WO, R_JM = fw.res(), fw.res(), fw.res()
            R_OTI, R_G1, R_TM, R_SS = [[fw.res(), fw.res()] for _ in range(4)]
            R_psT = [fw.pres(), fw.pres()]
            R_psM = [fw.pres() for _ in range(4)]
            wov = w_out[l].rearrange("(kc p) n -> p kc n", p=128)
            for kc in range(8):
                dma("pool", WO[:, kc, :], wov[:, kc, :], w=[R_WO])
            for s in range(2):
                gr = (l * 2 + s) * 2 + 0
                dma("sp", G1[s][:, :], grow[gr, :].partition_broadcast(128), r=[R_grow], w=[R_G1[s]])
            for n_, t in enumerate(tiles_out):
                b = n_ % 2
                dma("sp", OTI[b][:, :], oall_d[t * 128:(t + 1) * 128, :], r=[R_oall], w=[R_OTI[b]])
                for fc in range(8):
                    TR(psT[b][:, fc * 128:(fc + 1) * 128], OTI[b][:, fc * 128:(fc + 1) * 128], IDB[:, :],
                       [R_OTI[b], R_IDB], [R_psT[b]])
                if "oall" in dbg_out and l == dbgl:
                    CP(TM[b][:, :], OTI[b][:, :], [R_OTI[b]], [R_TM[b]])
                    dma("sp", dbg_out["oall"][t * 128:(t + 1) * 128, :], TM[b][:, :], r=[R_TM[b]])
                CP(OT[:, :, t * 128:(t + 1) * 128], psT[b][:, :].rearrange("p (a b) -> p a b", a=8), [R_psT[b]], [R_OT],
                   eng="act" if n_ % 2 else "dve")
            for n_, t in enumerate(tiles_out):
                b = n_ % 2
                s = 0 if t < 16 else 1
                for hf in range(2):
                    pm = b * 2 + hf
                    for kc in range(8):
                        MM(psM[pm][:, :], OT[:, kc, t * 128:(t + 1) * 128], WO[:, kc, hf * 512:(hf + 1) * 512], kc == 0, kc == 7,
                           [R_OT, R_WO], [R_psM[pm]])
                    ACT(JM[:, :], psM[pm][:, :], AF.Square, [R_psM[pm]], [R_JM, R_SS[b]], accum_out=SS[b][:, hf:hf + 1])
                    TT(TM[b][:, hf * 512:(hf + 1) * 512], psM[pm][:, :], G1[s][:, hf * 512:(hf + 1) * 512], ALU.mult,
                       [R_psM[pm], R_G1[s]], [R_TM[b]])
                TT(SS[b][:, 0:1], SS[b][:, 0:1], SS[b][:, 1:2], ALU.add, [], [R_SS[b]])
                rstd_from_ss(SS[b][:, 0:1], R_SS[b], D, NORM_EPS)
                STT(X[:, t, :], TM[b][:, :], SS[b][:, 0:1], X[:, t, :], ALU.mult, ALU.add, [R_TM[b], R_SS[b]], [R_X[t]])
            fw.barrier()
            fw.flush()
        if "xm" in dbg_out and l == dbgl:
            for t in range(NT):
                dma("sp", dbg_out["xm"][t * 128:(t + 1) * 128, :], X[:, t, :], r=[R_X[t]])

        if do_ffn:
            with ExitStack() as st:
                H2T = sb(st, "H2T", [128, 8, NCOL], BF16)
                R_H2T = fw.res()
                psF = [pst(st, "psF%d" % i, [128, 512]) for i in range(8)]
                R_psF = [fw.pres() for _ in range(8)]
                with ExitStack() as st2:
                    XS = [sb(st2, "XSf%d" % i, [128, D]) for i in range(2)]
                    R_XS = [fw.res(), fw.res()]
                    RSTD = sb(st2, "RSTDf", [128, NT])
                    R_RSTD = fw.res()
                    norm_transpose(l, 2, tiles_out, H2T, R_H2T, psF[0:4], R_psF[0:4], XS, R_XS, RSTD, R_RSTD, XS[1], R_XS[1])
                    fw.barrier()
                    fw.flush()
                R_H2Th = fw.res()
                halo_exchange(st, H2T, R_H2T, R_H2Th, snd5, rcv5, R_snd5, R_rcv5, l, "f")
                ACTT = sb(st, "ACTTf", [128, NCH, 512], BF16)
                WU = [sb(st, "WU%d" % i, [128, 8, 256], BF16) for i in range(3)]
                WD = [sb(st, "WD%d" % i, [128, D], BF16) for i in range(4)]
                US = [sb(st, "US%d" % i, [128, 2, 514]) for i in range(2)]
                CV = [sb(st, "CV%d" % i, [128, 2, 512]) for i in range(2)]
                SGf = [sb(st, "SGf%d" % i, [128, 512]) for i in range(2)]
                G2 = [sb(st, "G2_%d" % i, [128, D]) for i in range(2)]
                TM = [sb(st, "TMf%d" % i, [128, D]) for i in range(2)]
                JM = sb(st, "JMf", [128, 512])
                SS = [sb(st, "SSf%d" % i, [128, 2]) for i in range(2)]
                R_ACTT, R_JM = fw.res(), fw.res()
                R_WU = [fw.res() for _ in range(3)]
                R_WD = [fw.res() for _ in range(4)]
                R_US, R_CV, R_SGf, R_G2, R_TM, R_SS = [[fw.res(), fw.res()] for _ in range(6)]
                for s in range(2):
                    gr = (l * 2 + s) * 2 + 1
                    dma("sp", G2[s][:, :], grow[gr, :].partition_broadcast(128), r=[R_grow], w=[R_G2[s]])
                wupv = wup_bf.rearrange("(kc p) n -> p kc n", p=128)
                blocks = [(i * 512, 512, 0, NLAT + 1, i * 4) for i in range(4)]
                if ctx_out:
                    blocks.append((CTX0, 256, CTX0, CTX0 + 256, 16))
                nwu = 0
                nwd = 0
                for (c0, n, seq0, seq1, tile0) in blocks:
                    lo = max(c0 - 1, seq0)
                    hi = min(c0 + n + 1, seq1)
                    off0 = lo - (c0 - 1)
                    len1 = min(512, hi - lo)
                    len2 = (hi - lo) - len1
                    for cc in range(NCH):
                        wb = nwu % 3
                        ub = nwu % 2
                        nwu += 1
                        dma("sp", WU[wb][:, :, 0:128], wupv[:, :, cc * 128:(cc + 1) * 128], r=[R_wup], w=[R_WU[wb]])
                        dma("sp", WU[wb][:, :, 128:256], wupv[:, :, (NCH + cc) * 128:(NCH + cc + 1) * 128], r=[R_wup], w=[R_WU[wb]])
                        for gv in range(2):
                            pa = (ub * 2 + gv) * 2
                            pbk = pa + 1
                            hx = [R_H2Th] if (lo <= HALO < hi) else []
                            for kc in range(8):
                                MM(psF[pa][:, 0:len1], WU[wb][:, kc, gv * 128:(gv + 1) * 128], H2T[:, kc, lo:lo + len1],
                                   kc == 0, kc == 7, [R_WU[wb], R_H2T] + hx, [R_psF[pa]])
                            CP(US[ub][:, gv, off0:off0 + len1], psF[pa][:, 0:len1], [R_psF[pa]], [R_US[ub]], eng="act")
                            if len2 > 0:
                                for kc in range(8):
                                    MM(psF[pbk][:, 0:len2], WU[wb][:, kc, gv * 128:(gv + 1) * 128],
                                       H2T[:, kc, lo + len1:lo + len1 + len2], kc == 0, kc == 7, [R_WU[wb], R_H2T] + hx, [R_psF[pbk]])
                                CP(US[ub][:, gv, off0 + len1:off0 + len1 + len2], psF[pbk][:, 0:len2], [R_psF[pbk]], [R_US[ub]],
                                   eng="act")
                            if off0 > 0:
                                MSET(US[ub][:, gv, 0:1], 0.0, [R_US[ub]])
                            if off0 + len1 + len2 < n + 2:
                                MSET(US[ub][:, gv, n + 1:n + 2], 0.0, [R_US[ub]])
                            ci = gv * NCH + cc
                            eng = "dve" if gv == 0 else "pool"
                            TS(CV[ub][:, gv, 0:n], US[ub][:, gv, 1:n + 1], pc(l, "cw1", ci, ci + 1), ALU.mult, [R_US[ub], R_PCS],
                               [R_CV[ub]], s2=pc(l, "cb", ci, ci + 1), op1=ALU.add, eng=eng)
                            STT(CV[ub][:, gv, 0:n], US[ub][:, gv, 0:n], pc(l, "cw0", ci, ci + 1), CV[ub][:, gv, 0:n], ALU.mult, ALU.add,
                                [R_US[ub], R_PCS], [R_CV[ub]], eng=eng)
                            STT(CV[ub][:, gv, 0:n], US[ub][:, gv, 2:n + 2], pc(l, "cw2", ci, ci + 1), CV[ub][:, gv, 0:n], ALU.mult, ALU.add,
                                [R_US[ub], R_PCS], [R_CV[ub]], eng=eng)
                        ACT(SGf[ub][:, 0:n], CV[ub][:, 0, 0:n], AF.Silu, [R_CV[ub]], [R_SGf[ub]])
                        TT(ACTT[:, cc, 0:n], SGf[ub][:, 0:n], CV[ub][:, 1, 0:n], ALU.mult, [R_SGf[ub], R_CV[ub]], [R_ACTT])
                    ntl = n // 128
                    for cc in range(NCH):
                        wd = nwd % 4
                        nwd += 1
                        dma("sp", WD[wd][:, :], wdn_bf[cc * 128:(cc + 1) * 128, :], r=[R_wdn], w=[R_WD[wd]])
                        for tt in range(ntl):
                            for hf in range(2):
                                pm = tt * 2 + hf
                                MM(psF[pm][:, :], ACTT[:, cc, tt * 128:(tt + 1) * 128], WD[wd][:, hf * 512:(hf + 1) * 512],
                                   cc == 0, cc == NCH - 1, [R_ACTT, R_WD[wd]], [R_psF[pm]])
                    for tt in range(ntl):
                        t = tile0 + tt
                        b = tt % 2
                        s = 0 if t < 16 else 1
                        for hf in range(2):
                            pm = tt * 2 + hf
                            ACT(JM[:, :], psF[pm][:, :], AF.Square, [R_psF[pm]], [R_JM, R_SS[b]], accum_out=SS[b][:, hf:hf + 1])
                            TT(TM[b][:, hf * 512:(hf + 1) * 512], psF[pm][:, :], G2[s][:, hf * 512:(hf + 1) * 512], ALU.mult,
                               [R_psF[pm], R_G2[s]], [R_TM[b]])
                        TT(SS[b][:, 0:1], SS[b][:, 0:1], SS[b][:, 1:2], ALU.add, [], [R_SS[b]])
                        rstd_from_ss(SS[b][:, 0:1], R_SS[b], D, NORM_EPS)
                        STT(X[:, t, :], TM[b][:, :], SS[b][:, 0:1], X[:, t, :], ALU.mult, ALU.add, [R_TM[b], R_SS[b]], [R_X[t]])
                fw.barrier()
                fw.flush()
        if "xo" in dbg_out and l == dbgl:
            for t in range(NT):
                dma("sp", dbg_out["xo"][t * 128:(t + 1) * 128, :], X[:, t, :], r=[R_X[t]])

    yv = yout.rearrange("(t p) f -> p t f", p=128)
    for t in range(16):
        dma("sp", yv[:, t, :], X[:, t, :], r=[R_X[t]])
    fw.barrier()
    fw.flush()
    glob.close()
    return nc, fw


def _consts():
    cf = np.zeros((128, NCF), np.float32)
    o = CF_OFF
    cf[:, o["ident"][0]:o["ident"][0] + 128] = np.eye(128, dtype=np.float32)
    cf[:, o["ones"][0]:o["ones"][0] + 128] = 1.0
    blk = np.zeros((128, 128), np.float32)
    blk[0:64, 0:64] = 1.0
    blk[64:128, 64:128] = 1.0
    cf[:, o["blk"][0]:o["blk"][0] + 128] = blk
    RA = np.zeros((128, 128), np.float32)
    for m in range(128):
        if (m % 32) < 16:
            RA[m + 16, m] = -1.0
        else:
            RA[m - 16, m] = 1.0
    cf[:, o["RA"][0]:o["RA"][0] + 128] = RA
    RBm = np.zeros((128, 128), np.float32)
    for m in range(64, 96):
        e = m - 64
        if (e % 16) < 8:
            RBm[m + 8, m] = -1.0
        else:
            RBm[m - 8, m] = 1.0
    cf[:, o["RB"][0]:o["RB"][0] + 128] = RBm
    RK = np.zeros((128, 32), np.float32)
    for m in range(32):
        if (m % 16) < 8:
            RK[m + 8, m] = -1.0
        else:
            RK[m - 8, m] = 1.0
    cf[:, o["RK"][0]:o["RK"][0] + 32] = RK
    cf[0:64, o["hsel"][0]] = 1.0
    cf[64:128, o["hsel"][0] + 1] = 1.0
    cr = np.zeros((128, NCR), np.float32)
    j = np.arange(128)[:, None]
    i = np.arange(128)[None, :]
    same = (j // 64) == (i // 64)
    r = CR_OFF
    cr[:, r["Us"][0]:r["Us"][0] + 128] = (same & (i > j))
    cr[:, r["Ui"][0]:r["Ui"][0] + 128] = (same & (i >= j))
    cr[:, r["Ls"][0]:r["Ls"][0] + 128] = (same & (i < j))
    cr[:, r["Li"][0]:r["Li"][0] + 128] = (same & (i <= j))
    tf = np.concatenate([(same & (j <= i)), (same & (j < i)), (same & (j > i))], axis=1).astype(np.float32) * DECAY_C
    tb = np.concatenate([(same & (j >= i)), (same & (j > i)), (same & (j < i))], axis=1).astype(np.float32) * DECAY_C
    cr[:, r["trif"][0]:r["trif"][0] + 384] = tf
    cr[:, r["trib"][0]:r["trib"][0] + 384] = tb
    return cf, cr


def _rope_tables(s):
    li = np.arange(NLAT)
    tt = li if s == 0 else (4095 - li)
    rows = (tt // 64).astype(np.float64)
    cols = (tt % 64).astype(np.float64)
    ropeA = np.zeros((2, 128, NCOL), np.float32)
    ropeA[0] = 1.0
    invA = 10000.0 ** (-np.arange(16, dtype=np.float32) / 16)
    for p in range(128):
        dd = p % 64
        a, f = dd // 32, dd % 16
        pos = rows if a == 0 else cols
        ang = (pos.astype(np.float32) * invA[f]).astype(np.float32)
        ropeA[0, p, 0:NLAT] = np.cos(ang)
        ropeA[1, p, 0:NLAT] = np.sin(ang)
    ropeB = np.zeros((2, 96, NCOL), np.float32)
    ropeB[0] = 1.0
    invB = 10000.0 ** (-np.arange(8, dtype=np.float32) / 8)
    for e in range(32):
        a, f = e // 16, e % 8
        pos = rows if a == 0 else cols
        ang = (pos.astype(np.float32) * invB[f]).astype(np.float32)
        ropeB[0, 64 + e, 0:NLAT] = np.cos(ang)
        ropeB[1, 64 + e, 0:NLAT] = np.sin(ang)
    return ropeA, ropeB


def _pcol(v, n):
    return np.ascontiguousarray(np.asarray(v, np.float32).reshape(n, 128).T)


def _pack_core(inp, c):
    b, s = c // 2, c % 2
    f32 = np.float32
    xl = inp["x"][b, s * NLAT:(s + 1) * NLAT]
    cx = inp["ctx"][b]
    if s == 1:
        xl = xl[::-1]
        cx = cx[::-1]
    m = {}
    m["xin"] = np.ascontiguousarray(np.concatenate([xl, cx], 0), dtype=f32)
    m["cvec"] = np.ascontiguousarray(np.concatenate([_pcol(inp["c"][b], 8), _pcol(inp["c_ctx"], 8)], 1))
    m["ada_w"] = np.ascontiguousarray(inp["ada_w"], dtype=f32)
    w_in = np.array(inp["w_in"], dtype=f32)
    mup = np.array(inp["c_mu_prev"], dtype=f32)
    mun = np.array(inp["c_mu_next"], dtype=f32)
    dirs = (0, 1) if s == 0 else (1, 0)
    if s == 1:
        perm = np.arange(1152)
        perm[768:832], perm[832:896] = np.arange(832, 896), np.arange(768, 832)
        perm[896:960], perm[960:1024] = np.arange(960, 1024), np.arange(896, 960)
        w_in[:, :, 1888:3040] = w_in[:, :, 1888 + perm]
        mup, mun = mun[:, perm], mup[:, perm]
    m["w_in"] = np.ascontiguousarray(w_in)
    m["w_out"] = np.ascontiguousarray(inp["w_out"], dtype=f32)
    m["ffn_up"] = np.ascontiguousarray(inp["ffn_w_up"], dtype=f32)
    m["ffn_dn"] = np.ascontiguousarray(inp["ffn_w_down"], dtype=f32)
    m["wq_up"] = np.ascontiguousarray(inp["b_w_q_up"], dtype=f32)
    wkv = np.asarray(inp["b_w_kv_up"], f32).reshape(L, 128, 4, 2, 64)
    m["wkv_up"] = np.ascontiguousarray(np.concatenate([wkv[:, :, :, 0, :].reshape(L, 128, 256),
                                                       wkv[:, :, :, 1, :].reshape(L, 128, 256)], axis=2))
    m["cw2"] = np.ascontiguousarray(np.concatenate([inp["c_w2"][:, dirs[0]], inp["c_w2"][:, dirs[1]]], axis=1), dtype=f32)
    m["ca2"] = np.ascontiguousarray(np.concatenate([inp["c_a2"][:, dirs[0]], inp["c_a2"][:, dirs[1]]], axis=1), dtype=f32)
    m["cg2"] = np.ascontiguousarray(inp["c_g2"], dtype=f32)
    pcs = np.zeros((L, 128, NPC), f32)
    rows = np.zeros((L, NROW), f32)
    cw = np.array(inp["ffn_conv_w"], dtype=f32)
    if s == 1:
        cw = cw[:, ::-1, :]
    for l in range(L):
        def put(name, arr):
            o, w = PC_OFF[name]
            assert arr.shape == (128, w), (name, arr.shape)
            pcs[l, :, o:o + w] = arr
        ab = inp["ada_b"][l]
        put("adab", np.concatenate([_pcol(ab[0:1024], 8), _pcol(ab[1024:2048], 8), _pcol(ab[3072:4096], 8),
                                    _pcol(ab[4096:5120], 8)], 1))
        put("preg1", _pcol(inp["mix_pre_g"][l], 8))
        put("preg2", _pcol(inp["ffn_pre_g"][l], 8))
        qn = np.zeros((128, 2), f32)
        qn[:, 0] = inp["b_q_norm_g"][l][0:128]
        qn[0:64, 1] = inp["b_q_norm_g"][l][128:192]
        put("qng", qn)
        put("kvng", np.asarray(inp["b_kv_norm_g"][l], f32).reshape(128, 1))
        put("mup", _pcol(mup[l], 9))
        put("mun", _pcol(mun[l], 9))
        put("a0", np.concatenate([_pcol(inp["c_a0"][l][dirs[0]], 2), _pcol(inp["c_a0"][l][dirs[1]], 2)], 1))
        put("kk", _pcol(inp["c_k_k"][l], 2))
        put("ka", _pcol(inp["c_k_a"][l], 2))
        put("rk", _pcol(np.asarray(inp["c_r_k"][l]).reshape(256), 2))
        put("cw0", _pcol(cw[l, 0], 44))
        put("cw1", _pcol(cw[l, 1], 44))
        put("cw2", _pcol(cw[l, 2], 44))
        put("cb", _pcol(inp["ffn_conv_b"][l], 44))
        sel = np.zeros((128, 2), f32)
        sel[:, 1 - s] = 1.0
        put("sel", sel)

        def putr(name, arr):
            o, w = ROW_OFF[name]
            arr = np.asarray(arr, f32).reshape(-1)
            assert arr.shape[0] == w, (name, arr.shape)
            rows[l, o:o + w] = arr
        putr("adab_g1", ab[2048:3072])
        putr("adab_g2", ab[5120:6144])
        putr("postg1", inp["mix_post_g"][l])
        putr("postg2", inp["ffn_post_g"][l])
        putr("lam", np.concatenate([inp["lam_q1"][l], inp["lam_k1"][l], inp["lam_q2"][l], inp["lam_k2"][l]]))
        putr("subln", inp["a_subln_g"][l])
        putr("w0", np.concatenate([inp["c_w0"][l][dirs[0]], inp["c_w0"][l][dirs[1]]]))
        putr("gng", inp["c_gn_g"][l])
        putr("gnb", inp["c_gn_b"][l])
    m["pc"] = pcs
    m["row"] = rows
    cf_, cr_ = _consts()
    m["cf"] = cf_
    m["cr"] = cr_
    ra, rb = _rope_tables(s)
    m["ropeA"] = ra
    m["ropeB"] = rb
    return m


_CACHE = {}


def kernel(**inputs):
    inp = {k: np.asarray(v) for k, v in inputs.items()}
    if "nc" not in _CACHE:
        _CACHE["nc"] = build_program()[0]
    nc = _CACHE["nc"]
    in_maps = [_pack_core(inp, c) for c in range(8)]
    res = run_bass_kernel_spmd(nc, in_maps, core_ids=list(range(8)))
    out = np.zeros((4, 4096, D), np.float32)
    for c in range(8):
        b, s = c // 2, c % 2
        y = np.asarray(res.results[c]["yout"], dtype=np.float32)
        if s == 1:
            y = y[::-1]
        out[b, s * NLAT:(s + 1) * NLAT] = y
    return out
```

```python
import math
from contextlib import ExitStack
import numpy as np
import concourse.bass as bass
import concourse.mybir as mybir
from concourse.bass_utils import run_bass_kernel_spmd

F32 = mybir.dt.float32
BF16 = mybir.dt.bfloat16
AF = mybir.ActivationFunctionType
ALU = mybir.AluOpType

D = 1024
L = 2
NLAT = 2048
NCTX = 256
NTOK = NLAT + NCTX
NT = NTOK // 128
NCOL = NTOK + 1
HALO = NLAT
CTX0 = NLAT + 1
NKEY = 4096 + NCTX
NKT = NKEY // 128
N_IN = 3040
DFF = 2816
NCH = DFF // 128
GROUPS = [[0, 1], [2, 3], [4, 5], [6, 7]]
NORM_EPS = 1e-6
SUBLN_EPS = 1e-5
GN_EPS = 64e-5
DECAY_C = -math.exp(-0.5)
RB = 256


def col(t):
    return t * 128 if t < 16 else CTX0 + (t - 16) * 128


def _mk(lst):
    d, o = {}, 0
    for n, w in lst:
        d[n] = (o, w)
        o += w
    return d, o


PC_OFF, NPC = _mk([("adab", 32), ("preg1", 8), ("preg2", 8), ("qng", 2), ("kvng", 1), ("mup", 9), ("mun", 9),
                   ("a0", 4), ("kk", 2), ("ka", 2), ("rk", 2), ("cw0", 44), ("cw1", 44), ("cw2", 44), ("cb", 44),
                   ("sel", 2)])
ROW_OFF, NROW = _mk([("adab_g1", 1024), ("adab_g2", 1024), ("postg1", 1024), ("postg2", 1024), ("lam", 256),
                     ("subln", 128), ("w0", 512), ("gng", 256), ("gnb", 256)])
CF_OFF, NCF = _mk([("ident", 128), ("ones", 128), ("blk", 128), ("RA", 128), ("RB", 128), ("RK", 32), ("hsel", 2)])
CR_OFF, NCR = _mk([("Us", 128), ("Ui", 128), ("Ls", 128), ("Li", 128), ("trif", 384), ("trib", 384)])


class Res:
    __slots__ = ("name", "w", "rs", "excl")

    def __init__(self, name, excl=False):
        self.name = name
        self.w = None
        self.rs = {}
        self.excl = excl


class Eng:
    def __init__(self, key, sem):
        self.key = key
        self.sem = sem
        self.count = 0
        self.items = []
        self.waited = {}


class FW:
    NDS = 6

    def __init__(self, nc):
        self.nc = nc
        self.engs = {k: Eng(k, nc.alloc_semaphore("es_" + k)) for k in ("pe", "act", "dve", "pool", "sp")}
        self.dsems = {q: [[nc.alloc_semaphore("ds_%s%d" % (q, i)), 0] for i in range(self.NDS)] for q in ("sp", "pool")}
        self.drr = {"sp": 0, "pool": 0}
        self.cctoks = []
        self.nres = 0
        self.same_engine_sync = True
        self.ninst = 0
        self.muted = False
        self.skip_waw = False

    def res(self, name=None):
        self.nres += 1
        return Res(name or ("r%d" % self.nres))

    def pres(self):
        self.nres += 1
        return Res("p%d" % self.nres, excl=True)

    def _wait(self, e, tok):
        sem, val = tok
        if e.waited.get(sem, 0) >= val:
            return
        e.waited[sem] = val
        e.items.append(("w", sem, val))

    def _deps(self, r, w, own=None):
        deps = []
        for x in r:
            if x.w is not None:
                deps.append(x.w)
            if x.excl:
                for s, v in x.rs.items():
                    if s is not own:
                        deps.append((s, v))
        for x in w:
            if x.w is not None and not (self.skip_waw and x.w[0] is own and x not in r):
                deps.append(x.w)
            for s, v in x.rs.items():
                deps.append((s, v))
        return deps

    def _update(self, tok, r, w):
        for x in w:
            x.w = tok
            x.rs = {}
        for x in r:
            if x in w:
                continue
            if x.rs.get(tok[0], 0) < tok[1]:
                x.rs[tok[0]] = tok[1]

    def op(self, ek, fn, r=(), w=()):
        if self.muted:
            return None
        e = self.engs[ek]
        for tok in self._deps(r, w, e.sem):
            if tok[0] is e.sem and (ek == "pe" or not self.same_engine_sync):
                continue
            self._wait(e, tok)
        e.count += 1
        self.ninst += 1
        tok = (e.sem, e.count)
        e.items.append(("o", fn))
        self._update(tok, r, w)
        return tok

    def dma(self, q, out, in_, r=(), w=(), **kw):
        if self.muted:
            return None
        e = self.engs[q]
        for tok in self._deps(r, w):
            self._wait(e, tok)
        slot = self.dsems[q][self.drr[q]]
        self.drr[q] = (self.drr[q] + 1) % self.NDS
        sem, val = slot
        if val > 0:
            self._wait(e, (sem, val))
        slot[1] = val + 16
        tok = (sem, val + 16)
        self.ninst += 1
        e.items.append(("d", out, in_, sem, kw))
        self._update(tok, r, w)
        return tok

    def collective(self, in_ap, out_ap, r=(), w=()):
        if self.muted:
            return None
        e = self.engs["pool"]
        for tok in self._deps(r, w):
            self._wait(e, tok)
        sem = self.nc.alloc_semaphore("cc%d" % len(self.cctoks))
        tok = (sem, 1)
        e.items.append(("c", in_ap, out_ap, sem))
        self.cctoks.append(tok)
        self._update(tok, r, w)
        return tok

    def barrier(self):
        toks = [(e.sem, e.count) for e in self.engs.values() if e.count > 0]
        for q in self.dsems:
            toks += [(s, v) for s, v in self.dsems[q] if v > 0]
        toks += self.cctoks
        for e in self.engs.values():
            for tok in toks:
                self._wait(e, tok)

    def flush(self):
        nc = self.nc
        hmap = {"pe": "tensor", "act": "scalar", "dve": "vector", "pool": "gpsimd", "sp": "sync"}
        with nc.Block() as block:
            for k, e in self.engs.items():
                items = e.items
                e.items = []

                def f(h, items=items, e=e):
                    for it in items:
                        if it[0] == "w":
                            h.wait_ge(it[1], it[2])
                        elif it[0] == "o":
                            it[1](h).then_inc(e.sem, 1)
                        elif it[0] == "d":
                            h.dma_start(out=it[1], in_=it[2], **it[4]).then_inc(it[3], 16)
                        else:
                            h.collective_compute("AllGather", ALU.bypass, replica_groups=GROUPS,
                                                 ins=[it[1]], outs=[it[2]]).then_inc(it[3], 1)
                getattr(block, hmap[k])(f)


def build_program(nlayers=L, dbg=None, do_rwkv=True, do_attn=True, do_ffn=True, do_cc=True, do_p=True, do_merge=True):
    dbg = dbg or {}
    dbgl = dbg.get("_layer", 0)
    pstop = dbg.get("_pstop", 0)

    def STAGE(k):
        if pstop == k:
            fw.muted = True

    nc = bass.Bass("TRN2", target_bir_lowering=False)
    fw = FW(nc)
    op, dma = fw.op, fw.dma

    def MM(out, lhsT, rhs, start, stop, r, w):
        op("pe", lambda e: e.matmul(out, lhsT=lhsT, rhs=rhs, start=start, stop=stop), r, w)

    def TR(out, in_, ident, r, w):
        op("pe", lambda e: e.transpose(out=out, in_=in_, identity=ident), r, w)

    def ACT(out, in_, func, r, w, bias=None, scale=None, accum_out=None, eng="act"):
        kw = {}
        if bias is not None:
            kw["bias"] = bias
        if scale is not None:
            kw["scale"] = scale
        if accum_out is not None:
            kw["accum_out"] = accum_out
        op(eng, lambda e: e.activation(out=out, in_=in_, func=func, **kw), r, w)

    def TT(out, in0, in1, alu, r, w, eng="dve"):
        op(eng, lambda e: e.tensor_tensor(out=out, in0=in0, in1=in1, op=alu), r, w)

    def TS(out, in0, s1, op0, r, w, s2=None, op1=None, eng="dve"):
        if op1 is None:
            op(eng, lambda e: e.tensor_scalar(out=out, in0=in0, scalar1=s1, scalar2=None, op0=op0), r, w)
        else:
            op(eng, lambda e: e.tensor_scalar(out=out, in0=in0, scalar1=s1, scalar2=s2, op0=op0, op1=op1), r, w)

    def STT(out, in0, scalar, in1, op0, op1, r, w, eng="dve"):
        eng = "dve"
        op(eng, lambda e: e.scalar_tensor_tensor(out=out, in0=in0, scalar=scalar, in1=in1, op0=op0, op1=op1), r, w)

    def CP(out, in_, r, w, eng="dve"):
        if eng == "act":
            op("act", lambda e: e.copy(out=out, in_=in_), r, w)
        else:
            op(eng, lambda e: e.tensor_copy(out=out, in_=in_), r, w)

    def RECIP(out, in_, r, w):
        op("dve", lambda e: e.reciprocal(out=out, in_=in_), r, w)

    def MSET(ap, v, w, eng="pool"):
        op(eng, lambda e: e.memset(ap, v), (), w)

    def CC(snd, rcv, R_s, R_r):
        if do_cc:
            for q_ in fw.dsems:
                for s_, v_ in fw.dsems[q_]:
                    if v_ > 0:
                        fw._wait(fw.engs["pool"], (s_, v_))
            fw.collective(snd.opt(), rcv.opt(), r=[R_s], w=[R_r])
        else:
            nr = snd.shape[0]
            dma("sp", rcv[0:nr, :], snd[:, :], r=[R_s], w=[R_r])
            dma("sp", rcv[nr:2 * nr, :], snd[:, :], r=[R_s], w=[R_r])

    def din(name, shape, dt=F32):
        return nc.dram_tensor(name, list(shape), dt, kind="ExternalInput").ap()

    def dscr(name, shape, dt=F32):
        return nc.dram_tensor(name, list(shape), dt).ap()

    xin = din("xin", [NTOK, D])
    cvec = din("cvec", [128, 16])
    ada_w = din("ada_w", [L, D, 6 * D])
    w_in = din("w_in", [L, D, N_IN])
    w_out = din("w_out", [L, D, D])
    ffn_up = din("ffn_up", [L, D, 2 * DFF])
    ffn_dn = din("ffn_dn", [L, DFF, D])
    wq_up = din("wq_up", [L, 192, 384])
    wkv_up = din("wkv_up", [L, 128, 512])
    cw2 = din("cw2", [L, 128, 256])
    ca2 = din("ca2", [L, 128, 256])
    cg2 = din("cg2", [L, 128, 256])
    pcd = din("pc", [L, 128, NPC])
    rowd = din("row", [L, NROW])
    cfd = din("cf", [128, NCF])
    crd = din("cr", [128, NCR])
    ropeA = din("ropeA", [2, 128, NCOL])
    ropeB = din("ropeB", [2, 96, NCOL])
    yout = nc.dram_tensor("yout", [NLAT, D], F32, kind="ExternalOutput").ap()
    dbg_out = {}
    for k, shp in dbg.items():
        if k.startswith("_"):
            continue
        dbg_out[k] = nc.dram_tensor("dbg_" + k, list(shp), F32, kind="ExternalOutput").ap()

    grow = dscr("grow", [L * 4, D])
    qA_d = dscr("qA_d", [4 * 128, NTOK], BF16)
    qB_d = dscr("qB_d", [4 * 96, NTOK], BF16)
    kctx_d = dscr("kctx_d", [896, NCTX], BF16)
    snd1A = [dscr("snd1A%d" % i, [256, NLAT], BF16) for i in range(2)]
    rcv1A = [dscr("rcv1A%d" % i, [512, NLAT], BF16) for i in range(2)]
    snd1B = [dscr("snd1B%d" % i, [192, NLAT], BF16) for i in range(2)]
    rcv1B = [dscr("rcv1B%d" % i, [384, NLAT], BF16) for i in range(2)]
    snd2 = [dscr("snd2_%d" % i, [512, 768], BF16) for i in range(4)]
    rcv2 = [dscr("rcv2_%d" % i, [1024, 768], BF16) for i in range(4)]
    vctx_d = dscr("vctx_d", [NCTX, 768], BF16)
    pcT_d = dscr("pcT_d", [1152, NCOL])
    snd3 = dscr("snd3", [128, 128])
    rcv3 = dscr("rcv3", [256, 128])
    snd4 = dscr("snd4", [128, 8])
    rcv4 = dscr("rcv4", [256, 8])
    snd5 = dscr("snd5", [128, 8])
    rcv5 = dscr("rcv5", [256, 8])
    wup_bf = dscr("wup_bf", [D, 2 * DFF], BF16)
    wdn_bf = dscr("wdn_bf", [DFF, D], BF16)
    R_grow, R_qA, R_qB, R_kctx, R_snd1, R_rcv1 = [fw.res() for _ in range(6)]
    R_snd2, R_rcv2, R_vctx, R_pcT = [fw.res() for _ in range(4)]
    R_snd3, R_rcv3, R_snd4, R_rcv4, R_snd5, R_rcv5 = [fw.res() for _ in range(6)]
    R_wup, R_wdn = fw.res(), fw.res()

    glob = ExitStack()

    uniq = [0]

    def sb(st, name, shape, dt=F32):
        uniq[0] += 1
        return st.enter_context(nc.sbuf_tensor("%s_u%d" % (name, uniq[0]), list(shape), dt))

    def pst(st, name, shape, dt=F32):
        uniq[0] += 1
        return st.enter_context(nc.psum_tensor("%s_u%d" % (name, uniq[0]), list(shape), dt))

    X = sb(glob, "X", [128, NT, D])
    CF = sb(glob, "CF", [128, NCF])
    IDB = sb(glob, "IDB", [128, 128], BF16)
    ONB = sb(glob, "ONB", [128, 128], BF16)
    MOD = sb(glob, "MOD", [128, L * 4 * 2 * 8])
    PCS = sb(glob, "PCS", [128, L * NPC])
    EPSB = sb(glob, "EPSB", [128, 4])
    R_X = [fw.res("X%d" % t) for t in range(NT)]
    R_CF, R_IDB, R_MOD, R_PCS = fw.res(), fw.res(), fw.res(), fw.res()

    def cf(name, p0=0, p1=128, c0=0, c1=None):
        o, w = CF_OFF[name]
        c1 = w if c1 is None else c1
        return CF[p0:p1, o + c0:o + c1]

    def pc(l, name, c0=0, c1=None, p0=0, p1=128):
        o, w = PC_OFF[name]
        c1 = w if c1 is None else c1
        return PCS[p0:p1, l * NPC + o + c0:l * NPC + o + c1]

    def modv(l, which, s):
        o = ((l * 4 + which) * 2 + s) * 8
        return MOD[:, o:o + 8]

    def row_b(l, name, c0=0, c1=None, parts=128):
        o, w = ROW_OFF[name]
        c1 = w if c1 is None else c1
        return rowd[l, o + c0:o + c1].partition_broadcast(parts)

    EPS_T = {NORM_EPS: EPSB[:, 0:1], SUBLN_EPS: EPSB[:, 1:2], GN_EPS: EPSB[:, 2:3], 1e-24: EPSB[:, 3:4]}

    dma("sp", CF[:, :], cfd[:, :], w=[R_CF])
    for l in range(L):
        dma("sp", PCS[:, l * NPC:(l + 1) * NPC], pcd[l, :, :], w=[R_PCS])
    xv = xin.rearrange("(t p) f -> p t f", p=128)
    for t in range(NT):
        dma("sp", X[:, t, :], xv[:, t, :], w=[R_X[t]])
    CP(IDB[:, :], cf("ident"), [R_CF], [R_IDB])
    CP(ONB[:, :], cf("ones"), [R_CF], [R_IDB])
    for i, v in enumerate((NORM_EPS, SUBLN_EPS, GN_EPS, 1e-24)):
        MSET(EPSB[:, i:i + 1], v, [R_CF])

    def rstd_from_ss(RS, R_RS, n, eps):
        ACT(RS, RS, AF.Sqrt, [R_RS, R_CF], [R_RS], bias=EPS_T[eps], scale=1.0 / n)
        RECIP(RS, RS, [R_RS], [R_RS])

    with ExitStack() as st:
        ACTF = sb(st, "ACTF", [128, 16])
        ACTT = sb(st, "ACTT", [128, 8, 2], BF16)
        ACTB = sb(st, "ACTB", [128, 2, 8, 128], BF16)
        ADW = [sb(st, "ADW%d" % i, [128, 8, 1024], BF16) for i in range(2)]
        MRAW = sb(st, "MRAW", [128, 4, 2, 8])
        GB = sb(st, "GB", [128, 1024])
        GP = sb(st, "GP", [128, 1024])
        GO = [sb(st, "GO%d" % i, [128, 512]) for i in range(2)]
        psA = [pst(st, "psA%d" % i, [128, 512]) for i in range(4)]
        R_ACTF, R_ACTT, R_ACTB, R_MRAW, R_GB, R_GP = [fw.res() for _ in range(6)]
        R_ADW = [fw.res(), fw.res()]
        R_GO = [fw.res(), fw.res()]
        R_ps = [fw.pres() for _ in range(4)]
        dma("sp", ACTF[:, :], cvec[:, :], w=[R_ACTF])
        ACT(ACTF[:, :], ACTF[:, :], AF.Silu, [R_ACTF], [R_ACTF])
        for s in range(2):
            CP(ACTT[:, :, s], ACTF[:, s * 8:(s + 1) * 8], [R_ACTF], [R_ACTT])
            for kc in range(8):
                TS(ACTB[:, s, kc, :], cf("ones"), ACTF[:, s * 8 + kc:s * 8 + kc + 1], ALU.mult, [R_ACTF, R_CF], [R_ACTB])
        nload = 0
        for l in range(nlayers):
            awv = ada_w[l].rearrange("(kc p) n -> p kc n", p=128)
            for j in range(6):
                bi = nload % 2
                nload += 1
                for kc in range(8):
                    dma("pool", ADW[bi][:, kc, :], awv[:, kc, j * 1024:(j + 1) * 1024], w=[R_ADW[bi]])
                if j in (0, 1, 3, 4):
                    blk = {0: 0, 1: 1, 3: 2, 4: 3}[j]
                    pb = psA[blk % 2]
                    for fc in range(8):
                        for kc in range(8):
                            MM(pb[:, 2 * fc:2 * fc + 2], ADW[bi][:, kc, fc * 128:(fc + 1) * 128], ACTT[:, kc, :],
                               kc == 0, kc == 7, [R_ADW[bi], R_ACTT], [R_ps[blk % 2]])
                    for s in range(2):
                        TT(MRAW[:, blk, s, :], pb[:, s:16:2], pc(l, "adab", blk * 8, blk * 8 + 8), ALU.add,
                           [R_ps[blk % 2], R_PCS], [R_MRAW])
                else:
                    gate = 0 if j == 2 else 1
                    dma("sp", GB[:, :], row_b(l, "adab_g1" if gate == 0 else "adab_g2"), w=[R_GB])
                    dma("sp", GP[:, :], row_b(l, "postg1" if gate == 0 else "postg2"), w=[R_GP])
                    for s in range(2):
                        for hf in range(2):
                            pi = 2 + hf
                            for kc in range(8):
                                MM(psA[pi][:, :], ACTB[:, s, kc, :], ADW[bi][:, kc, hf * 512:(hf + 1) * 512],
                                   kc == 0, kc == 7, [R_ADW[bi], R_ACTB], [R_ps[pi]])
                            TT(GO[hf][:, :], psA[pi][:, :], GB[:, hf * 512:(hf + 1) * 512], ALU.add,
                               [R_ps[pi], R_GB], [R_GO[hf]])
                            TT(GO[hf][:, :], GO[hf][:, :], GP[:, hf * 512:(hf + 1) * 512], ALU.mult, [R_GP], [R_GO[hf]])
                            gr = (l * 2 + s) * 2 + gate
                            dma("sp", grow[gr:gr + 1, hf * 512:(hf + 1) * 512], GO[hf][0:1, :], r=[R_GO[hf]], w=[R_grow])
            for s in range(2):
                for (which, sc_blk, sh_blk, gname) in ((0, 1, 0, "preg1"), (2, 3, 2, "preg2")):
                    STT(modv(l, which, s), MRAW[:, sc_blk, s, :], 1.0, pc(l, gname), ALU.add, ALU.mult,
                        [R_MRAW, R_PCS], [R_MOD])
                    CP(modv(l, which + 1, s), MRAW[:, sh_blk, s, :], [R_MRAW], [R_MOD])
        fw.barrier()
        fw.flush()
    if "mod" in dbg_out:
        dma("sp", dbg_out["mod"], MOD[:, :], r=[R_MOD])

    def norm_transpose(l, which, tiles, HT, R_HT, psb, R_psb, XS, R_XS, RSTD, R_RSTD, JUNK, R_JUNK):
        for t in tiles:
            ACT(JUNK[:, :], X[:, t, :], AF.Square, [R_X[t]], [R_JUNK, R_RSTD], accum_out=RSTD[:, t:t + 1])
        rstd_from_ss(RSTD[:, tiles[0]:tiles[-1] + 1], R_RSTD, D, NORM_EPS)
        for n_, t in enumerate(tiles):
            s = 0 if t < 16 else 1
            xb = n_ % 2
            TS(XS[xb][:, :], X[:, t, :], RSTD[:, t:t + 1], ALU.mult, [R_X[t], R_RSTD], [R_XS[xb]])
            for half in range(2):
                pi = (n_ * 2 + half) % len(psb)
                for j in range(4):
                    fc = half * 4 + j
                    TR(psb[pi][:, j * 128:(j + 1) * 128], XS[xb][:, fc * 128:(fc + 1) * 128], cf("ident"),
                       [R_XS[xb], R_CF], [R_psb[pi]])
                for j in range(4):
                    fc = half * 4 + j
                    ACT(HT[:, fc, col(t):col(t) + 128], psb[pi][:, j * 128:(j + 1) * 128], AF.Identity,
                        [R_psb[pi], R_MOD], [R_HT], scale=modv(l, which, s)[:, fc:fc + 1],
                        bias=modv(l, which + 1, s)[:, fc:fc + 1])

    def halo_exchange(st, HT, R_HT, R_HTh, snd, rcv, R_snd, R_rcv, l, tag):
        HS = sb(st, "HS" + tag, [128, 8])
        HR = sb(st, "HR" + tag, [128, 2, 8])
        R_HS, R_HR = fw.res(), fw.res()
        CP(HS[:, :], HT[:, :, NLAT - 1], [R_HT], [R_HS])
        dma("sp", snd[:, :], HS[:, :], r=[R_HS], w=[R_snd])
        CC(snd, rcv, R_snd, R_rcv)
        dma("sp", HR[:, :, :], rcv.rearrange("(r p) c -> p r c", p=128), r=[R_rcv], w=[R_HR])
        TS(HS[:, :], HR[:, 0, :], pc(l, "sel", 0, 1), ALU.mult, [R_HR, R_PCS], [R_HS])
        STT(HS[:, :], HR[:, 1, :], pc(l, "sel", 1, 2), HS[:, :], ALU.mult, ALU.add, [R_HR, R_PCS], [R_HS])
        CP(HT[:, :, HALO], HS[:, :], [R_HS], [R_HTh])

    oall_d = dscr("oall_d", [NTOK, D], BF16)
    R_oall = fw.res()
    ys_d = dscr("ys_d", [NTOK, 256])
    R_ysd_t = [fw.res() for _ in range(NT)]
    AXX = mybir.AxisListType.X

    def RWKV_PHASE(l, ctx_out, tiles_out):
        with ExitStack() as st:
            CR = sb(st, "CR", [128, NCR])
            R_CR = fw.res()
            dma("sp", CR[:, :], crd[:, :], w=[R_CR])

            def cr(name, c0=0, c1=None):
                o, w = CR_OFF[name]
                c1 = w if c1 is None else c1
                return CR[:, o + c0:o + c1]

            BS = sb(st, "BS", [128, NT, 4])
            R_BS = fw.res()
            W2 = sb(st, "W2", [128, 256])
            A2 = sb(st, "A2", [128, 256])
            G2w = sb(st, "G2w", [128, 256], BF16)
            W0R = sb(st, "W0R", [128, 512])
            GNG = sb(st, "GNG", [128, 256])
            GNB = sb(st, "GNB", [128, 256])
            MUC = sb(st, "MUC", [128, 9])
            OMKA = sb(st, "OMKA", [128, 2])
            DI2 = sb(st, "DI2", [128, 64])
            R_W = fw.res()
            dma("sp", W2[:, :], cw2[l, :, :], w=[R_W])
            dma("sp", A2[:, :], ca2[l, :, :], w=[R_W])
            dma("pool", G2w[:, :], cg2[l, :, :], w=[R_W])
            dma("sp", W0R[:, :], row_b(l, "w0"), w=[R_W])
            dma("sp", GNG[:, :], row_b(l, "gng"), w=[R_W])
            dma("sp", GNB[:, :], row_b(l, "gnb"), w=[R_W])
            TT(MUC[:, :], pc(l, "mup"), pc(l, "mun"), ALU.add, [R_PCS], [R_W])
            TS(MUC[:, :], MUC[:, :], -1.0, ALU.mult, [], [R_W], s2=1.0, op1=ALU.add)
            TS(OMKA[:, :], pc(l, "ka"), -1.0, ALU.mult, [R_PCS], [R_W], s2=1.0, op1=ALU.add)
            TT(DI2[:, :], cf("ident", c0=0, c1=64), cf("ident", c0=64, c1=128), ALU.add, [R_CF], [R_W])
            Hb = [sb(st, "Hb%d" % i, [128, 2, 64]) for i in range(2)]
            R_H = [fw.res(), fw.res()]
            HR = sb(st, "HRr", [128, 2, 128])
            R_HR = fw.res()
            PCB = sb(st, "PCB", [128, 9, 130])
            SH = sb(st, "SH", [128, 9, 128])
            SH2 = sb(st, "SH2", [128, 9, 128])
            R_SH2 = fw.res()
            TW = sb(st, "TW", [128, 128])
            AA = sb(st, "AA", [128, 2, 128])
            LW = sb(st, "LW", [128, 256])
            KS = sb(st, "KS", [128, 2, 128])
            SQf = sb(st, "SQf", [128, 2, 128])
            RN = sb(st, "RN", [128, 2, 128])
            KK = sb(st, "KK", [128, 2, 128])
            TMP = sb(st, "TMPr", [128, 2, 128])
            KD = sb(st, "KD", [128, 2, 128])
            BB = sb(st, "BB", [128, 2, 128])
            BT = sb(st, "BT", [128, 2, 128])
            KTt = sb(st, "KTt", [128, 2, 128])
            BH = sb(st, "BH", [128, 2, 128])
            KH = sb(st, "KH", [128, 2, 128])
            ATm = sb(st, "ATm", [128, 2, 2, 128])
            BTm = sb(st, "BTm", [128, 2, 2, 128])
            KTm = sb(st, "KTm", [128, 2, 2, 128])
            SGs = [sb(st, "SG%d" % i, [128, 128], BF16) for i in range(2)]
            EEs = [sb(st, "EE%d" % i, [128, 2, 4, 128]) for i in range(2)]
            ARs = [sb(st, "AR%d" % i, [128, 2, 2, 128]) for i in range(2)]
            TOKs = [sb(st, "TOK%d" % i, [128, 2, 4, 128]) for i in range(2)]
            NMSs = [sb(st, "NMS%d" % i, [128, 4, 3, 128]) for i in range(2)]
            X0s = [sb(st, "X0_%d" % i, [128, 4, 128]) for i in range(2)]
            XT0s = [sb(st, "XT0_%d" % i, [128, 4, 128]) for i in range(2)]
            Z0s = [sb(st, "Z0_%d" % i, [128, 4, 128]) for i in range(2)]
            YLs = [sb(st, "YL%d" % i, [128, 256]) for i in range(2)]
            R_SGs, R_EEs, R_ARs, R_TOKs, R_NMSs, R_X0s, R_XT0s, R_Z0s, R_YLs = [[fw.res(), fw.res()] for _ in range(9)]
            TOKm = sb(st, "TOKm", [128, 2, 2, 2, 128])
            XB = [sb(st, "XB%d" % i, [128, 4, 128]) for i in range(2)]
            XTB = [sb(st, "XTB%d" % i, [128, 4, 128]) for i in range(2)]
            ZBf = [sb(st, "ZBf%d" % i, [128, 4, 128]) for i in range(2)]
            AN = sb(st, "AN", [128, 4, 128])
            WU = sb(st, "WUr", [128, 4, 128])
            WUm = sb(st, "WUm", [128, 2, 4, 128])
            WRT = sb(st, "WRT", [128, 2, 128])
            WRTm = sb(st, "WRTm", [128, 2, 2, 128])
            YV = sb(st, "YV", [128, 4, 64])
            PTS = sb(st, "PTS", [128, 2, 2, 64])
            PTSm = sb(st, "PTSm", [128, 2, 2, 2, 64])
            QS = sb(st, "QS", [128, 2, 2, 64])
            YSQ = sb(st, "YSQ", [128, 256])
            YN = sb(st, "YN", [128, 256])
            OC = [sb(st, "OC%d" % i, [128, 256], BF16) for i in range(2)]
            STt = sb(st, "STt", [128, 12])
            (R_PCB, R_SH, R_TW, R_AA, R_LW, R_KS, R_SQf, R_RN, R_KK, R_TMP, R_KD, R_BB, R_BT, R_KTt,
             R_BH, R_KH, R_ATm, R_BTm, R_KTm, R_TOKm, R_AN, R_WU, R_WUm, R_WRT, R_WRTm, R_YV, R_PTS,
             R_PTSm, R_QS, R_YSQ, R_YN, R_ST) = [fw.res() for _ in range(32)]
            R_XB, R_XTB, R_ZBf, R_OC = [[fw.res(), fw.res()] for _ in range(4)]
            bk = [pst(st, "psR%d" % i, [128, 512]) for i in range(8)]
            R_bk = [fw.pres() for _ in range(8)]
            b0, b1, bA, bB, bI1, bI2, bI3, bPQ = bk
            R_b0, R_b1, R_bA, R_bB, R_bI1, R_bI2, R_bI3, R_bPQ = R_bk
            bCH, R_bCH = bI3, R_bI3
            pcv = pcT_d.rearrange("(c p) n -> p c n", p=128)
            ident = cf("ident")
            hsel = [cf("hsel", c0=0, c1=1), cf("hsel", c0=1, c1=2)]

            def v3(ap, a):
                return ap.rearrange("p (a b) -> p a b", a=a)

            def MASK(out, in_, hcol, r, w):
                ACT(out, in_, AF.Identity, r + [R_CF], w, scale=hcol)

            def stage_a(d, t, final, pp):
                EE, AR, TOK, NMS, SG = EEs[pp], ARs[pp], TOKs[pp], NMSs[pp], SGs[pp]
                R_EE, R_AR, R_TOK, R_NMS, R_SG = R_EEs[pp], R_ARs[pp], R_TOKs[pp], R_NMSs[pp], R_SGs[pp]
                X0, XT0, Z0 = X0s[pp], XT0s[pp], Z0s[pp]
                R_X0, R_XT0, R_Z0 = R_X0s[pp], R_XT0s[pp], R_Z0s[pp]
                c0 = col(t)
                has_prev = t not in (0, 16)
                has_next = t != 17
                a_ = c0 - 1 if has_prev else c0
                b_ = c0 + 129 if has_next else c0 + 128
                dma("sp", PCB[:, :, a_ - (c0 - 1):b_ - (c0 - 1)], pcv[:, :, a_:b_], r=[R_pcT], w=[R_PCB])
                if not has_prev:
                    MSET(PCB[:, :, 0:1], 0.0, [R_PCB])
                if not has_next:
                    MSET(PCB[:, :, 129:130], 0.0, [R_PCB])
                if d == 1:
                    dma("sp", YLs[pp][:, :], ys_d[t * 128:(t + 1) * 128, :], r=[R_ysd_t[t]], w=[R_YLs[pp]])
                nch = 9 if final else 8
                bc = lambda ap: ap.unsqueeze(2).to_broadcast([128, nch, 128])
                TT(SH[:, 0:nch, :], PCB[:, 0:nch, 1:129], bc(MUC[:, 0:nch]), ALU.mult, [R_PCB, R_W], [R_SH])
                TT(SH2[:, 0:nch, :], PCB[:, 0:nch, 0:128], bc(pc(l, "mup", 0, nch)), ALU.mult, [R_PCB, R_PCS], [R_SH2], eng="pool")
                TT(SH[:, 0:nch, :], SH[:, 0:nch, :], SH2[:, 0:nch, :], ALU.add, [R_SH2], [R_SH])
                TT(SH2[:, 0:nch, :], PCB[:, 0:nch, 2:130], bc(pc(l, "mun", 0, nch)), ALU.mult, [R_PCB, R_PCS], [R_SH2], eng="pool")
                TT(SH[:, 0:nch, :], SH[:, 0:nch, :], SH2[:, 0:nch, :], ALU.add, [R_SH2], [R_SH])
                ACT(TW[:, :], SH[:, 6, :], AF.Tanh, [R_SH], [R_TW])
                if final:
                    ACT(SG[:, :], SH[:, 8, :], AF.Sigmoid, [R_SH], [R_SG])
                r0 = 64 * d
                for hp in range(2):
                    MM(b0[:, hp * 128:(hp + 1) * 128], A2[r0:r0 + 64, hp * 128:(hp + 1) * 128], SH[r0:r0 + 64, 7, :], True, True,
                       [R_W, R_SH], [R_b0])
                MM(b0[:, 256:512], TW[r0:r0 + 64, :], W2[r0:r0 + 64, :], True, True, [R_TW, R_W], [R_b0])
                for hp in range(2):
                    ACT(AA[:, hp, :], b0[:, hp * 128:(hp + 1) * 128], AF.Sigmoid, [R_b0, R_PCS], [R_AA],
                        bias=pc(l, "a0", d * 2 + hp, d * 2 + hp + 1))
                TT(LW[:, :], b0[:, 256:512], W0R[:, d * 256:(d + 1) * 256], ALU.add, [R_b0, R_W], [R_LW])
                ACT(LW[:, :], LW[:, :], AF.Sigmoid, [], [R_LW])
                tri = cr("trif") if d == 0 else cr("trib")
                for hp in range(2):
                    MM(b1[:, 0:384], LW[:, hp * 128:(hp + 1) * 128], tri, True, True, [R_LW, R_CR], [R_b1])
                    ACT(EE[:, hp, 0:2, :], v3(b1[:, 0:256], 2), AF.Exp, [R_b1], [R_EE])
                    ACT(EE[:, hp, 2, :], b1[:, 0:128], AF.Exp, [R_b1], [R_EE], scale=-1.0)
                    ACT(EE[:, hp, 3, :], b1[:, 256:384], AF.Exp, [R_b1], [R_EE])
                for hp in range(2):
                    TS(KS[:, hp, :], SH[:, 2 + hp, :], pc(l, "kk", hp, hp + 1), ALU.mult, [R_SH, R_PCS], [R_KS])
                ACT(SQf[:, :, :], KS[:, :, :], AF.Square, [R_KS], [R_SQf])
                MM(bA[:, 0:256], cf("blk"), SQf[:, :, :], True, True, [R_CF, R_SQf], [R_bA])
                TS(RN[:, :, :], v3(bA[:, 0:256], 2), 1e-24, ALU.max, [R_bA], [R_RN])
                ACT(RN[:, :, :], RN[:, :, :], AF.Sqrt, [], [R_RN])
                RECIP(RN[:, :, :], RN[:, :, :], [], [R_RN])
                TT(KK[:, :, :], KS[:, :, :], RN[:, :, :], ALU.mult, [R_KS, R_RN], [R_KK])
                for hp in range(2):
                    TS(TMP[:, hp, :], AA[:, hp, :], pc(l, "ka", hp, hp + 1), ALU.mult, [R_AA, R_PCS, R_W], [R_TMP],
                       s2=OMKA[:, hp:hp + 1], op1=ALU.add)
                TT(KD[:, :, :], TMP[:, :, :], SH[:, 2:4, :], ALU.mult, [R_TMP, R_SH], [R_KD])
                TT(BB[:, :, :], KK[:, :, :], AA[:, :, :], ALU.mult, [R_KK, R_AA], [R_BB], eng="pool")
                TT(TMP[:, :, :], SH[:, 0:2, :], KD[:, :, :], ALU.mult, [R_SH, R_KD], [R_TMP])
                for hp in range(2):
                    TS(TMP[:, hp, :], TMP[:, hp, :], pc(l, "rk", hp, hp + 1), ALU.mult, [R_PCS], [R_TMP])
                for hp in range(2):
                    MM(bB[:, 256 + hp * 2:256 + hp * 2 + 2], TMP[:, hp, :], cf("hsel"), True, True, [R_TMP, R_CF], [R_bB])
                if d == 0:
                    CP(BS[:, t, :], bB[:, 256:260], [R_bB], [R_BS])
                else:
                    TT(BS[:, t, :], bB[:, 256:260], BS[:, t, :], ALU.add, [R_bB], [R_BS])
                STT(AR[:, :, 0, :], KK[:, :, :], -1.0, EE[:, :, 1, :], ALU.mult, ALU.mult, [R_KK, R_EE], [R_AR])
                TT(AR[:, :, 1, :], SH[:, 0:2, :], EE[:, :, 0, :], ALU.mult, [R_SH, R_EE], [R_AR], eng="pool")
                TT(BT[:, :, :], BB[:, :, :], EE[:, :, 2, :], ALU.mult, [R_BB, R_EE], [R_BT])
                TT(KTt[:, :, :], KD[:, :, :], EE[:, :, 2, :], ALU.mult, [R_KD, R_EE], [R_KTt], eng="pool")
                TT(BH[:, :, :], BB[:, :, :], EE[:, :, 3, :], ALU.mult, [R_BB, R_EE], [R_BH])
                TT(KH[:, :, :], KD[:, :, :], EE[:, :, 3, :], ALU.mult, [R_KD, R_EE], [R_KH], eng="pool")
                for hh in range(2):
                    MASK(ATm[:, hh, :, :], AR[:, :, 0, :], hsel[hh], [R_AR], [R_ATm])
                    MASK(BTm[:, hh, :, :], BT[:, :, :], hsel[hh], [R_BT], [R_BTm])
                    MASK(KTm[:, hh, :, :], KTt[:, :, :], hsel[hh], [R_KTt], [R_KTm])
                for hp in range(2):
                    srcs = [AR[:, hp, 0, :], BH[:, hp, :], KH[:, hp, :], SH[:, 4 + hp, :]]
                    for q, s_ in enumerate(srcs):
                        TR(b0[:, q * 128:(q + 1) * 128], s_, ident, [R_AR, R_BH, R_KH, R_SH, R_CF], [R_b0])
                    CP(TOK[:, hp, :, :], v3(b0[:, :], 4), [R_b0], [R_TOK])
                if d == 0:
                    m_si, m_s, m_i, m_x = cr("Us", 0, 256), cr("Us"), cr("Ui"), cr("Ls")
                else:
                    m_si, m_s, m_i, m_x = cr("Ls", 0, 256), cr("Ls"), cr("Li"), cr("Us")
                for h in range(4):
                    hp, hh = h // 2, h % 2
                    MM(bA[:, 0:256], BTm[:, hh, hp, :], AR[:, hp, :, :], True, True, [R_BTm, R_AR], [R_bA])
                    MM(bA[:, 256:384], ATm[:, hh, hp, :], BT[:, hp, :], True, True, [R_ATm, R_BT], [R_bA])
                    MM(bB[:, 0:256], KTm[:, hh, hp, :], AR[:, hp, :, :], True, True, [R_KTm, R_AR], [R_bB])
                    TT(XT0[:, h, :], bA[:, 0:128], m_s, ALU.mult, [R_bA, R_CR], [R_XT0])
                    TT(NMS[:, h, 0, :], bA[:, 128:256], m_i, ALU.mult, [R_bA, R_CR], [R_NMS])
                    TT(X0[:, h, :], bA[:, 256:384], m_x, ALU.mult, [R_bA, R_CR], [R_X0])
                    TT(NMS[:, h, 1:3, :], v3(bB[:, 0:256], 2), v3(m_si, 2), ALU.mult, [R_bB, R_CR], [R_NMS])
                    TT(Z0[:, h, :], XT0[:, h, :], ident, ALU.add, [R_XT0, R_CF], [R_Z0], eng="pool")

            def stage_b(d, t, pp):
                EE, AR, TOK, NMS = EEs[pp], ARs[pp], TOKs[pp], NMSs[pp]
                R_EE, R_AR, R_TOK, R_NMS = R_EEs[pp], R_ARs[pp], R_TOKs[pp], R_NMSs[pp]
                for c in range(2):
                    MASK(TOKm[:, c, :, :, :], TOK[:, :, 1:3, :], hsel[c], [R_TOK], [R_TOKm])
                Xc, XTc, Zc = X0s[pp], XT0s[pp], Z0s[pp]
                R_Xc, R_XTc, R_Zc = R_X0s[pp], R_XT0s[pp], R_Z0s[pp]
                for m in range(5):
                    nx = m % 2
                    for h in range(4):
                        MM(bI1[:, h * 128:(h + 1) * 128], XTc[:, h, :], Xc[:, h, :], True, True, [R_XTc, R_Xc], [R_bI1])
                    if m < 4:
                        for h in range(4):
                            MM(bI2[:, h * 128:(h + 1) * 128], Xc[:, h, :], XTc[:, h, :], True, True, [R_XTc, R_Xc], [R_bI2])
                    CP(XB[nx][:, :, :], v3(bI1[:, :], 4), [R_bI1], [R_XB[nx]])
                    if m < 4:
                        CP(XTB[nx][:, :, :], v3(bI2[:, :], 4), [R_bI2], [R_XTB[nx]], eng="act")
                    for h in range(4):
                        MM(bI3[:, h * 128:(h + 1) * 128], XB[nx][:, h, :], Zc[:, h, :], True, True, [R_XB[nx], R_Zc], [R_bI3])
                    TT(ZBf[nx][:, :, :], v3(bI3[:, :], 4), Zc[:, :, :], ALU.add, [R_bI3, R_Zc], [R_ZBf[nx]])
                    Xc, XTc, Zc = XB[nx], XTB[nx], ZBf[nx]
                    R_Xc, R_XTc, R_Zc = R_XB[nx], R_XTB[nx], R_ZBf[nx]
                ZF, R_ZF = Zc, R_Zc
                for h in range(4):
                    hp, hh = h // 2, h % 2
                    MM(bI1[:, h * 64:(h + 1) * 64], NMS[:, h, 1, :], TOK[:, hp, 3, hh * 64:(hh + 1) * 64], True, True,
                       [R_NMS, R_TOK], [R_bI1])
                CP(AN[:, :, 64:128], v3(bI1[:, 0:256], 4), [R_bI1], [R_AN])
                for hp in range(2):
                    CP(AN[:, 2 * hp:2 * hp + 2, 0:64], v3(TOK[:, hp, 0, :], 2), [R_TOK], [R_AN], eng="act")
                for h in range(4):
                    MM(bI2[:, h * 128:(h + 1) * 128], ZF[:, h, :], AN[:, h, :], True, True, [R_ZF, R_AN], [R_bI2])
                CP(WU[:, :, :], v3(bI2[:, :], 4), [R_bI2], [R_WU])
                for c in range(2):
                    MASK(WUm[:, c, :, :], WU[:, :, :], hsel[c], [R_WU], [R_WUm])
                for h in range(4):
                    hp, hh = h // 2, h % 2
                    pb = 64 * hh
                    MM(bI3[pb:pb + 64, hp * 128:(hp + 1) * 128], WU[:, h, 0:64], NMS[:, h, 0, :], True, True,
                       [R_WU, R_NMS], [R_bI3])
                TT(WRT[:, :, :], v3(bI3[:, 0:256], 2), AR[:, :, 1, :], ALU.add, [R_bI3, R_AR], [R_WRT])
                for hh in range(2):
                    MASK(WRTm[:, hh, :, :], WRT[:, :, :], hsel[hh], [R_WRT], [R_WRTm])
                for h in range(4):
                    hp, hh = h // 2, h % 2
                    MM(bI1[:, 256 + h * 64:256 + (h + 1) * 64], NMS[:, h, 0, :], WU[:, h, 64:128], True, False,
                       [R_NMS, R_WU], [R_bI1])
                    MM(bI1[:, 256 + h * 64:256 + (h + 1) * 64], NMS[:, h, 2, :], TOK[:, hp, 3, hh * 64:(hh + 1) * 64], False, True,
                       [R_NMS, R_TOK], [R_bI1])
                CP(YV[:, :, :], v3(bI1[:, 256:512], 4), [R_bI1], [R_YV])
                for c in range(2):
                    for h in range(4):
                        hp, hh = h // 2, h % 2
                        pb = 64 * hh
                        o_ = (c * 2 + hp) * 64
                        MM(bPQ[pb:pb + 64, o_:o_ + 64], WUm[:, c, h, 0:64], TOK[:, hp, 1, hh * 64:(hh + 1) * 64], True, True,
                           [R_WUm, R_TOK], [R_bPQ])
                for c in range(2):
                    colc = c * 64 + (63 if d == 0 else 0)
                    for hp in range(2):
                        o_ = (c * 2 + hp) * 64
                        STT(PTS[:, c, hp, :], DI2[:, :], EE[:, hp, 0, colc:colc + 1], bPQ[:, o_:o_ + 64], ALU.mult, ALU.add,
                            [R_W, R_EE, R_bPQ], [R_PTS])
                for hh in range(2):
                    MASK(PTSm[:, hh, :, :, :], PTS[:, :, :, :], hsel[hh], [R_PTS], [R_PTSm])
                for c in range(2):
                    for h in range(4):
                        hp, hh = h // 2, h % 2
                        pb = 64 * hh
                        o_ = 256 + (c * 2 + hp) * 64
                        MM(bPQ[pb:pb + 64, o_:o_ + 64], TOKm[:, c, hp, 0, hh * 64:(hh + 1) * 64], WU[:, h, 64:128], True, False,
                           [R_TOKm, R_WU], [R_bPQ])
                        MM(bPQ[pb:pb + 64, o_:o_ + 64], TOKm[:, c, hp, 1, hh * 64:(hh + 1) * 64],
                           TOK[:, hp, 3, hh * 64:(hh + 1) * 64], False, True, [R_TOKm, R_TOK], [R_bPQ])
                CP(QS[:, :, :, :], bPQ[:, 256:512].rearrange("p (a b c) -> p a b c", a=2, b=2), [R_bPQ], [R_QS], eng="act")

            def tile_chain(d, t, cur, pp):
                order = [0, 1] if d == 0 else [1, 0]
                for c in order:
                    cb = 64 * c
                    nx = 1 - cur
                    for h in range(4):
                        hp, hh = h // 2, h % 2
                        MM(bCH[cb:cb + 64, h * 64:(h + 1) * 64], WRTm[:, hh, hp, cb:cb + 64], Hb[cur][:, hp, :], True, True,
                           [R_WRTm, R_H[cur]], [R_bCH])
                    for h in range(4):
                        hp, hh = h // 2, h % 2
                        pb = 64 * hh
                        MM(bCH[pb:pb + 64, 256 + hp * 64:256 + (hp + 1) * 64], PTSm[:, hh, c, hp, :], Hb[cur][:, hp, :], True, True,
                           [R_PTSm, R_H[cur]], [R_bCH])
                    TT(Hb[nx][:, :, :], v3(bCH[:, 256:384], 2), QS[:, c, :, :], ALU.add, [R_bCH, R_QS], [R_H[nx]])
                    cur = nx
                YL, R_YL = YLs[pp], R_YLs[pp]
                if d == 0:
                    TT(v3(YL[:, :], 4), v3(bCH[:, 0:256], 4), YV[:, :, :], ALU.add, [R_bCH, R_YV], [R_YL])
                    dma("sp", ys_d[t * 128:(t + 1) * 128, :], YL[:, :], r=[R_YL], w=[R_ysd_t[t]])
                else:
                    TT(v3(YSQ[:, :], 4), v3(bCH[:, 0:256], 4), YV[:, :, :], ALU.add, [R_bCH, R_YV], [R_YSQ])
                    TT(YL[:, :], YL[:, :], YSQ[:, :], ALU.add, [R_YSQ], [R_YL])
                return cur

            def tile_final(t, n_, pp):
                ob = n_ % 2
                YL, R_YL = YLs[pp], R_YLs[pp]
                TOK, R_TOK, SG, R_SG = TOKs[pp], R_TOKs[pp], SGs[pp], R_SGs[pp]
                op("dve", lambda e: e.tensor_reduce(out=STt[:, 0:4], in_=v3(YL[:, :], 4), axis=AXX, op=ALU.add),
                   [R_YL], [R_ST])
                ACT(YSQ[:, :], YL[:, :], AF.Square, [R_YL], [R_YSQ])
                op("dve", lambda e: e.tensor_reduce(out=STt[:, 4:8], in_=v3(YSQ[:, :], 4), axis=AXX, op=ALU.add),
                   [R_YSQ], [R_ST])
                TS(STt[:, 0:4], STt[:, 0:4], 1.0 / 64, ALU.mult, [], [R_ST])
                TT(STt[:, 8:12], STt[:, 0:4], STt[:, 0:4], ALU.mult, [], [R_ST])
                STT(STt[:, 4:8], STt[:, 4:8], 1.0 / 64, STt[:, 8:12], ALU.mult, ALU.subtract, [], [R_ST])
                ACT(STt[:, 4:8], STt[:, 4:8], AF.Sqrt, [R_CF], [R_ST], bias=EPS_T[GN_EPS], scale=1.0)
                RECIP(STt[:, 4:8], STt[:, 4:8], [], [R_ST])
                for h in range(4):
                    TS(YN[:, h * 64:(h + 1) * 64], YL[:, h * 64:(h + 1) * 64], STt[:, h:h + 1], ALU.subtract, [R_YL, R_ST],
                       [R_YN], s2=STt[:, 4 + h:5 + h], op1=ALU.mult)
                TT(YN[:, :], YN[:, :], GNG[:, :], ALU.mult, [R_W], [R_YN])
                TT(YN[:, :], YN[:, :], GNB[:, :], ALU.add, [R_W], [R_YN])
                for h in range(4):
                    hp, hh = h // 2, h % 2
                    STT(YN[:, h * 64:(h + 1) * 64], TOK[:, hp, 3, hh * 64:(hh + 1) * 64], BS[:, t, h:h + 1],
                        YN[:, h * 64:(h + 1) * 64], ALU.mult, ALU.add, [R_TOK, R_BS], [R_YN])
                MM(bI2[:, 0:256], SG[:, :], G2w[:, :], True, True, [R_SG, R_W], [R_bI2])
                TT(OC[ob][:, :], YN[:, :], bI2[:, 0:256], ALU.mult, [R_YN, R_bI2], [R_OC[ob]])
                dma("sp", oall_d[t * 128:(t + 1) * 128, 768:1024], OC[ob][:, :], r=[R_OC[ob]], w=[R_oall])

            def run_seq(d, tiles, cur, final, n0):
                n_ = n0
                if not tiles:
                    return cur, n_
                stage_a(d, tiles[0], final, n_ % 2)
                for i, t in enumerate(tiles):
                    pp = n_ % 2
                    if i + 1 < len(tiles):
                        stage_a(d, tiles[i + 1], final, (n_ + 1) % 2)
                    stage_b(d, t, pp)
                    cur = tile_chain(d, t, cur, pp)
                    if final:
                        tile_final(t, n_, pp)
                    n_ += 1
                return cur, n_

            MSET(Hb[0][:, :, :], 0.0, [R_H[0]])
            cur, n_ = run_seq(0, [16, 17] + list(range(16)), 0, False, 0)
            Hflat = lambda i: Hb[i][:, :, :].rearrange("p a b -> p (a b)")
            dma("sp", snd3[:, :], Hflat(cur), r=[R_H[cur]], w=[R_snd3])
            CC(snd3, rcv3, R_snd3, R_rcv3)
            dma("sp", HR[:, :, :], rcv3.rearrange("(r p) c -> p r c", p=128), r=[R_rcv3], w=[R_HR])
            TS(Hflat(0), HR[:, 0, :], pc(l, "sel", 0, 1), ALU.mult, [R_HR, R_PCS], [R_H[0]])
            STT(Hflat(0), HR[:, 1, :], pc(l, "sel", 1, 2), Hflat(0), ALU.mult, ALU.add, [R_HR, R_PCS], [R_H[0]])
            cur, n_ = run_seq(1, list(range(15, -1, -1)), 0, True, n_)
            if ctx_out:
                MSET(Hb[0][:, :, :], 0.0, [R_H[0]])
                cur, n_ = run_seq(1, [17, 16], 0, True, n_)
            fw.barrier()
            fw.flush()

    TBS = [(i * 512, 512, i * 512) for i in range(4)] + [(CTX0, 256, NLAT)]

    for l in range(nlayers):
        ctx_out = l < L - 1
        lam_init = 0.8 - 0.6 * math.exp(-0.3 * l)
        tiles_all = list(range(NT))
        tiles_out = tiles_all if ctx_out else list(range(16))
        if do_ffn:
            for rblk in range(8):
                for cb_ in range(4):
                    dma("pool", wup_bf[rblk * 128:(rblk + 1) * 128, cb_ * 1408:(cb_ + 1) * 1408],
                        ffn_up[l, rblk * 128:(rblk + 1) * 128, cb_ * 1408:(cb_ + 1) * 1408], w=[R_wup])
            for rblk in range(22):
                dma("pool", wdn_bf[rblk * 128:(rblk + 1) * 128, :], ffn_dn[l, rblk * 128:(rblk + 1) * 128, :], w=[R_wdn])

        with ExitStack() as st:
            if not do_p:
                break
            HT = sb(st, "HT", [128, 8, NCOL], BF16)
            R_HT = fw.res()
            XS = [sb(st, "XS%d" % i, [128, D]) for i in range(2)]
            R_XS = [fw.res(), fw.res()]
            RSTD = sb(st, "RSTD", [128, NT])
            R_RSTD = fw.res()
            WB = [sb(st, "WB%d" % i, [128, 8, 512], BF16) for i in range(2)]
            R_WB = [fw.res(), fw.res()]
            ps = [pst(st, "psP%d" % i, [128, 512]) for i in range(8)]
            R_ps = [fw.pres() for _ in range(8)]
            norm_transpose(l, 0, tiles_all, HT, R_HT, ps[0:4], R_ps[0:4], XS, R_XS, RSTD, R_RSTD, XS[1], R_XS[1])
            STAGE(1)
            R_HTh = fw.res()
            halo_exchange(st, HT, R_HT, R_HTh, snd4, rcv4, R_snd4, R_rcv4, l, "a")
            STAGE(2)
            if "hT" in dbg_out and l == dbgl:
                for fc in range(8):
                    for c0_ in range(0, NCOL, 1024):
                        n_ = min(1024, NCOL - c0_)
                        CP(XS[0][:, 0:n_], HT[:, fc, c0_:c0_ + n_], [R_HT], [R_XS[0]])
                        dma("sp", dbg_out["hT"][fc * 128:(fc + 1) * 128, c0_:c0_ + n_], XS[0][:, 0:n_], r=[R_XS[0]])
            wv = w_in[l].rearrange("(kc p) n -> p kc n", p=128)
            nwl = [0]

            def load_w(c0, c1):
                bi = nwl[0] % 2
                nwl[0] += 1
                for kc in range(8):
                    dma("pool", WB[bi][:, kc, 0:c1 - c0], wv[:, kc, c0:c1], w=[R_WB[bi]])
                return bi

            ROPE = [sb(st, "ROPE%d" % i, [128, 2, 512]) for i in range(2)]
            R_ROPE = [fw.res(), fw.res()]
            QF = [sb(st, "QF%d" % i, [128, 512], BF16) for i in range(2)]
            R_QF = [fw.res(), fw.res()]
            T1 = [sb(st, "T1_%d" % i, [128, 512]) for i in range(2)]
            R_T1 = [fw.res(), fw.res()]
            T2 = [sb(st, "T2_%d" % i, [128, 512]) for i in range(2)]
            R_T2 = [fw.res(), fw.res()]
            QO = [sb(st, "QO%d" % i, [128, NTOK], BF16) for i in range(2)]
            R_QO = [fw.res(), fw.res()]
            RAB = sb(st, "RAB", [128, 288], BF16)
            R_RAB = fw.res()
            CP(RAB[:, 0:128], cf("RA"), [R_CF], [R_RAB])
            CP(RAB[:, 128:256], cf("RB"), [R_CF], [R_RAB])
            CP(RAB[:, 256:288], cf("RK"), [R_CF], [R_RAB])
            cnt = [0]

            def proj_fm(bi, wc0, M, c0, n, pbank, extra=()):
                for kc in range(8):
                    MM(ps[pbank][0:M, 0:n], WB[bi][:, kc, wc0:wc0 + M], HT[:, kc, c0:c0 + n], kc == 0, kc == 7,
                       [R_WB[bi], R_HT] + list(extra), [R_ps[pbank]])

            def rope_block(M, n, pbank, p2, rcol, cos_ap, sin_ap, r_tab, dst_ap, rdst, k):
                CP(QF[k][0:M, 0:n], ps[pbank][0:M, 0:n], [R_ps[pbank]], [R_QF[k]], eng="act")
                STAGE(31)
                MM(ps[p2][0:M, 0:n], RAB[0:M, rcol:rcol + M], QF[k][0:M, 0:n], True, True, [R_RAB, R_QF[k]], [R_ps[p2]])
                STAGE(32)
                import os as _os
                _v = _os.environ.get("TTV", "")
                if _v == "a":
                    TT(T1[k][0:M, 0:n], T2[k][0:M, 0:n], cos_ap, ALU.mult, [R_ps[pbank], r_tab], [R_T1[k]])
                elif _v == "c":
                    TT(T1[k][0:M, 0:n], ps[pbank][0:M, 0:n], T2[k][0:M, 0:n], ALU.mult, [R_ps[pbank], r_tab], [R_T1[k]])
                elif _v == "d":
                    TT(T1[k][0:M, 0:n], ps[pbank][0:M, 0:n], cos_ap, ALU.mult, [R_ps[pbank], r_tab, R_QF[k]], [R_T1[k]])
                else:
                    TT(T1[k][0:M, 0:n], ps[pbank][0:M, 0:n], cos_ap, ALU.mult, [R_ps[pbank], r_tab], [R_T1[k]])
                STAGE(33)
                TT(T2[k][0:M, 0:n], ps[p2][0:M, 0:n], sin_ap, ALU.mult, [R_ps[p2], r_tab], [R_T2[k]])
                STAGE(34)
                TT(dst_ap, T1[k][0:M, 0:n], T2[k][0:M, 0:n], ALU.add, [R_T1[k], R_T2[k]], [rdst], eng="pool")

            for grp in range(2):
                bi = load_w(grp * 512, grp * 512 + 512)
                STAGE(21)
                for hh in range(4):
                    ci = grp * 4 + hh
                    qo = ci % 2
                    for (c0, n, t0) in TBS:
                        k = cnt[0] % 2
                        cnt[0] += 1
                        for i in range(2):
                            dma("sp", ROPE[k][:, i, 0:n], ropeA[i, :, c0:c0 + n], w=[R_ROPE[k]])
                        STAGE(22)
                        proj_fm(bi, hh * 128, 128, c0, n, k)
                        STAGE(23)
                        rope_block(128, n, k, 2 + k, 0, ROPE[k][:, 0, 0:n], ROPE[k][:, 1, 0:n], R_ROPE[k],
                                   QO[qo][:, t0:t0 + n], R_QO[qo], k)
                        STAGE(24)
                    STAGE(25)
                    if grp == 0:
                        dma("sp", qA_d[hh * 128:(hh + 1) * 128, :], QO[qo][:, :], r=[R_QO[qo]], w=[R_qA])
                    else:
                        dma("sp", snd1A[hh // 2][(hh % 2) * 128:(hh % 2 + 1) * 128, :], QO[qo][:, 0:NLAT], r=[R_QO[qo]], w=[R_snd1])
                        dma("sp", kctx_d[hh * 128:(hh + 1) * 128, :], QO[qo][:, NLAT:NTOK], r=[R_QO[qo]], w=[R_kctx])
                    if ("qkA" in dbg_out) and l == dbgl:
                        for (c0, n, t0) in TBS:
                            CP(T1[0][:, 0:n], QO[qo][:, t0:t0 + n], [R_QO[qo]], [R_T1[0]])
                            dma("sp", dbg_out["qkA"][ci * 128:(ci + 1) * 128, t0:t0 + n], T1[0][:, 0:n], r=[R_T1[0]])
            STAGE(3)
            VT = [sb(st, "VT%d" % i, [128, 768], BF16) for i in range(2)]
            R_VT = [fw.res(), fw.res()]
            WQ = sb(st, "WQ", [128, 2, 384], BF16)
            WQF = sb(st, "WQF", [128, 2, 384])
            WKV = sb(st, "WKV", [128, 512], BF16)
            WKVF = sb(st, "WKVF", [128, 512])
            R_WQ, R_WKV, R_WQF, R_WKVF = fw.res(), fw.res(), fw.res(), fw.res()
            MSET(WQF[:, :, :], 0.0, [R_WQF])
            dma("sp", WQF[:, 0, :], wq_up[l, 0:128, :], w=[R_WQF])
            dma("sp", WQF[0:64, 1, :], wq_up[l, 128:192, :], w=[R_WQF])
            dma("sp", WKVF[:, :], wkv_up[l, :, :], w=[R_WKVF])
            for c in range(2):
                TS(WQ[:, c, :], WQF[:, c, :], pc(l, "qng", c, c + 1), ALU.mult, [R_WQF, R_PCS], [R_WQ])
            TS(WKV[:, :], WKVF[:, :], pc(l, "kvng", 0, 1), ALU.mult, [R_WKVF, R_PCS], [R_WKV])
            STAGE(4)
            bV = load_w(1024, 1536)
            bB = load_w(1536, 1888)
            CQ = [sb(st, "CQ0", [128, 2, 512], BF16)] * 2
            SQ = [sb(st, "SQ0", [128, 3, 512], BF16)] * 2
            CKV = [sb(st, "CKV0", [128, 512], BF16)] * 2
            RQ = [sb(st, "RQ0", [128, 2, 512])] * 2
            RPB = [sb(st, "RPB0", [128, 2, 512])] * 2
            RPK = [sb(st, "RPK0", [32, 2, 512])] * 2
            RKT = [sb(st, "RKT%d" % i, [128, 4]) for i in range(2)]
            KR = [sb(st, "KR%d" % i, [32, 512], BF16) for i in range(2)]
            KN = [sb(st, "KN%d" % i, [64, 512], BF16) for i in range(2)]
            QBO = [sb(st, "QBO%d" % i, [128, 512], BF16) for i in range(2)]
            R_CQ, R_SQ, R_CKV, R_RQ, R_RPB, R_RPK = [[fw.res()] * 2 for _ in range(6)]
            R_RKT, R_KR, R_KN, R_QBO = [[fw.res(), fw.res()] for _ in range(4)]
            for bix, (c0, n, t0) in enumerate(TBS):
                k = bix % 2
                nt_ = n // 128
                proj_fm(bB, 0, 128, c0, n, 0)
                proj_fm(bB, 128, 64, c0, n, 1)
                proj_fm(bB, 192, 128, c0, n, 2)
                proj_fm(bB, 320, 32, c0, n, 3)
                CP(CQ[k][:, 0, 0:n], ps[0][:, 0:n], [R_ps[0]], [R_CQ[k]], eng="act")
                CP(CQ[k][0:64, 1, 0:n], ps[1][0:64, 0:n], [R_ps[1]], [R_CQ[k]], eng="act")
                CP(CKV[k][:, 0:n], ps[2][:, 0:n], [R_ps[2]], [R_CKV[k]], eng="act")
                ACT(SQ[k][:, 0, 0:n], ps[0][:, 0:n], AF.Square, [R_ps[0]], [R_SQ[k]])
                ACT(SQ[k][0:64, 1, 0:n], ps[1][0:64, 0:n], AF.Square, [R_ps[1]], [R_SQ[k]])
                ACT(SQ[k][:, 2, 0:n], ps[2][:, 0:n], AF.Square, [R_ps[2]], [R_SQ[k]])
                MM(ps[4][:, 0:n], ONB[:, :], SQ[k][:, 0, 0:n], True, False, [R_IDB, R_SQ[k]], [R_ps[4]])
                MM(ps[4][:, 0:n], ONB[0:64, :], SQ[k][0:64, 1, 0:n], False, True, [R_IDB, R_SQ[k]], [R_ps[4]])
                MM(ps[5][:, 0:n], ONB[:, :], SQ[k][:, 2, 0:n], True, True, [R_IDB, R_SQ[k]], [R_ps[5]])
                ACT(RQ[k][:, 0, 0:n], ps[4][:, 0:n], AF.Sqrt, [R_ps[4], R_CF], [R_RQ[k]], bias=EPS_T[NORM_EPS], scale=1.0 / 192)
                ACT(RQ[k][:, 1, 0:n], ps[5][:, 0:n], AF.Sqrt, [R_ps[5], R_CF], [R_RQ[k]], bias=EPS_T[NORM_EPS], scale=1.0 / 128)
                RECIP(RQ[k][:, :, 0:n], RQ[k][:, :, 0:n], [R_RQ[k]], [R_RQ[k]])
                for tt in range(nt_):
                    MM(ps[6][:, tt:tt + 1], SQ[k][:, 2, tt * 128:(tt + 1) * 128], ONB[:, 0:1], True, True,
                       [R_IDB, R_SQ[k]], [R_ps[6]])
                ACT(RKT[k][:, 0:nt_], ps[6][:, 0:nt_], AF.Sqrt, [R_ps[6], R_CF], [R_RKT[k]], bias=EPS_T[NORM_EPS], scale=1.0 / 128)
                RECIP(RKT[k][:, 0:nt_], RKT[k][:, 0:nt_], [R_RKT[k]], [R_RKT[k]])
                for i in range(2):
                    dma("sp", RPB[k][0:96, i, 0:n], ropeB[i, :, c0:c0 + n], w=[R_RPB[k]])
                    dma("sp", RPK[k][0:32, i, 0:n], ropeB[i, 64:96, c0:c0 + n], w=[R_RPK[k]])
                for i in range(2):
                    TT(RPB[k][0:96, i, 0:n], RPB[k][0:96, i, 0:n], RQ[k][0:96, 0, 0:n], ALU.mult, [R_RQ[k]], [R_RPB[k]])
                rope_block(32, n, 3, 7, 256, RPK[k][0:32, 0, 0:n], RPK[k][0:32, 1, 0:n], R_RPK[k],
                           KR[k][0:32, 0:n], R_KR[k], k)
                for hh in range(4):
                    r0 = 512 + hh * 96 + 64
                    rb = (hh % 2) * 96 + 64
                    if t0 < NLAT:
                        dma("sp", snd1B[hh // 2][rb:rb + 32, t0:t0 + n], KR[k][0:32, 0:n], r=[R_KR[k]], w=[R_snd1])
                    else:
                        dma("sp", kctx_d[r0:r0 + 32, :], KR[k][0:32, 0:n], r=[R_KR[k]], w=[R_kctx])
                for hh in range(4):
                    pb_ = hh % 2
                    MM(ps[pb_][0:96, 0:n], WQ[:, 0, hh * 96:(hh + 1) * 96], CQ[k][:, 0, 0:n], True, False,
                       [R_WQ, R_CQ[k]], [R_ps[pb_]])
                    MM(ps[pb_][0:96, 0:n], WQ[0:64, 1, hh * 96:(hh + 1) * 96], CQ[k][0:64, 1, 0:n], False, True,
                       [R_WQ, R_CQ[k]], [R_ps[pb_]])
                    rope_block(96, n, pb_, 2 + pb_, 128, RPB[k][0:96, 0, 0:n], RPB[k][0:96, 1, 0:n], R_RPB[k],
                               QBO[pb_][0:96, 0:n], R_QBO[pb_], pb_)
                    dma("sp", qB_d[hh * 96:(hh + 1) * 96, t0:t0 + n], QBO[pb_][0:96, 0:n], r=[R_QBO[pb_]], w=[R_qB])
                    MM(ps[4 + pb_][0:64, 0:n], WKV[:, hh * 64:(hh + 1) * 64], CKV[k][:, 0:n], True, True,
                       [R_WKV, R_CKV[k]], [R_ps[4 + pb_]])
                    TT(KN[pb_][0:64, 0:n], ps[4 + pb_][0:64, 0:n], RQ[k][0:64, 1, 0:n], ALU.mult,
                       [R_ps[4 + pb_], R_RQ[k]], [R_KN[pb_]])
                    r0 = 512 + hh * 96
                    rb = (hh % 2) * 96
                    if t0 < NLAT:
                        dma("sp", snd1B[hh // 2][rb:rb + 64, t0:t0 + n], KN[pb_][0:64, 0:n], r=[R_KN[pb_]], w=[R_snd1])
                    else:
                        dma("sp", kctx_d[r0:r0 + 64, :], KN[pb_][0:64, 0:n], r=[R_KN[pb_]], w=[R_kctx])
                for tt in range(nt_):
                    vb = tt % 2
                    cc0 = c0 + tt * 128
                    pv = 4 + vb
                    for kc in range(8):
                        MM(ps[pv][:, :], HT[:, kc, cc0:cc0 + 128], WB[bV][:, kc, 0:512], kc == 0, kc == 7,
                           [R_HT, R_WB[bV]], [R_ps[pv]])
                    CP(VT[vb][:, 0:512], ps[pv][:, :], [R_ps[pv]], [R_VT[vb]], eng="act")
                    MM(ps[7][:, 0:256], CKV[k][:, tt * 128:(tt + 1) * 128], WKV[:, 256:512], True, True,
                       [R_CKV[k], R_WKV], [R_ps[7]])
                    TS(VT[vb][:, 512:768], ps[7][:, 0:256], RKT[k][:, tt:tt + 1], ALU.mult, [R_ps[7], R_RKT[k]], [R_VT[vb]])
                    tk = t0 + tt * 128
                    if tk < NLAT:
                        dma("sp", snd2[tk // 512][tk % 512:tk % 512 + 128, :], VT[vb][:, :], r=[R_VT[vb]], w=[R_snd2])
                    else:
                        dma("sp", vctx_d[tk - NLAT:tk - NLAT + 128, :], VT[vb][:, :], r=[R_VT[vb]], w=[R_vctx])
            STAGE(5)
            for i in range(2):
                CC(snd1A[i], rcv1A[i], R_snd1, R_rcv1)
                CC(snd1B[i], rcv1B[i], R_snd1, R_rcv1)
            for i in range(4):
                CC(snd2[i], rcv2[i], R_snd2, R_rcv2)
            if do_rwkv:
                PCO = [sb(st, "PCO%d" % i, [128, 512]) for i in range(2)]
                R_PCO = [fw.res(), fw.res()]
                TBC = [(i * 512, 512) for i in range(4)] + [(HALO, 1), (CTX0, 256)]
                n_ = 0
                for (w0_, w1_) in ((1888, 2400), (2400, 2912), (2912, 3040)):
                    bi = load_w(w0_, w1_)
                    for cc in range((w1_ - w0_) // 128):
                        crow = (w0_ - 1888) + cc * 128
                        for (c0, n) in TBC:
                            k = n_ % 2
                            n_ += 1
                            proj_fm(bi, cc * 128, 128, c0, n, k, extra=[R_HTh] if n == 1 else [])
                            CP(PCO[k][:, 0:n], ps[k][:, 0:n], [R_ps[k]], [R_PCO[k]], eng="act")
                            dma("sp", pcT_d[crow:crow + 128, c0:c0 + n], PCO[k][:, 0:n], r=[R_PCO[k]], w=[R_pcT],
                                **({"allow_slow_non_contiguous": True} if n == 1 else {}))
            fw.muted = False
            fw.barrier()
            fw.flush()

        if do_attn:
            with ExitStack() as st:
                KT = [sb(st, "KT%d" % i, [128, NKEY], BF16) for i in range(2)]
                V1 = [sb(st, "V1_%d" % i, [128, NKT, 129], BF16) for i in range(2)]
                V1B = [sb(st, "V1B%d" % i, [128, NKT, 65], BF16) for i in range(2)]
                QT = [sb(st, "QT%d" % i, [128, NTOK], BF16) for i in range(2)]
                PT = [sb(st, "PT%d" % i, [128, 512], BF16) for i in range(4)]
                ZB = sb(st, "ZB", [128, 512], BF16)
                OAH = [sb(st, "OAH%d" % i, [128, NT, 128], BF16) for i in range(2)]
                O1 = [sb(st, "O1_%d" % i, [128, 128]) for i in range(4)]
                OCP = sb(st, "OCP", [128, 3, 512])
                R_OCP = fw.res()
                JK = sb(st, "JK", [128, 128])
                RCP = [sb(st, "RCP%d" % i, [128, 4]) for i in range(4)]
                LQ = sb(st, "LQ", [128, 256])
                LS = sb(st, "LS", [128, 4])
                GSUB = sb(st, "GSUB", [128, 128])
                psS = [pst(st, "psS%d" % i, [128, 512]) for i in range(4)]
                psO = [pst(st, "psO%d" % i, [128, 512]) for i in range(3)]
                R_KT, R_V1, R_V1B, R_QT, R_OAH = [[fw.res(), fw.res()] for _ in range(5)]
                R_O1, R_RCP = [[fw.res() for _ in range(4)] for _ in range(2)]
                R_PT = [fw.res() for _ in range(4)]
                R_psS = [fw.pres() for _ in range(4)]
                R_psO = [fw.pres() for _ in range(3)]
                R_ZB, R_JK, R_LQ, R_LS, R_GSUB = [fw.res() for _ in range(5)]
                MSET(ZB[:, :], 0.0, [R_ZB])
                for i in range(2):
                    MSET(V1[i][:, :, 128:129], 1.0, [R_V1[i]])
                    MSET(V1B[i][:, :, 64:65], 1.0, [R_V1B[i]])
                dma("sp", LQ[:, :], row_b(l, "lam"), w=[R_LQ])
                dma("sp", GSUB[:, :], row_b(l, "subln"), w=[R_GSUB])
                TT(LQ[:, 0:64], LQ[:, 0:64], LQ[:, 64:128], ALU.mult, [], [R_LQ])
                TT(LQ[:, 128:192], LQ[:, 128:192], LQ[:, 192:256], ALU.mult, [], [R_LQ])
                ACT(JK[:, 0:64], LQ[:, 0:64], AF.Identity, [R_LQ], [R_JK, R_LS], accum_out=LS[:, 0:1])
                ACT(JK[:, 0:64], LQ[:, 128:192], AF.Identity, [R_LQ], [R_JK, R_LS], accum_out=LS[:, 1:2])
                ACT(LS[:, 0:2], LS[:, 0:2], AF.Exp, [], [R_LS])
                TS(LS[:, 2:3], LS[:, 0:1], LS[:, 1:2], ALU.subtract, [], [R_LS], s2=lam_init, op1=ALU.add)
                TS(LS[:, 3:4], LS[:, 2:3], -1.0, ALU.mult, [], [R_LS])
                TS(GSUB[:, :], GSUB[:, :], 1.0 - lam_init, ALU.mult, [], [R_GSUB])
                if "lam" in dbg_out and l == dbgl:
                    dma("sp", dbg_out["lam"], LS[:, :], r=[R_LS])
                vctx_v = vctx_d.rearrange("(t p) c -> p t c", p=128)
                rcv2_v = [r_.rearrange("(t p) c -> p t c", p=128) for r_ in rcv2]
                qgroups = [(i * 512, 512, 0, NKT) for i in range(4)]
                if ctx_out:
                    qgroups.append((NLAT, 256, 0, 2))
                hsets = [("A", h) for h in range(4)] + [("B", h) for h in range(4)]
                for hi, (kind, h) in enumerate(hsets):
                    b = hi % 2
                    if kind == "A":
                        nm, dk, e, scale = 2, 64, 128, 64 ** -0.5
                        kr0, qsrc, Vb, R_Vb, vc0 = h * 128, qA_d[h * 128:(h + 1) * 128, :], V1[b], R_V1[b], h * 128
                        dkt = 128
                    else:
                        nm, dk, e, scale = 1, 96, 64, 96 ** -0.5
                        kr0, qsrc, Vb, R_Vb, vc0 = 512 + h * 96, qB_d[h * 96:(h + 1) * 96, :], V1B[b], R_V1B[b], 512 + h * 64
                        dkt = 96
                    dma("sp", KT[b][0:dkt, 0:NCTX], kctx_d[kr0:kr0 + dkt, :], r=[R_kctx], w=[R_KT[b]])
                    rsrc = rcv1A[h // 2] if kind == "A" else rcv1B[h // 2]
                    nrw = 256 if kind == "A" else 192
                    ro = (h % 2) * dkt
                    dma("sp", KT[b][0:dkt, NCTX:NCTX + NLAT], rsrc[ro:ro + dkt, :], r=[R_rcv1], w=[R_KT[b]])
                    dma("sp", KT[b][0:dkt, NCTX + NLAT:NKEY], rsrc[nrw + ro:nrw + ro + dkt, :], r=[R_rcv1], w=[R_KT[b]])
                    dma("sp", Vb[:, 0:2, 0:e], vctx_v[:, :, vc0:vc0 + e], r=[R_vctx], w=[R_Vb])
                    for rk_ in range(2):
                        for q4 in range(4):
                            kt_ = 2 + rk_ * 16 + q4 * 4
                            dma("sp", Vb[:, kt_:kt_ + 4, 0:e], rcv2_v[q4][:, rk_ * 4:(rk_ + 1) * 4, vc0:vc0 + e],
                                r=[R_rcv2], w=[R_Vb])
                    dma("sp", QT[b][0:dkt, :], qsrc, r=[R_qA, R_qB], w=[R_QT[b]])
                    wacc = e + 1
                    per_bank = 512 // wacc if kind == "A" else 4

                    def acc(qb, m):
                        i = qb * nm + m
                        return i // per_bank, (i % per_bank) * wacc

                    for (q0, nq, kt0, kt1) in qgroups:
                        nqb = nq // 128
                        nbanks = (nqb * nm + per_bank - 1) // per_bank
                        for bk in range(nbanks):
                            MM(psO[bk][:, :], ZB[:, 0:128], ZB[:, :], True, False, [R_ZB], [R_psO[bk]])
                        sidx = [0]

                        def emit_S(kt):
                            lst = []
                            for m in range(nm):
                                si = sidx[0] % 4
                                sidx[0] += 1
                                MM(psS[si][:, 0:nq], KT[b][64 * m:64 * m + dk, kt * 128:(kt + 1) * 128],
                                   QT[b][64 * m:64 * m + dk, q0:q0 + nq], True, True, [R_KT[b], R_QT[b]], [R_psS[si]])
                                lst.append(si)
                            return lst

                        cur = emit_S(kt0)
                        for kt in range(kt0, kt1):
                            nxt = emit_S(kt + 1) if kt + 1 < kt1 else None
                            for m in range(nm):
                                si = cur[m]
                                ACT(PT[si][:, 0:nq], psS[si][:, 0:nq], AF.Exp, [R_psS[si]], [R_PT[si]], scale=scale)
                            for m in range(nm):
                                si = cur[m]
                                for qb in range(nqb):
                                    bk, off = acc(qb, m)
                                    MM(psO[bk][:, off:off + wacc], PT[si][:, qb * 128:(qb + 1) * 128], Vb[:, kt, :],
                                       False, kt == kt1 - 1, [R_PT[si], R_Vb], [R_psO[bk]])
                            cur = nxt
                        for bk in range(nbanks):
                            CP(OCP[:, bk, :], psO[bk][:, :], [R_psO[bk]], [R_OCP], eng="act" if bk % 2 else "dve")
                        tl = [(q0 + qb * 128) // 128 for qb in range(nqb)]
                        if kind == "A":
                            A0 = [acc(qb, 0) for qb in range(nqb)]
                            A1 = [acc(qb, 1) for qb in range(nqb)]
                            for qb in range(nqb):
                                b0, o0 = A0[qb]
                                RECIP(RCP[qb][:, 0:1], OCP[:, b0, o0 + 128:o0 + 129], [R_OCP], [R_RCP[qb]])
                            for qb in range(nqb):
                                b1, o1 = A1[qb]
                                RECIP(RCP[qb][:, 1:2], OCP[:, b1, o1 + 128:o1 + 129], [R_OCP], [R_RCP[qb]])
                            for qb in range(nqb):
                                TT(RCP[qb][:, 1:2], RCP[qb][:, 1:2], LS[:, 3:4], ALU.mult, [R_LS], [R_RCP[qb]])
                            for qb in range(nqb):
                                b0, o0 = A0[qb]
                                TS(O1[qb][:, :], OCP[:, b0, o0:o0 + 128], RCP[qb][:, 0:1], ALU.mult, [R_OCP, R_RCP[qb]], [R_O1[qb]])
                            for qb in range(nqb):
                                b1, o1 = A1[qb]
                                STT(O1[qb][:, :], OCP[:, b1, o1:o1 + 128], RCP[qb][:, 1:2], O1[qb][:, :], ALU.mult, ALU.add,
                                    [R_OCP, R_RCP[qb]], [R_O1[qb]])
                            for qb in range(nqb):
                                ACT(JK[:, :], O1[qb][:, :], AF.Square, [R_O1[qb]], [R_JK, R_RCP[qb]], accum_out=RCP[qb][:, 2:3])
                            for qb in range(nqb):
                                ACT(RCP[qb][:, 2:3], RCP[qb][:, 2:3], AF.Sqrt, [R_CF], [R_RCP[qb]], bias=EPS_T[SUBLN_EPS], scale=1.0 / 128)
                            for qb in range(nqb):
                                RECIP(RCP[qb][:, 2:3], RCP[qb][:, 2:3], [], [R_RCP[qb]])
                            for qb in range(nqb):
                                STT(OAH[b][:, tl[qb], :], O1[qb][:, :], RCP[qb][:, 2:3], GSUB[:, :], ALU.mult, ALU.mult,
                                    [R_O1[qb], R_RCP[qb], R_GSUB], [R_OAH[b]])
                        else:
                            for qb in range(nqb):
                                b0, o0 = acc(qb, 0)
                                RECIP(RCP[qb][:, 0:1], OCP[:, b0, o0 + 64:o0 + 65], [R_OCP], [R_RCP[qb]])
                            for qb in range(nqb):
                                b0, o0 = acc(qb, 0)
                                TS(OAH[b][:, tl[qb], 0:64], OCP[:, b0, o0:o0 + 64], RCP[qb][:, 0:1], ALU.mult,
                                   [R_OCP, R_RCP[qb]], [R_OAH[b]])
                    nto = len(tiles_out)
                    dst = oall_d.rearrange("(t p) c -> p t c", p=128)
                    if kind == "A":
                        dma("sp", dst[:, 0:nto, h * 128:(h + 1) * 128], OAH[b][:, 0:nto, :], r=[R_OAH[b]], w=[R_oall])
                    else:
                        dma("sp", dst[:, 0:nto, 512 + h * 64:512 + (h + 1) * 64], OAH[b][:, 0:nto, 0:64], r=[R_OAH[b]], w=[R_oall])
                fw.barrier()
                fw.flush()

        if do_rwkv:
            RWKV_PHASE(l, ctx_out, tiles_out)

        with ExitStack() as st:
            if not do_merge:
                break
            OT = sb(st, "OT", [128, 8, NTOK], BF16)
            WO = sb(st, "WO", [128, 8, D], BF16)
            OTI = [sb(st, "OTI%d" % i, [128, D], BF16) for i in range(2)]
            G1 = [sb(st, "G1_%d" % i, [128, D]) for i in range(2)]
            TM = [sb(st, "TM%d" % i, [128, D]) for i in range(2)]
            JM = sb(st, "JM", [128, 512])
            SS = [sb(st, "SSm%d" % i, [128, 2]) for i in range(2)]
            psT = [pst(st, "psTb%d" % i, [128, 1024], BF16) for i in range(2)]
            psM = [pst(st, "psM%d" % i, [128, 512]) for i in range(4)]
            R_OT, R_WO, R_JM = fw.res(), fw.res(), fw.res()
            R_OTI, R_G1, R_TM, R_SS = [[fw.res(), fw.res()] for _ in range(4)]
            R_psT = [fw.pres(), fw.pres()]
            R_psM = [fw.pres() for _ in range(4)]
            wov = w_out[l].rearrange("(kc p) n -> p kc n", p=128)
            for kc in range(8):
                dma("pool", WO[:, kc, :], wov[:, kc, :], w=[R_WO])
            for s in range(2):
                gr = (l * 2 + s) * 2 + 0
                dma("sp", G1[s][:, :], grow[gr, :].partition_broadcast(128), r=[R_grow], w=[R_G1[s]])
            for n_, t in enumerate(tiles_out):
                b = n_ % 2
                dma("sp", OTI[b][:, :], oall_d[t * 128:(t + 1) * 128, :], r=[R_oall], w=[R_OTI[b]])
                for fc in range(8):
                    TR(psT[b][:, fc * 128:(fc + 1) * 128], OTI[b][:, fc * 128:(fc + 1) * 128], IDB[:, :],
                       [R_OTI[b], R_IDB], [R_psT[b]])
                if "oall" in dbg_out and l == dbgl:
                    CP(TM[b][:, :], OTI[b][:, :], [R_OTI[b]], [R_TM[b]])
                    dma("sp", dbg_out["oall"][t * 128:(t + 1) * 128, :], TM[b][:, :], r=[R_TM[b]])
                CP(OT[:, :, t * 128:(t + 1) * 128], psT[b][:, :].rearrange("p (a b) -> p a b", a=8), [R_psT[b]], [R_OT],
                   eng="act" if n_ % 2 else "dve")
            for n_, t in enumerate(tiles_out):
                b = n_ % 2
                s = 0 if t < 16 else 1
                for hf in range(2):
                    pm = b * 2 + hf
                    for kc in range(8):
                        MM(psM[pm][:, :], OT[:, kc, t * 128:(t + 1) * 128], WO[:, kc, hf * 512:(hf + 1) * 512], kc == 0, kc == 7,
                           [R_OT, R_WO], [R_psM[pm]])
                    ACT(JM[:, :], psM[pm][:, :], AF.Square, [R_psM[pm]], [R_JM, R_SS[b]], accum_out=SS[b][:, hf:hf + 1])
                    TT(TM[b][:, hf * 512:(hf + 1) * 512], psM[pm][:, :], G1[s][:, hf * 512:(hf + 1) * 512], ALU.mult,
                       [R_psM[pm], R_G1[s]], [R_TM[b]])
                TT(SS[b][:, 0:1], SS[b][:, 0:1], SS[b][:, 1:2], ALU.add, [], [R_SS[b]])
                rstd_from_ss(SS[b][:, 0:1], R_SS[b], D, NORM_EPS)
                STT(X[:, t, :], TM[b][:, :], SS[b][:, 0:1], X[:, t, :], ALU.mult, ALU.add, [R_TM[b], R_SS[b]], [R_X[t]])
            fw.barrier()
            fw.flush()
        if "xm" in dbg_out and l == dbgl:
            for t in range(NT):
                dma("sp", dbg_out["xm"][t * 128:(t + 1) * 128, :], X[:, t, :], r=[R_X[t]])

        if do_ffn:
            with ExitStack() as st:
                H2T = sb(st, "H2T", [128, 8, NCOL], BF16)
                R_H2T = fw.res()
                psF = [pst(st, "psF%d" % i, [128, 512]) for i in range(8)]
                R_psF = [fw.pres() for _ in range(8)]
                with ExitStack() as st2:
                    XS = [sb(st2, "XSf%d" % i, [128, D]) for i in range(2)]
                    R_XS = [fw.res(), fw.res()]
                    RSTD = sb(st2, "RSTDf", [128, NT])
                    R_RSTD = fw.res()
                    norm_transpose(l, 2, tiles_out, H2T, R_H2T, psF[0:4], R_psF[0:4], XS, R_XS, RSTD, R_RSTD, XS[1], R_XS[1])
                    fw.barrier()
                    fw.flush()
                R_H2Th = fw.res()
                halo_exchange(st, H2T, R_H2T, R_H2Th, snd5, rcv5, R_snd5, R_rcv5, l, "f")
                ACTT = sb(st, "ACTTf", [128, NCH, 512], BF16)
                WU = [sb(st, "WU%d" % i, [128, 8, 256], BF16) for i in range(3)]
                WD = [sb(st, "WD%d" % i, [128, D], BF16) for i in range(4)]
                US = [sb(st, "US%d" % i, [128, 2, 514]) for i in range(2)]
                CV = [sb(st, "CV%d" % i, [128, 2, 512]) for i in range(2)]
                SGf = [sb(st, "SGf%d" % i, [128, 512]) for i in range(2)]
                G2 = [sb(st, "G2_%d" % i, [128, D]) for i in range(2)]
                TM = [sb(st, "TMf%d" % i, [128, D]) for i in range(2)]
                JM = sb(st, "JMf", [128, 512])
                SS = [sb(st, "SSf%d" % i, [128, 2]) for i in range(2)]
                R_ACTT, R_JM = fw.res(), fw.res()
                R_WU = [fw.res() for _ in range(3)]
                R_WD = [fw.res() for _ in range(4)]
                R_US, R_CV, R_SGf, R_G2, R_TM, R_SS = [[fw.res(), fw.res()] for _ in range(6)]
                for s in range(2):
                    gr = (l * 2 + s) * 2 + 1
                    dma("sp", G2[s][:, :], grow[gr, :].partition_broadcast(128), r=[R_grow], w=[R_G2[s]])
                wupv = wup_bf.rearrange("(kc p) n -> p kc n", p=128)
                blocks = [(i * 512, 512, 0, NLAT + 1, i * 4) for i in range(4)]
                if ctx_out:
                    blocks.append((CTX0, 256, CTX0, CTX0 + 256, 16))
                nwu = 0
                nwd = 0
                for (c0, n, seq0, seq1, tile0) in blocks:
                    lo = max(c0 - 1, seq0)
                    hi = min(c0 + n + 1, seq1)
                    off0 = lo - (c0 - 1)
                    len1 = min(512, hi - lo)
                    len2 = (hi - lo) - len1
                    for cc in range(NCH):
                        wb = nwu % 3
                        ub = nwu % 2
                        nwu += 1
                        dma("sp", WU[wb][:, :, 0:128], wupv[:, :, cc * 128:(cc + 1) * 128], r=[R_wup], w=[R_WU[wb]])
                        dma("sp", WU[wb][:, :, 128:256], wupv[:, :, (NCH + cc) * 128:(NCH + cc + 1) * 128], r=[R_wup], w=[R_WU[wb]])
                        for gv in range(2):
                            pa = (ub * 2 + gv) * 2
                            pbk = pa + 1
                            hx = [R_H2Th] if (lo <= HALO < hi) else []
                            for kc in range(8):
                                MM(psF[pa][:, 0:len1], WU[wb][:, kc, gv * 128:(gv + 1) * 128], H2T[:, kc, lo:lo + len1],
                                   kc == 0, kc == 7, [R_WU[wb], R_H2T] + hx, [R_psF[pa]])
                            CP(US[ub][:, gv, off0:off0 + len1], psF[pa][:, 0:len1], [R_psF[pa]], [R_US[ub]], eng="act")
                            if len2 > 0:
                                for kc in range(8):
                                    MM(psF[pbk][:, 0:len2], WU[wb][:, kc, gv * 128:(gv + 1) * 128],
                                       H2T[:, kc, lo + len1:lo + len1 + len2], kc == 0, kc == 7, [R_WU[wb], R_H2T] + hx, [R_psF[pbk]])
                                CP(US[ub][:, gv, off0 + len1:off0 + len1 + len2], psF[pbk][:, 0:len2], [R_psF[pbk]], [R_US[ub]],
                                   eng="act")
                            if off0 > 0:
                                MSET(US[ub][:, gv, 0:1], 0.0, [R_US[ub]])
                            if off0 + len1 + len2 < n + 2:
                                MSET(US[ub][:, gv, n + 1:n + 2], 0.0, [R_US[ub]])
                            ci = gv * NCH + cc
                            eng = "dve" if gv == 0 else "pool"
                            TS(CV[ub][:, gv, 0:n], US[ub][:, gv, 1:n + 1], pc(l, "cw1", ci, ci + 1), ALU.mult, [R_US[ub], R_PCS],
                               [R_CV[ub]], s2=pc(l, "cb", ci, ci + 1), op1=ALU.add, eng=eng)
                            STT(CV[ub][:, gv, 0:n], US[ub][:, gv, 0:n], pc(l, "cw0", ci, ci + 1), CV[ub][:, gv, 0:n], ALU.mult, ALU.add,
                                [R_US[ub], R_PCS], [R_CV[ub]], eng=eng)
                            STT(CV[ub][:, gv, 0:n], US[ub][:, gv, 2:n + 2], pc(l, "cw2", ci, ci + 1), CV[ub][:, gv, 0:n], ALU.mult, ALU.add,
                                [R_US[ub], R_PCS], [R_CV[ub]], eng=eng)
                        ACT(SGf[ub][:, 0:n], CV[ub][:, 0, 0:n], AF.Silu, [R_CV[ub]], [R_SGf[ub]])
                        TT(ACTT[:, cc, 0:n], SGf[ub][:, 0:n], CV[ub][:, 1, 0:n], ALU.mult, [R_SGf[ub], R_CV[ub]], [R_ACTT])
                    ntl = n // 128
                    for cc in range(NCH):
                        wd = nwd % 4
                        nwd += 1
                        dma("sp", WD[wd][:, :], wdn_bf[cc * 128:(cc + 1) * 128, :], r=[R_wdn], w=[R_WD[wd]])
                        for tt in range(ntl):
                            for hf in range(2):
                                pm = tt * 2 + hf
                                MM(psF[pm][:, :], ACTT[:, cc, tt * 128:(tt + 1) * 128], WD[wd][:, hf * 512:(hf + 1) * 512],
                                   cc == 0, cc == NCH - 1, [R_ACTT, R_WD[wd]], [R_psF[pm]])
                    for tt in range(ntl):
                        t = tile0 + tt
                        b = tt % 2
                        s = 0 if t < 16 else 1
                        for hf in range(2):
                            pm = tt * 2 + hf
                            ACT(JM[:, :], psF[pm][:, :], AF.Square, [R_psF[pm]], [R_JM, R_SS[b]], accum_out=SS[b][:, hf:hf + 1])
                            TT(TM[b][:, hf * 512:(hf + 1) * 512], psF[pm][:, :], G2[s][:, hf * 512:(hf + 1) * 512], ALU.mult,
                               [R_psF[pm], R_G2[s]], [R_TM[b]])
                        TT(SS[b][:, 0:1], SS[b][:, 0:1], SS[b][:, 1:2], ALU.add, [], [R_SS[b]])
                        rstd_from_ss(SS[b][:, 0:1], R_SS[b], D, NORM_EPS)
                        STT(X[:, t, :], TM[b][:, :], SS[b][:, 0:1], X[:, t, :], ALU.mult, ALU.add, [R_TM[b], R_SS[b]], [R_X[t]])
                fw.barrier()
                fw.flush()
        if "xo" in dbg_out and l == dbgl:
            for t in range(NT):
                dma("sp", dbg_out["xo"][t * 128:(t + 1) * 128, :], X[:, t, :], r=[R_X[t]])

    yv = yout.rearrange("(t p) f -> p t f", p=128)
    for t in range(16):
        dma("sp", yv[:, t, :], X[:, t, :], r=[R_X[t]])
    fw.barrier()
    fw.flush()
    glob.close()
    return nc, fw


def _consts():
    cf = np.zeros((128, NCF), np.float32)
    o = CF_OFF
    cf[:, o["ident"][0]:o["ident"][0] + 128] = np.eye(128, dtype=np.float32)
    cf[:, o["ones"][0]:o["ones"][0] + 128] = 1.0
    blk = np.zeros((128, 128), np.float32)
    blk[0:64, 0:64] = 1.0
    blk[64:128, 64:128] = 1.0
    cf[:, o["blk"][0]:o["blk"][0] + 128] = blk
    RA = np.zeros((128, 128), np.float32)
    for m in range(128):
        if (m % 32) < 16:
            RA[m + 16, m] = -1.0
        else:
            RA[m - 16, m] = 1.0
    cf[:, o["RA"][0]:o["RA"][0] + 128] = RA
    RBm = np.zeros((128, 128), np.float32)
    for m in range(64, 96):
        e = m - 64
        if (e % 16) < 8:
            RBm[m + 8, m] = -1.0
        else:
            RBm[m - 8, m] = 1.0
    cf[:, o["RB"][0]:o["RB"][0] + 128] = RBm
    RK = np.zeros((128, 32), np.float32)
    for m in range(32):
        if (m % 16) < 8:
            RK[m + 8, m] = -1.0
        else:
            RK[m - 8, m] = 1.0
    cf[:, o["RK"][0]:o["RK"][0] + 32] = RK
    cf[0:64, o["hsel"][0]] = 1.0
    cf[64:128, o["hsel"][0] + 1] = 1.0
    cr = np.zeros((128, NCR), np.float32)
    j = np.arange(128)[:, None]
    i = np.arange(128)[None, :]
    same = (j // 64) == (i // 64)
    r = CR_OFF
    cr[:, r["Us"][0]:r["Us"][0] + 128] = (same & (i > j))
    cr[:, r["Ui"][0]:r["Ui"][0] + 128] = (same & (i >= j))
    cr[:, r["Ls"][0]:r["Ls"][0] + 128] = (same & (i < j))
    cr[:, r["Li"][0]:r["Li"][0] + 128] = (same & (i <= j))
    tf = np.concatenate([(same & (j <= i)), (same & (j < i)), (same & (j > i))], axis=1).astype(np.float32) * DECAY_C
    tb = np.concatenate([(same & (j >= i)), (same & (j > i)), (same & (j < i))], axis=1).astype(np.float32) * DECAY_C
    cr[:, r["trif"][0]:r["trif"][0] + 384] = tf
    cr[:, r["trib"][0]:r["trib"][0] + 384] = tb
    return cf, cr


def _rope_tables(s):
    li = np.arange(NLAT)
    tt = li if s == 0 else (4095 - li)
    rows = (tt // 64).astype(np.float64)
    cols = (tt % 64).astype(np.float64)
    ropeA = np.zeros((2, 128, NCOL), np.float32)
    ropeA[0] = 1.0
    invA = 10000.0 ** (-np.arange(16, dtype=np.float32) / 16)
    for p in range(128):
        dd = p % 64
        a, f = dd // 32, dd % 16
        pos = rows if a == 0 else cols
        ang = (pos.astype(np.float32) * invA[f]).astype(np.float32)
        ropeA[0, p, 0:NLAT] = np.cos(ang)
        ropeA[1, p, 0:NLAT] = np.sin(ang)
    ropeB = np.zeros((2, 96, NCOL), np.float32)
    ropeB[0] = 1.0
    invB = 10000.0 ** (-np.arange(8, dtype=np.float32) / 8)
    for e in range(32):
        a, f = e // 16, e % 8
        pos = rows if a == 0 else cols
        ang = (pos.astype(np.float32) * invB[f]).astype(np.float32)
        ropeB[0, 64 + e, 0:NLAT] = np.cos(ang)
        ropeB[1, 64 + e, 0:NLAT] = np.sin(ang)
    return ropeA, ropeB


def _pcol(v, n):
    return np.ascontiguousarray(np.asarray(v, np.float32).reshape(n, 128).T)


def _pack_core(inp, c):
    b, s = c // 2, c % 2
    f32 = np.float32
    xl = inp["x"][b, s * NLAT:(s + 1) * NLAT]
    cx = inp["ctx"][b]
    if s == 1:
        xl = xl[::-1]
        cx = cx[::-1]
    m = {}
    m["xin"] = np.ascontiguousarray(np.concatenate([xl, cx], 0), dtype=f32)
    m["cvec"] = np.ascontiguousarray(np.concatenate([_pcol(inp["c"][b], 8), _pcol(inp["c_ctx"], 8)], 1))
    m["ada_w"] = np.ascontiguousarray(inp["ada_w"], dtype=f32)
    w_in = np.array(inp["w_in"], dtype=f32)
    mup = np.array(inp["c_mu_prev"], dtype=f32)
    mun = np.array(inp["c_mu_next"], dtype=f32)
    dirs = (0, 1) if s == 0 else (1, 0)
    if s == 1:
        perm = np.arange(1152)
        perm[768:832], perm[832:896] = np.arange(832, 896), np.arange(768, 832)
        perm[896:960], perm[960:1024] = np.arange(960, 1024), np.arange(896, 960)
        w_in[:, :, 1888:3040] = w_in[:, :, 1888 + perm]
        mup, mun = mun[:, perm], mup[:, perm]
    m["w_in"] = np.ascontiguousarray(w_in)
    m["w_out"] = np.ascontiguousarray(inp["w_out"], dtype=f32)
    m["ffn_up"] = np.ascontiguousarray(inp["ffn_w_up"], dtype=f32)
    m["ffn_dn"] = np.ascontiguousarray(inp["ffn_w_down"], dtype=f32)
    m["wq_up"] = np.ascontiguousarray(inp["b_w_q_up"], dtype=f32)
    wkv = np.asarray(inp["b_w_kv_up"], f32).reshape(L, 128, 4, 2, 64)
    m["wkv_up"] = np.ascontiguousarray(np.concatenate([wkv[:, :, :, 0, :].reshape(L, 128, 256),
                                                       wkv[:, :, :, 1, :].reshape(L, 128, 256)], axis=2))
    m["cw2"] = np.ascontiguousarray(np.concatenate([inp["c_w2"][:, dirs[0]], inp["c_w2"][:, dirs[1]]], axis=1), dtype=f32)
    m["ca2"] = np.ascontiguousarray(np.concatenate([inp["c_a2"][:, dirs[0]], inp["c_a2"][:, dirs[1]]], axis=1), dtype=f32)
    m["cg2"] = np.ascontiguousarray(inp["c_g2"], dtype=f32)
    pcs = np.zeros((L, 128, NPC), f32)
    rows = np.zeros((L, NROW), f32)
    cw = np.array(inp["ffn_conv_w"], dtype=f32)
    if s == 1:
        cw = cw[:, ::-1, :]
    for l in range(L):
        def put(name, arr):
            o, w = PC_OFF[name]
            assert arr.shape == (128, w), (name, arr.shape)
            pcs[l, :, o:o + w] = arr
        ab = inp["ada_b"][l]
        put("adab", np.concatenate([_pcol(ab[0:1024], 8), _pcol(ab[1024:2048], 8), _pcol(ab[3072:4096], 8),
                                    _pcol(ab[4096:5120], 8)], 1))
        put("preg1", _pcol(inp["mix_pre_g"][l], 8))
        put("preg2", _pcol(inp["ffn_pre_g"][l], 8))
        qn = np.zeros((128, 2), f32)
        qn[:, 0] = inp["b_q_norm_g"][l][0:128]
        qn[0:64, 1] = inp["b_q_norm_g"][l][128:192]
        put("qng", qn)
        put("kvng", np.asarray(inp["b_kv_norm_g"][l], f32).reshape(128, 1))
        put("mup", _pcol(mup[l], 9))
        put("mun", _pcol(mun[l], 9))
        put("a0", np.concatenate([_pcol(inp["c_a0"][l][dirs[0]], 2), _pcol(inp["c_a0"][l][dirs[1]], 2)], 1))
        put("kk", _pcol(inp["c_k_k"][l], 2))
        put("ka", _pcol(inp["c_k_a"][l], 2))
        put("rk", _pcol(np.asarray(inp["c_r_k"][l]).reshape(256), 2))
        put("cw0", _pcol(cw[l, 0], 44))
        put("cw1", _pcol(cw[l, 1], 44))
        put("cw2", _pcol(cw[l, 2], 44))
        put("cb", _pcol(inp["ffn_conv_b"][l], 44))
        sel = np.zeros((128, 2), f32)
        sel[:, 1 - s] = 1.0
        put("sel", sel)

        def putr(name, arr):
            o, w = ROW_OFF[name]
            arr = np.asarray(arr, f32).reshape(-1)
            assert arr.shape[0] == w, (name, arr.shape)
            rows[l, o:o + w] = arr
        putr("adab_g1", ab[2048:3072])
        putr("adab_g2", ab[5120:6144])
        putr("postg1", inp["mix_post_g"][l])
        putr("postg2", inp["ffn_post_g"][l])
        putr("lam", np.concatenate([inp["lam_q1"][l], inp["lam_k1"][l], inp["lam_q2"][l], inp["lam_k2"][l]]))
        putr("subln", inp["a_subln_g"][l])
        putr("w0", np.concatenate([inp["c_w0"][l][dirs[0]], inp["c_w0"][l][dirs[1]]]))
        putr("gng", inp["c_gn_g"][l])
        putr("gnb", inp["c_gn_b"][l])
    m["pc"] = pcs
    m["row"] = rows
    cf_, cr_ = _consts()
    m["cf"] = cf_
    m["cr"] = cr_
    ra, rb = _rope_tables(s)
    m["ropeA"] = ra
    m["ropeB"] = rb
    return m


_CACHE = {}


def kernel(**inputs):
    inp = {k: np.asarray(v) for k, v in inputs.items()}
    if "nc" not in _CACHE:
        _CACHE["nc"] = build_program()[0]
    nc = _CACHE["nc"]
    in_maps = [_pack_core(inp, c) for c in range(8)]
    res = run_bass_kernel_spmd(nc, in_maps, core_ids=list(range(8)))
    out = np.zeros((4, 4096, D), np.float32)
    for c in range(8):
        b, s = c // 2, c % 2
        y = np.asarray(res.results[c]["yout"], dtype=np.float32)
        if s == 1:
            y = y[::-1]
        out[b, s * NLAT:(s + 1) * NLAT] = y
    return out
```

```python
import math
from contextlib import ExitStack
import numpy as np
import concourse.bass as bass
import concourse.mybir as mybir
from concourse.bass_utils import run_bass_kernel_spmd

F32 = mybir.dt.float32
BF16 = mybir.dt.bfloat16
AF = mybir.ActivationFunctionType
ALU = mybir.AluOpType

D = 1024
L = 2
NLAT = 2048
NCTX = 256
NTOK = NLAT + NCTX
NT = NTOK // 128
NCOL = NTOK + 1
HALO = NLAT
CTX0 = NLAT + 1
NKEY = 4096 + NCTX
NKT = NKEY // 128
N_IN = 3040
DFF = 2816
NCH = DFF // 128
GROUPS = [[0, 1], [2, 3], [4, 5], [6, 7]]
NORM_EPS = 1e-6
SUBLN_EPS = 1e-5
GN_EPS = 64e-5
DECAY_C = -math.exp(-0.5)
RB = 256


def col(t):
    return t * 128 if t < 16 else CTX0 + (t - 16) * 128


def _mk(lst):
    d, o = {}, 0
    for n, w in lst:
        d[n] = (o, w)
        o += w
    return d, o


PC_OFF, NPC = _mk([("adab", 32), ("preg1", 8), ("preg2", 8), ("qng", 2), ("kvng", 1), ("mup", 9), ("mun", 9),
                   ("a0", 4), ("kk", 2), ("ka", 2), ("rk", 2), ("cw0", 44), ("cw1", 44), ("cw2", 44), ("cb", 44),
                   ("sel", 2)])
ROW_OFF, NROW = _mk([("adab_g1", 1024), ("adab_g2", 1024), ("postg1", 1024), ("postg2", 1024), ("lam", 256),
                     ("subln", 128), ("w0", 512), ("gng", 256), ("gnb", 256)])
CF_OFF, NCF = _mk([("ident", 128), ("ones", 128), ("blk", 128), ("RA", 128), ("RB", 128), ("RK", 32), ("hsel", 2)])
CR_OFF, NCR = _mk([("Us", 128), ("Ui", 128), ("Ls", 128), ("Li", 128), ("trif", 384), ("trib", 384)])


class Res:
    __slots__ = ("name", "w", "rs", "excl")

    def __init__(self, name, excl=False):
        self.name = name
        self.w = None
        self.rs = {}
        self.excl = excl


class Eng:
    def __init__(self, key, sem):
        self.key = key
        self.sem = sem
        self.count = 0
        self.items = []
        self.waited = {}


class FW:
    NDS = 6

    def __init__(self, nc):
        self.nc = nc
        self.engs = {k: Eng(k, nc.alloc_semaphore("es_" + k)) for k in ("pe", "act", "dve", "pool", "sp")}
        self.dsems = {q: [[nc.alloc_semaphore("ds_%s%d" % (q, i)), 0] for i in range(self.NDS)] for q in ("sp", "pool")}
        self.drr = {"sp": 0, "pool": 0}
        self.cctoks = []
        self.nres = 0
        self.same_engine_sync = True
        self.ninst = 0
        self.muted = False
        self.skip_waw = False

    def res(self, name=None):
        self.nres += 1
        return Res(name or ("r%d" % self.nres))

    def pres(self):
        self.nres += 1
        return Res("p%d" % self.nres, excl=True)

    def _wait(self, e, tok):
        sem, val = tok
        if e.waited.get(sem, 0) >= val:
            return
        e.waited[sem] = val
        e.items.append(("w", sem, val))

    def _deps(self, r, w, own=None):
        deps = []
        for x in r:
            if x.w is not None:
                deps.append(x.w)
            if x.excl:
                for s, v in x.rs.items():
                    if s is not own:
                        deps.append((s, v))
        for x in w:
            if x.w is not None and not (self.skip_waw and x.w[0] is own and x not in r):
                deps.append(x.w)
            for s, v in x.rs.items():
                deps.append((s, v))
        return deps

    def _update(self, tok, r, w):
        for x in w:
            x.w = tok
            x.rs = {}
        for x in r:
            if x in w:
                continue
            if x.rs.get(tok[0], 0) < tok[1]:
                x.rs[tok[0]] = tok[1]

    def op(self, ek, fn, r=(), w=()):
        if self.muted:
            return None
        e = self.engs[ek]
        for tok in self._deps(r, w, e.sem):
            if tok[0] is e.sem and (ek == "pe" or not self.same_engine_sync):
                continue
            self._wait(e, tok)
        e.count += 1
        self.ninst += 1
        tok = (e.sem, e.count)
        e.items.append(("o", fn))
        self._update(tok, r, w)
        return tok

    def dma(self, q, out, in_, r=(), w=(), **kw):
        if self.muted:
            return None
        e = self.engs[q]
        for tok in self._deps(r, w):
            self._wait(e, tok)
        slot = self.dsems[q][self.drr[q]]
        self.drr[q] = (self.drr[q] + 1) % self.NDS
        sem, val = slot
        if val > 0:
            self._wait(e, (sem, val))
        slot[1] = val + 16
        tok = (sem, val + 16)
        self.ninst += 1
        e.items.append(("d", out, in_, sem, kw))
        self._update(tok, r, w)
        return tok

    def collective(self, in_ap, out_ap, r=(), w=()):
        if self.muted:
            return None
        e = self.engs["pool"]
        for tok in self._deps(r, w):
            self._wait(e, tok)
        sem = self.nc.alloc_semaphore("cc%d" % len(self.cctoks))
        tok = (sem, 1)
        e.items.append(("c", in_ap, out_ap, sem))
        self.cctoks.append(tok)
        self._update(tok, r, w)
        return tok

    def barrier(self):
        toks = [(e.sem, e.count) for e in self.engs.values() if e.count > 0]
        for q in self.dsems:
            toks += [(s, v) for s, v in self.dsems[q] if v > 0]
        toks += self.cctoks
        for e in self.engs.values():
            for tok in toks:
                self._wait(e, tok)

    def flush(self):
        nc = self.nc
        hmap = {"pe": "tensor", "act": "scalar", "dve": "vector", "pool": "gpsimd", "sp": "sync"}
        with nc.Block() as block:
            for k, e in self.engs.items():
                items = e.items
                e.items = []

                def f(h, items=items, e=e):
                    for it in items:
                        if it[0] == "w":
                            h.wait_ge(it[1], it[2])
                        elif it[0] == "o":
                            it[1](h).then_inc(e.sem, 1)
                        elif it[0] == "d":
                            h.dma_start(out=it[1], in_=it[2], **it[4]).then_inc(it[3], 16)
                        else:
                            h.collective_compute("AllGather", ALU.bypass, replica_groups=GROUPS,
                                                 ins=[it[1]], outs=[it[2]]).then_inc(it[3], 1)
                getattr(block, hmap[k])(f)


def build_program(nlayers=L, dbg=None, do_rwkv=True, do_attn=True, do_ffn=True, do_cc=True, do_p=True, do_merge=True):
    dbg = dbg or {}
    dbgl = dbg.get("_layer", 0)
    pstop = dbg.get("_pstop", 0)

    def STAGE(k):
        if pstop == k:
            fw.muted = True

    nc = bass.Bass("TRN2", target_bir_lowering=False)
    fw = FW(nc)
    op, dma = fw.op, fw.dma

    def MM(out, lhsT, rhs, start, stop, r, w):
        op("pe", lambda e: e.matmul(out, lhsT=lhsT, rhs=rhs, start=start, stop=stop), r, w)

    def TR(out, in_, ident, r, w):
        op("pe", lambda e: e.transpose(out=out, in_=in_, identity=ident), r, w)

    def ACT(out, in_, func, r, w, bias=None, scale=None, accum_out=None, eng="act"):
        kw = {}
        if bias is not None:
            kw["bias"] = bias
        if scale is not None:
            kw["scale"] = scale
        if accum_out is not None:
            kw["accum_out"] = accum_out
        op(eng, lambda e: e.activation(out=out, in_=in_, func=func, **kw), r, w)

    def TT(out, in0, in1, alu, r, w, eng="dve"):
        op(eng, lambda e: e.tensor_tensor(out=out, in0=in0, in1=in1, op=alu), r, w)

    def TS(out, in0, s1, op0, r, w, s2=None, op1=None, eng="dve"):
        if op1 is None:
            op(eng, lambda e: e.tensor_scalar(out=out, in0=in0, scalar1=s1, scalar2=None, op0=op0), r, w)
        else:
            op(eng, lambda e: e.tensor_scalar(out=out, in0=in0, scalar1=s1, scalar2=s2, op0=op0, op1=op1), r, w)

    def STT(out, in0, scalar, in1, op0, op1, r, w, eng="dve"):
        eng = "dve"
        op(eng, lambda e: e.scalar_tensor_tensor(out=out, in0=in0, scalar=scalar, in1=in1, op0=op0, op1=op1), r, w)

    def CP(out, in_, r, w, eng="dve"):
        if eng == "act":
            op("act", lambda e: e.copy(out=out, in_=in_), r, w)
        else:
            op(eng, lambda e: e.tensor_copy(out=out, in_=in_), r, w)

    def RECIP(out, in_, r, w):
        op("dve", lambda e: e.reciprocal(out=out, in_=in_), r, w)

    def MSET(ap, v, w, eng="pool"):
        op(eng, lambda e: e.memset(ap, v), (), w)

    def CC(snd, rcv, R_s, R_r):
        if do_cc:
            for q_ in fw.dsems:
                for s_, v_ in fw.dsems[q_]:
                    if v_ > 0:
                        fw._wait(fw.engs["pool"], (s_, v_))
            fw.collective(snd.opt(), rcv.opt(), r=[R_s], w=[R_r])
        else:
            nr = snd.shape[0]
            dma("sp", rcv[0:nr, :], snd[:, :], r=[R_s], w=[R_r])
            dma("sp", rcv[nr:2 * nr, :], snd[:, :], r=[R_s], w=[R_r])

    def din(name, shape, dt=F32):
        return nc.dram_tensor(name, list(shape), dt, kind="ExternalInput").ap()

    def dscr(name, shape, dt=F32):
        return nc.dram_tensor(name, list(shape), dt).ap()

    xin = din("xin", [NTOK, D])
    cvec = din("cvec", [128, 16])
    ada_w = din("ada_w", [L, D, 6 * D])
    w_in = din("w_in", [L, D, N_IN])
    w_out = din("w_out", [L, D, D])
    ffn_up = din("ffn_up", [L, D, 2 * DFF])
    ffn_dn = din("ffn_dn", [L, DFF, D])
    wq_up = din("wq_up", [L, 192, 384])
    wkv_up = din("wkv_up", [L, 128, 512])
    cw2 = din("cw2", [L, 128, 256])
    ca2 = din("ca2", [L, 128, 256])
    cg2 = din("cg2", [L, 128, 256])
    pcd = din("pc", [L, 128, NPC])
    rowd = din("row", [L, NROW])
    cfd = din("cf", [128, NCF])
    crd = din("cr", [128, NCR])
    ropeA = din("ropeA", [2, 128, NCOL])
    ropeB = din("ropeB", [2, 96, NCOL])
    yout = nc.dram_tensor("yout", [NLAT, D], F32, kind="ExternalOutput").ap()
    dbg_out = {}
    for k, shp in dbg.items():
        if k.startswith("_"):
            continue
        dbg_out[k] = nc.dram_tensor("dbg_" + k, list(shp), F32, kind="ExternalOutput").ap()

    grow = dscr("grow", [L * 4, D])
    qA_d = dscr("qA_d", [4 * 128, NTOK], BF16)
    qB_d = dscr("qB_d", [4 * 96, NTOK], BF16)
    kctx_d = dscr("kctx_d", [896, NCTX], BF16)
    snd1A = [dscr("snd1A%d" % i, [256, NLAT], BF16) for i in range(2)]
    rcv1A = [dscr("rcv1A%d" % i, [512, NLAT], BF16) for i in range(2)]
    snd1B = [dscr("snd1B%d" % i, [192, NLAT], BF16) for i in range(2)]
    rcv1B = [dscr("rcv1B%d" % i, [384, NLAT], BF16) for i in range(2)]
    snd2 = [dscr("snd2_%d" % i, [512, 768], BF16) for i in range(4)]
    rcv2 = [dscr("rcv2_%d" % i, [1024, 768], BF16) for i in range(4)]
    vctx_d = dscr("vctx_d", [NCTX, 768], BF16)
    pcT_d = dscr("pcT_d", [1152, NCOL])
    snd3 = dscr("snd3", [128, 128])
    rcv3 = dscr("rcv3", [256, 128])
    snd4 = dscr("snd4", [128, 8])
    rcv4 = dscr("rcv4", [256, 8])
    snd5 = dscr("snd5", [128, 8])
    rcv5 = dscr("rcv5", [256, 8])
    wup_bf = dscr("wup_bf", [D, 2 * DFF], BF16)
    wdn_bf = dscr("wdn_bf", [DFF, D], BF16)
    R_grow, R_qA, R_qB, R_kctx, R_snd1, R_rcv1 = [fw.res() for _ in range(6)]
    R_snd2, R_rcv2, R_vctx, R_pcT = [fw.res() for _ in range(4)]
    R_snd3, R_rcv3, R_snd4, R_rcv4, R_snd5, R_rcv5 = [fw.res() for _ in range(6)]
    R_wup, R_wdn = fw.res(), fw.res()

    glob = ExitStack()

    uniq = [0]

    def sb(st, name, shape, dt=F32):
        uniq[0] += 1
        return st.enter_context(nc.sbuf_tensor("%s_u%d" % (name, uniq[0]), list(shape), dt))

    def pst(st, name, shape, dt=F32):
        uniq[0] += 1
        return st.enter_context(nc.psum_tensor("%s_u%d" % (name, uniq[0]), list(shape), dt))

    X = sb(glob, "X", [128, NT, D])
    CF = sb(glob, "CF", [128, NCF])
    IDB = sb(glob, "IDB", [128, 128], BF16)
    ONB = sb(glob, "ONB", [128, 128], BF16)
    MOD = sb(glob, "MOD", [128, L * 4 * 2 * 8])
    PCS = sb(glob, "PCS", [128, L * NPC])
    EPSB = sb(glob, "EPSB", [128, 4])
    R_X = [fw.res("X%d" % t) for t in range(NT)]
    R_CF, R_IDB, R_MOD, R_PCS = fw.res(), fw.res(), fw.res(), fw.res()

    def cf(name, p0=0, p1=128, c0=0, c1=None):
        o, w = CF_OFF[name]
        c1 = w if c1 is None else c1
        return CF[p0:p1, o + c0:o + c1]

    def pc(l, name, c0=0, c1=None, p0=0, p1=128):
        o, w = PC_OFF[name]
        c1 = w if c1 is None else c1
        return PCS[p0:p1, l * NPC + o + c0:l * NPC + o + c1]

    def modv(l, which, s):
        o = ((l * 4 + which) * 2 + s) * 8
        return MOD[:, o:o + 8]

    def row_b(l, name, c0=0, c1=None, parts=128):
        o, w = ROW_OFF[name]
        c1 = w if c1 is None else c1
        return rowd[l, o + c0:o + c1].partition_broadcast(parts)

    EPS_T = {NORM_EPS: EPSB[:, 0:1], SUBLN_EPS: EPSB[:, 1:2], GN_EPS: EPSB[:, 2:3], 1e-24: EPSB[:, 3:4]}

    dma("sp", CF[:, :], cfd[:, :], w=[R_CF])
    for l in range(L):
        dma("sp", PCS[:, l * NPC:(l + 1) * NPC], pcd[l, :, :], w=[R_PCS])
    xv = xin.rearrange("(t p) f -> p t f", p=128)
    for t in range(NT):
        dma("sp", X[:, t, :], xv[:, t, :], w=[R_X[t]])
    CP(IDB[:, :], cf("ident"), [R_CF], [R_IDB])
    CP(ONB[:, :], cf("ones"), [R_CF], [R_IDB])
    for i, v in enumerate((NORM_EPS, SUBLN_EPS, GN_EPS, 1e-24)):
        MSET(EPSB[:, i:i + 1], v, [R_CF])

    def rstd_from_ss(RS, R_RS, n, eps):
        ACT(RS, RS, AF.Sqrt, [R_RS, R_CF], [R_RS], bias=EPS_T[eps], scale=1.0 / n)
        RECIP(RS, RS, [R_RS], [R_RS])

    with ExitStack() as st:
        ACTF = sb(st, "ACTF", [128, 16])
        ACTT = sb(st, "ACTT", [128, 8, 2], BF16)
        ACTB = sb(st, "ACTB", [128, 2, 8, 128], BF16)
        ADW = [sb(st, "ADW%d" % i, [128, 8, 1024], BF16) for i in range(2)]
        MRAW = sb(st, "MRAW", [128, 4, 2, 8])
        GB = sb(st, "GB", [128, 1024])
        GP = sb(st, "GP", [128, 1024])
        GO = [sb(st, "GO%d" % i, [128, 512]) for i in range(2)]
        psA = [pst(st, "psA%d" % i, [128, 512]) for i in range(4)]
        R_ACTF, R_ACTT, R_ACTB, R_MRAW, R_GB, R_GP = [fw.res() for _ in range(6)]
        R_ADW = [fw.res(), fw.res()]
        R_GO = [fw.res(), fw.res()]
        R_ps = [fw.pres() for _ in range(4)]
        dma("sp", ACTF[:, :], cvec[:, :], w=[R_ACTF])
        ACT(ACTF[:, :], ACTF[:, :], AF.Silu, [R_ACTF], [R_ACTF])
        for s in range(2):
            CP(ACTT[:, :, s], ACTF[:, s * 8:(s + 1) * 8], [R_ACTF], [R_ACTT])
            for kc in range(8):
                TS(ACTB[:, s, kc, :], cf("ones"), ACTF[:, s * 8 + kc:s * 8 + kc + 1], ALU.mult, [R_ACTF, R_CF], [R_ACTB])
        nload = 0
        for l in range(nlayers):
            awv = ada_w[l].rearrange("(kc p) n -> p kc n", p=128)
            for j in range(6):
                bi = nload % 2
                nload += 1
                for kc in range(8):
                    dma("pool", ADW[bi][:, kc, :], awv[:, kc, j * 1024:(j + 1) * 1024], w=[R_ADW[bi]])
                if j in (0, 1, 3, 4):
                    blk = {0: 0, 1: 1, 3: 2, 4: 3}[j]
                    pb = psA[blk % 2]
                    for fc in range(8):
                        for kc in range(8):
                            MM(pb[:, 2 * fc:2 * fc + 2], ADW[bi][:, kc, fc * 128:(fc + 1) * 128], ACTT[:, kc, :],
                               kc == 0, kc == 7, [R_ADW[bi], R_ACTT], [R_ps[blk % 2]])
                    for s in range(2):
                        TT(MRAW[:, blk, s, :], pb[:, s:16:2], pc(l, "adab", blk * 8, blk * 8 + 8), ALU.add,
                           [R_ps[blk % 2], R_PCS], [R_MRAW])
                else:
                    gate = 0 if j == 2 else 1
                    dma("sp", GB[:, :], row_b(l, "adab_g1" if gate == 0 else "adab_g2"), w=[R_GB])
                    dma("sp", GP[:, :], row_b(l, "postg1" if gate == 0 else "postg2"), w=[R_GP])
                    for s in range(2):
                        for hf in range(2):
                            pi = 2 + hf
                            for kc in range(8):
                                MM(psA[pi][:, :], ACTB[:, s, kc, :], ADW[bi][:, kc, hf * 512:(hf + 1) * 512],
                                   kc == 0, kc == 7, [R_ADW[bi], R_ACTB], [R_ps[pi]])
                            TT(GO[hf][:, :], psA[pi][:, :], GB[:, hf * 512:(hf + 1) * 512], ALU.add,
                               [R_ps[pi], R_GB], [R_GO[hf]])
                            TT(GO[hf][:, :], GO[hf][:, :], GP[:, hf * 512:(hf + 1) * 512], ALU.mult, [R_GP], [R_GO[hf]])
                            gr = (l * 2 + s) * 2 + gate
                            dma("sp", grow[gr:gr + 1, hf * 512:(hf + 1) * 512], GO[hf][0:1, :], r=[R_GO[hf]], w=[R_grow])
            for s in range(2):
                for (which, sc_blk, sh_blk, gname) in ((0, 1, 0, "preg1"), (2, 3, 2, "preg2")):
                    STT(modv(l, which, s), MRAW[:, sc_blk, s, :], 1.0, pc(l, gname), ALU.add, ALU.mult,
                        [R_MRAW, R_PCS], [R_MOD])
                    CP(modv(l, which + 1, s), MRAW[:, sh_blk, s, :], [R_MRAW], [R_MOD])
        fw.barrier()
        fw.flush()
    if "mod" in dbg_out:
        dma("sp", dbg_out["mod"], MOD[:, :], r=[R_MOD])

    def norm_transpose(l, which, tiles, HT, R_HT, psb, R_psb, XS, R_XS, RSTD, R_RSTD, JUNK, R_JUNK):
        for t in tiles:
            ACT(JUNK[:, :], X[:, t, :], AF.Square, [R_X[t]], [R_JUNK, R_RSTD], accum_out=RSTD[:, t:t + 1])
        rstd_from_ss(RSTD[:, tiles[0]:tiles[-1] + 1], R_RSTD, D, NORM_EPS)
        for n_, t in enumerate(tiles):
            s = 0 if t < 16 else 1
            xb = n_ % 2
            TS(XS[xb][:, :], X[:, t, :], RSTD[:, t:t + 1], ALU.mult, [R_X[t], R_RSTD], [R_XS[xb]])
            for half in range(2):
                pi = (n_ * 2 + half) % len(psb)
                for j in range(4):
                    fc = half * 4 + j
                    TR(psb[pi][:, j * 128:(j + 1) * 128], XS[xb][:, fc * 128:(fc + 1) * 128], cf("ident"),
                       [R_XS[xb], R_CF], [R_psb[pi]])
                for j in range(4):
                    fc = half * 4 + j
                    ACT(HT[:, fc, col(t):col(t) + 128], psb[pi][:, j * 128:(j + 1) * 128], AF.Identity,
                        [R_psb[pi], R_MOD], [R_HT], scale=modv(l, which, s)[:, fc:fc + 1],
                        bias=modv(l, which + 1, s)[:, fc:fc + 1])

    def halo_exchange(st, HT, R_HT, R_HTh, snd, rcv, R_snd, R_rcv, l, tag):
        HS = sb(st, "HS" + tag, [128, 8])
        HR = sb(st, "HR" + tag, [128, 2, 8])
        R_HS, R_HR = fw.res(), fw.res()
        CP(HS[:, :], HT[:, :, NLAT - 1], [R_HT], [R_HS])
        dma("sp", snd[:, :], HS[:, :], r=[R_HS], w=[R_snd])
        CC(snd, rcv, R_snd, R_rcv)
        dma("sp", HR[:, :, :], rcv.rearrange("(r p) c -> p r c", p=128), r=[R_rcv], w=[R_HR])
        TS(HS[:, :], HR[:, 0, :], pc(l, "sel", 0, 1), ALU.mult, [R_HR, R_PCS], [R_HS])
        STT(HS[:, :], HR[:, 1, :], pc(l, "sel", 1, 2), HS[:, :], ALU.mult, ALU.add, [R_HR, R_PCS], [R_HS])
        CP(HT[:, :, HALO], HS[:, :], [R_HS], [R_HTh])

    oall_d = dscr("oall_d", [NTOK, D], BF16)
    R_oall = fw.res()
    ys_d = dscr("ys_d", [NTOK, 256])
    R_ysd_t = [fw.res() for _ in range(NT)]
    AXX = mybir.AxisListType.X

    def RWKV_PHASE(l, ctx_out, tiles_out):
        with ExitStack() as st:
            CR = sb(st, "CR", [128, NCR])
            R_CR = fw.res()
            dma("sp", CR[:, :], crd[:, :], w=[R_CR])

            def cr(name, c0=0, c1=None):
                o, w = CR_OFF[name]
                c1 = w if c1 is None else c1
                return CR[:, o + c0:o + c1]

            BS = sb(st, "BS", [128, NT, 4])
            R_BS = fw.res()
            W2 = sb(st, "W2", [128, 256])
            A2 = sb(st, "A2", [128, 256])
            G2w = sb(st, "G2w", [128, 256], BF16)
            W0R = sb(st, "W0R", [128, 512])
            GNG = sb(st, "GNG", [128, 256])
            GNB = sb(st, "GNB", [128, 256])
            MUC = sb(st, "MUC", [128, 9])
            OMKA = sb(st, "OMKA", [128, 2])
            DI2 = sb(st, "DI2", [128, 64])
            R_W = fw.res()
            dma("sp", W2[:, :], cw2[l, :, :], w=[R_W])
            dma("sp", A2[:, :], ca2[l, :, :], w=[R_W])
            dma("pool", G2w[:, :], cg2[l, :, :], w=[R_W])
            dma("sp", W0R[:, :], row_b(l, "w0"), w=[R_W])
            dma("sp", GNG[:, :], row_b(l, "gng"), w=[R_W])
            dma("sp", GNB[:, :], row_b(l, "gnb"), w=[R_W])
            TT(MUC[:, :], pc(l, "mup"), pc(l, "mun"), ALU.add, [R_PCS], [R_W])
            TS(MUC[:, :], MUC[:, :], -1.0, ALU.mult, [], [R_W], s2=1.0, op1=ALU.add)
            TS(OMKA[:, :], pc(l, "ka"), -1.0, ALU.mult, [R_PCS], [R_W], s2=1.0, op1=ALU.add)
            TT(DI2[:, :], cf("ident", c0=0, c1=64), cf("ident", c0=64, c1=128), ALU.add, [R_CF], [R_W])
            Hb = [sb(st, "Hb%d" % i, [128, 2, 64]) for i in range(2)]
            R_H = [fw.res(), fw.res()]
            HR = sb(st, "HRr", [128, 2, 128])
            R_HR = fw.res()
            PCB = sb(st, "PCB", [128, 9, 130])
            SH = sb(st, "SH", [128, 9, 128])
            SH2 = sb(st, "SH2", [128, 9, 128])
            R_SH2 = fw.res()
            TW = sb(st, "TW", [128, 128])
            AA = sb(st, "AA", [128, 2, 128])
            LW = sb(st, "LW", [128, 256])
            KS = sb(st, "KS", [128, 2, 128])
            SQf = sb(st, "SQf", [128, 2, 128])
            RN = sb(st, "RN", [128, 2, 128])
            KK = sb(st, "KK", [128, 2, 128])
            TMP = sb(st, "TMPr", [128, 2, 128])
            KD = sb(st, "KD", [128, 2, 128])
            BB = sb(st, "BB", [128, 2, 128])
            BT = sb(st, "BT", [128, 2, 128])
            KTt = sb(st, "KTt", [128, 2, 128])
            BH = sb(st, "BH", [128, 2, 128])
            KH = sb(st, "KH", [128, 2, 128])
            ATm = sb(st, "ATm", [128, 2, 2, 128])
            BTm = sb(st, "BTm", [128, 2, 2, 128])
            KTm = sb(st, "KTm", [128, 2, 2, 128])
            SGs = [sb(st, "SG%d" % i, [128, 128], BF16) for i in range(2)]
            EEs = [sb(st, "EE%d" % i, [128, 2, 4, 128]) for i in range(2)]
            ARs = [sb(st, "AR%d" % i, [128, 2, 2, 128]) for i in range(2)]
            TOKs = [sb(st, "TOK%d" % i, [128, 2, 4, 128]) for i in range(2)]
            NMSs = [sb(st, "NMS%d" % i, [128, 4, 3, 128]) for i in range(2)]
            X0s = [sb(st, "X0_%d" % i, [128, 4, 128]) for i in range(2)]
            XT0s = [sb(st, "XT0_%d" % i, [128, 4, 128]) for i in range(2)]
            Z0s = [sb(st, "Z0_%d" % i, [128, 4, 128]) for i in range(2)]
            YLs = [sb(st, "YL%d" % i, [128, 256]) for i in range(2)]
            R_SGs, R_EEs, R_ARs, R_TOKs, R_NMSs, R_X0s, R_XT0s, R_Z0s, R_YLs = [[fw.res(), fw.res()] for _ in range(9)]
            TOKm = sb(st, "TOKm", [128, 2, 2, 2, 128])
            XB = [sb(st, "XB%d" % i, [128, 4, 128]) for i in range(2)]
            XTB = [sb(st, "XTB%d" % i, [128, 4, 128]) for i in range(2)]
            ZBf = [sb(st, "ZBf%d" % i, [128, 4, 128]) for i in range(2)]
            AN = sb(st, "AN", [128, 4, 128])
            WU = sb(st, "WUr", [128, 4, 128])
            WUm = sb(st, "WUm", [128, 2, 4, 128])
            WRT = sb(st, "WRT", [128, 2, 128])
            WRTm = sb(st, "WRTm", [128, 2, 2, 128])
            YV = sb(st, "YV", [128, 4, 64])
            PTS = sb(st, "PTS", [128, 2, 2, 64])
            PTSm = sb(st, "PTSm", [128, 2, 2, 2, 64])
            QS = sb(st, "QS", [128, 2, 2, 64])
            YSQ = sb(st, "YSQ", [128, 256])
            YN = sb(st, "YN", [128, 256])
            OC = [sb(st, "OC%d" % i, [128, 256], BF16) for i in range(2)]
            STt = sb(st, "STt", [128, 12])
            (R_PCB, R_SH, R_TW, R_AA, R_LW, R_KS, R_SQf, R_RN, R_KK, R_TMP, R_KD, R_BB, R_BT, R_KTt,
             R_BH, R_KH, R_ATm, R_BTm, R_KTm, R_TOKm, R_AN, R_WU, R_WUm, R_WRT, R_WRTm, R_YV, R_PTS,
             R_PTSm, R_QS, R_YSQ, R_YN, R_ST) = [fw.res() for _ in range(32)]
            R_XB, R_XTB, R_ZBf, R_OC = [[fw.res(), fw.res()] for _ in range(4)]
            bk = [pst(st, "psR%d" % i, [128, 512]) for i in range(8)]
            R_bk = [fw.pres() for _ in range(8)]
            b0, b1, bA, bB, bI1, bI2, bI3, bPQ = bk
            R_b0, R_b1, R_bA, R_bB, R_bI1, R_bI2, R_bI3, R_bPQ = R_bk
            bCH, R_bCH = bI3, R_bI3
            pcv = pcT_d.rearrange("(c p) n -> p c n", p=128)
            ident = cf("ident")
            hsel = [cf("hsel", c0=0, c1=1), cf("hsel", c0=1, c1=2)]

            def v3(ap, a):
                return ap.rearrange("p (a b) -> p a b", a=a)

            def MASK(out, in_, hcol, r, w):
                ACT(out, in_, AF.Identity, r + [R_CF], w, scale=hcol)

            def stage_a(d, t, final, pp):
                EE, AR, TOK, NMS, SG = EEs[pp], ARs[pp], TOKs[pp], NMSs[pp], SGs[pp]
                R_EE, R_AR, R_TOK, R_NMS, R_SG = R_EEs[pp], R_ARs[pp], R_TOKs[pp], R_NMSs[pp], R_SGs[pp]
                X0, XT0, Z0 = X0s[pp], XT0s[pp], Z0s[pp]
                R_X0, R_XT0, R_Z0 = R_X0s[pp], R_XT0s[pp], R_Z0s[pp]
                c0 = col(t)
                has_prev = t not in (0, 16)
                has_next = t != 17
                a_ = c0 - 1 if has_prev else c0
                b_ = c0 + 129 if has_next else c0 + 128
                dma("sp", PCB[:, :, a_ - (c0 - 1):b_ - (c0 - 1)], pcv[:, :, a_:b_], r=[R_pcT], w=[R_PCB])
                if not has_prev:
                    MSET(PCB[:, :, 0:1], 0.0, [R_PCB])
                if not has_next:
                    MSET(PCB[:, :, 129:130], 0.0, [R_PCB])
                if d == 1:
                    dma("sp", YLs[pp][:, :], ys_d[t * 128:(t + 1) * 128, :], r=[R_ysd_t[t]], w=[R_YLs[pp]])
                nch = 9 if final else 8
                bc = lambda ap: ap.unsqueeze(2).to_broadcast([128, nch, 128])
                TT(SH[:, 0:nch, :], PCB[:, 0:nch, 1:129], bc(MUC[:, 0:nch]), ALU.mult, [R_PCB, R_W], [R_SH])
                TT(SH2[:, 0:nch, :], PCB[:, 0:nch, 0:128], bc(pc(l, "mup", 0, nch)), ALU.mult, [R_PCB, R_PCS], [R_SH2], eng="pool")
                TT(SH[:, 0:nch, :], SH[:, 0:nch, :], SH2[:, 0:nch, :], ALU.add, [R_SH2], [R_SH])
                TT(SH2[:, 0:nch, :], PCB[:, 0:nch, 2:130], bc(pc(l, "mun", 0, nch)), ALU.mult, [R_PCB, R_PCS], [R_SH2], eng="pool")
                TT(SH[:, 0:nch, :], SH[:, 0:nch, :], SH2[:, 0:nch, :], ALU.add, [R_SH2], [R_SH])
                ACT(TW[:, :], SH[:, 6, :], AF.Tanh, [R_SH], [R_TW])
                if final:
                    ACT(SG[:, :], SH[:, 8, :], AF.Sigmoid, [R_SH], [R_SG])
                r0 = 64 * d
                for hp in range(2):
                    MM(b0[:, hp * 128:(hp + 1) * 128], A2[r0:r0 + 64, hp * 128:(hp + 1) * 128], SH[r0:r0 + 64, 7, :], True, True,
                       [R_W, R_SH], [R_b0])
                MM(b0[:, 256:512], TW[r0:r0 + 64, :], W2[r0:r0 + 64, :], True, True, [R_TW, R_W], [R_b0])
                for hp in range(2):
                    ACT(AA[:, hp, :], b0[:, hp * 128:(hp + 1) * 128], AF.Sigmoid, [R_b0, R_PCS], [R_AA],
                        bias=pc(l, "a0", d * 2 + hp, d * 2 + hp + 1))
                TT(LW[:, :], b0[:, 256:512], W0R[:, d * 256:(d + 1) * 256], ALU.add, [R_b0, R_W], [R_LW])
                ACT(LW[:, :], LW[:, :], AF.Sigmoid, [], [R_LW])
                tri = cr("trif") if d == 0 else cr("trib")
                for hp in range(2):
                    MM(b1[:, 0:384], LW[:, hp * 128:(hp + 1) * 128], tri, True, True, [R_LW, R_CR], [R_b1])
                    ACT(EE[:, hp, 0:2, :], v3(b1[:, 0:256], 2), AF.Exp, [R_b1], [R_EE])
                    ACT(EE[:, hp, 2, :], b1[:, 0:128], AF.Exp, [R_b1], [R_EE], scale=-1.0)
                    ACT(EE[:, hp, 3, :], b1[:, 256:384], AF.Exp, [R_b1], [R_EE])
                for hp in range(2):
                    TS(KS[:, hp, :], SH[:, 2 + hp, :], pc(l, "kk", hp, hp + 1), ALU.mult, [R_SH, R_PCS], [R_KS])
                ACT(SQf[:, :, :], KS[:, :, :], AF.Square, [R_KS], [R_SQf])
                MM(bA[:, 0:256], cf("blk"), SQf[:, :, :], True, True, [R_CF, R_SQf], [R_bA])
                TS(RN[:, :, :], v3(bA[:, 0:256], 2), 1e-24, ALU.max, [R_bA], [R_RN])
                ACT(RN[:, :, :], RN[:, :, :], AF.Sqrt, [], [R_RN])
                RECIP(RN[:, :, :], RN[:, :, :], [], [R_RN])
                TT(KK[:, :, :], KS[:, :, :], RN[:, :, :], ALU.mult, [R_KS, R_RN], [R_KK])
                for hp in range(2):
                    TS(TMP[:, hp, :], AA[:, hp, :], pc(l, "ka", hp, hp + 1), ALU.mult, [R_AA, R_PCS, R_W], [R_TMP],
                       s2=OMKA[:, hp:hp + 1], op1=ALU.add)
                TT(KD[:, :, :], TMP[:, :, :], SH[:, 2:4, :], ALU.mult, [R_TMP, R_SH], [R_KD])
                TT(BB[:, :, :], KK[:, :, :], AA[:, :, :], ALU.mult, [R_KK, R_AA], [R_BB], eng="pool")
                TT(TMP[:, :, :], SH[:, 0:2, :], KD[:, :, :], ALU.mult, [R_SH, R_KD], [R_TMP])
                for hp in range(2):
                    TS(TMP[:, hp, :], TMP[:, hp, :], pc(l, "rk", hp, hp + 1), ALU.mult, [R_PCS], [R_TMP])
                for hp in range(2):
                    MM(bB[:, 256 + hp * 2:256 + hp * 2 + 2], TMP[:, hp, :], cf("hsel"), True, True, [R_TMP, R_CF], [R_bB])
                if d == 0:
                    CP(BS[:, t, :], bB[:, 256:260], [R_bB], [R_BS])
                else:
                    TT(BS[:, t, :], bB[:, 256:260], BS[:, t, :], ALU.add, [R_bB], [R_BS])
                STT(AR[:, :, 0, :], KK[:, :, :], -1.0, EE[:, :, 1, :], ALU.mult, ALU.mult, [R_KK, R_EE], [R_AR])
                TT(AR[:, :, 1, :], SH[:, 0:2, :], EE[:, :, 0, :], ALU.mult, [R_SH, R_EE], [R_AR], eng="pool")
                TT(BT[:, :, :], BB[:, :, :], EE[:, :, 2, :], ALU.mult, [R_BB, R_EE], [R_BT])
                TT(KTt[:, :, :], KD[:, :, :], EE[:, :, 2, :], ALU.mult, [R_KD, R_EE], [R_KTt], eng="pool")
                TT(BH[:, :, :], BB[:, :, :], EE[:, :, 3, :], ALU.mult, [R_BB, R_EE], [R_BH])
                TT(KH[:, :, :], KD[:, :, :], EE[:, :, 3, :], ALU.mult, [R_KD, R_EE], [R_KH], eng="pool")
                hs4 = cf("hsel").unsqueeze(2).to_broadcast([128, 2, 256])

                def bh(ap3):
                    return ap3.rearrange("p a b -> p (a b)").unsqueeze(1).to_broadcast([128, 2, 256])
                TT(ATm[:, :, :, :], AR[:, :, 0, :].unsqueeze(1).to_broadcast([128, 2, 2, 128]),
                   cf("hsel").unsqueeze(2).unsqueeze(3).to_broadcast([128, 2, 2, 128]), ALU.mult, [R_AR, R_CF], [R_ATm])
                TT(BTm[:, :, :, :].rearrange("p h a b -> p h (a b)"), bh(BT[:, :, :]), hs4, ALU.mult, [R_BT, R_CF], [R_BTm], eng="pool")
                TT(KTm[:, :, :, :].rearrange("p h a b -> p h (a b)"), bh(KTt[:, :, :]), hs4, ALU.mult, [R_KTt, R_CF], [R_KTm])
                for hp in range(2):
                    srcs = [AR[:, hp, 0, :], BH[:, hp, :], KH[:, hp, :], SH[:, 4 + hp, :]]
                    for q, s_ in enumerate(srcs):
                        TR(b0[:, q * 128:(q + 1) * 128], s_, ident, [R_AR, R_BH, R_KH, R_SH, R_CF], [R_b0])
                    CP(TOK[:, hp, :, :], v3(b0[:, :], 4), [R_b0], [R_TOK])
                if d == 0:
                    m_si, m_s, m_i, m_x = cr("Us", 0, 256), cr("Us"), cr("Ui"), cr("Ls")
                else:
                    m_si, m_s, m_i, m_x = cr("Ls", 0, 256), cr("Ls"), cr("Li"), cr("Us")
                b4 = lambda m: m.unsqueeze(1).to_broadcast([128, 4, 128])
                for h in range(4):
                    hp, hh = h // 2, h % 2
                    MM(bA[:, h * 128:(h + 1) * 128], BTm[:, hh, hp, :], AR[:, hp, 0, :], True, True, [R_BTm, R_AR], [R_bA])
                for h in range(4):
                    hp, hh = h // 2, h % 2
                    MM(bB[:, h * 128:(h + 1) * 128], BTm[:, hh, hp, :], AR[:, hp, 1, :], True, True, [R_BTm, R_AR], [R_bB])
                TT(XT0[:, :, :], v3(bA[:, :], 4), b4(m_s), ALU.mult, [R_bA, R_CR], [R_XT0])
                TT(NMS[:, :, 0, :], v3(bB[:, :], 4), b4(m_i), ALU.mult, [R_bB, R_CR], [R_NMS])
                TT(Z0[:, :, :], XT0[:, :, :], b4(ident), ALU.add, [R_XT0, R_CF], [R_Z0], eng="pool")
                for h in range(4):
                    hp, hh = h // 2, h % 2
                    MM(bA[:, h * 128:(h + 1) * 128], KTm[:, hh, hp, :], AR[:, hp, 0, :], True, True, [R_KTm, R_AR], [R_bA])
                for h in range(4):
                    hp, hh = h // 2, h % 2
                    MM(bB[:, h * 128:(h + 1) * 128], KTm[:, hh, hp, :], AR[:, hp, 1, :], True, True, [R_KTm, R_AR], [R_bB])
                TT(NMS[:, :, 1, :], v3(bA[:, :], 4), b4(m_s), ALU.mult, [R_bA, R_CR], [R_NMS])
                TT(NMS[:, :, 2, :], v3(bB[:, :], 4), b4(m_i), ALU.mult, [R_bB, R_CR], [R_NMS])
                for h in range(4):
                    hp, hh = h // 2, h % 2
                    MM(b0[:, h * 128:(h + 1) * 128], ATm[:, hh, hp, :], BT[:, hp, :], True, True, [R_ATm, R_BT], [R_b0])
                TT(X0[:, :, :], v3(b0[:, :], 4), b4(m_x), ALU.mult, [R_b0, R_CR], [R_X0])

            def stage_b(d, t, pp):
                EE, AR, TOK, NMS = EEs[pp], ARs[pp], TOKs[pp], NMSs[pp]
                R_EE, R_AR, R_TOK, R_NMS = R_EEs[pp], R_ARs[pp], R_TOKs[pp], R_NMSs[pp]
                for c in range(2):
                    MASK(TOKm[:, c, :, :, :], TOK[:, :, 1:3, :], hsel[c], [R_TOK], [R_TOKm])
                Xc, XTc, Zc = X0s[pp], XT0s[pp], Z0s[pp]
                R_Xc, R_XTc, R_Zc = R_X0s[pp], R_XT0s[pp], R_Z0s[pp]
                for m in range(5):
                    nx = m % 2
                    for h in range(4):
                        MM(bI1[:, h * 128:(h + 1) * 128], XTc[:, h, :], Xc[:, h, :], True, True, [R_XTc, R_Xc], [R_bI1])
                    if m < 4:
                        for h in range(4):
                            MM(bI2[:, h * 128:(h + 1) * 128], Xc[:, h, :], XTc[:, h, :], True, True, [R_XTc, R_Xc], [R_bI2])
                    CP(XB[nx][:, :, :], v3(bI1[:, :], 4), [R_bI1], [R_XB[nx]])
                    if m < 4:
                        CP(XTB[nx][:, :, :], v3(bI2[:, :], 4), [R_bI2], [R_XTB[nx]], eng="act")
                    for h in range(4):
                        MM(bI3[:, h * 128:(h + 1) * 128], XB[nx][:, h, :], Zc[:, h, :], True, True, [R_XB[nx], R_Zc], [R_bI3])
                    TT(ZBf[nx][:, :, :], v3(bI3[:, :], 4), Zc[:, :, :], ALU.add, [R_bI3, R_Zc], [R_ZBf[nx]])
                    Xc, XTc, Zc = XB[nx], XTB[nx], ZBf[nx]
                    R_Xc, R_XTc, R_Zc = R_XB[nx], R_XTB[nx], R_ZBf[nx]
                ZF, R_ZF = Zc, R_Zc
                for h in range(4):
                    hp, hh = h // 2, h % 2
                    MM(bI1[:, h * 64:(h + 1) * 64], NMS[:, h, 1, :], TOK[:, hp, 3, hh * 64:(hh + 1) * 64], True, True,
                       [R_NMS, R_TOK], [R_bI1])
                CP(AN[:, :, 64:128], v3(bI1[:, 0:256], 4), [R_bI1], [R_AN])
                for hp in range(2):
                    CP(AN[:, 2 * hp:2 * hp + 2, 0:64], v3(TOK[:, hp, 0, :], 2), [R_TOK], [R_AN], eng="act")
                for h in range(4):
                    MM(bI2[:, h * 128:(h + 1) * 128], ZF[:, h, :], AN[:, h, :], True, True, [R_ZF, R_AN], [R_bI2])
                CP(WU[:, :, :], v3(bI2[:, :], 4), [R_bI2], [R_WU])
                TT(WUm[:, :, :, :].rearrange("p c a b -> p c (a b)"),
                   WU[:, :, :].rearrange("p a b -> p (a b)").unsqueeze(1).to_broadcast([128, 2, 512]),
                   cf("hsel").unsqueeze(2).to_broadcast([128, 2, 512]), ALU.mult, [R_WU, R_CF], [R_WUm])
                for h in range(4):
                    hp, hh = h // 2, h % 2
                    pb = 64 * hh
                    MM(bI3[pb:pb + 64, hp * 128:(hp + 1) * 128], WU[:, h, 0:64], NMS[:, h, 0, :], True, True,
                       [R_WU, R_NMS], [R_bI3])
                TT(WRT[:, :, :], v3(bI3[:, 0:256], 2), AR[:, :, 1, :], ALU.add, [R_bI3, R_AR], [R_WRT])
                TT(WRTm[:, :, :, :].rearrange("p c a b -> p c (a b)"),
                   WRT[:, :, :].rearrange("p a b -> p (a b)").unsqueeze(1).to_broadcast([128, 2, 256]),
                   cf("hsel").unsqueeze(2).to_broadcast([128, 2, 256]), ALU.mult, [R_WRT, R_CF], [R_WRTm], eng="pool")
                for h in range(4):
                    hp, hh = h // 2, h % 2
                    MM(bI1[:, 256 + h * 64:256 + (h + 1) * 64], NMS[:, h, 0, :], WU[:, h, 64:128], True, False,
                       [R_NMS, R_WU], [R_bI1])
                    MM(bI1[:, 256 + h * 64:256 + (h + 1) * 64], NMS[:, h, 2, :], TOK[:, hp, 3, hh * 64:(hh + 1) * 64], False, True,
                       [R_NMS, R_TOK], [R_bI1])
                CP(YV[:, :, :], v3(bI1[:, 256:512], 4), [R_bI1], [R_YV])
                for c in range(2):
                    for h in range(4):
                        hp, hh = h // 2, h % 2
                        pb = 64 * hh
                        o_ = (c * 2 + hp) * 64
                        MM(bPQ[pb:pb + 64, o_:o_ + 64], WUm[:, c, h, 0:64], TOK[:, hp, 1, hh * 64:(hh + 1) * 64], True, True,
                           [R_WUm, R_TOK], [R_bPQ])
                for c in range(2):
                    colc = c * 64 + (63 if d == 0 else 0)
                    for hp in range(2):
                        o_ = (c * 2 + hp) * 64
                        STT(PTS[:, c, hp, :], DI2[:, :], EE[:, hp, 0, colc:colc + 1], bPQ[:, o_:o_ + 64], ALU.mult, ALU.add,
                            [R_W, R_EE, R_bPQ], [R_PTS])
                TT(PTSm[:, :, :, :, :].rearrange("p h a b c -> p h (a b c)"),
                   PTS[:, :, :, :].rearrange("p a b c -> p (a b c)").unsqueeze(1).to_broadcast([128, 2, 256]),
                   cf("hsel").unsqueeze(2).to_broadcast([128, 2, 256]), ALU.mult, [R_PTS, R_CF], [R_PTSm])
                for c in range(2):
                    for h in range(4):
                        hp, hh = h // 2, h % 2
                        pb = 64 * hh
                        o_ = 256 + (c * 2 + hp) * 64
                        MM(bPQ[pb:pb + 64, o_:o_ + 64], TOKm[:, c, hp, 0, hh * 64:(hh + 1) * 64], WU[:, h, 64:128], True, False,
                           [R_TOKm, R_WU], [R_bPQ])
                        MM(bPQ[pb:pb + 64, o_:o_ + 64], TOKm[:, c, hp, 1, hh * 64:(hh + 1) * 64],
                           TOK[:, hp, 3, hh * 64:(hh + 1) * 64], False, True, [R_TOKm, R_TOK], [R_bPQ])
                CP(QS[:, :, :, :], bPQ[:, 256:512].rearrange("p (a b c) -> p a b c", a=2, b=2), [R_bPQ], [R_QS], eng="act")

            def tile_chain(d, t, cur, pp):
                order = [0, 1] if d == 0 else [1, 0]
                for c in order:
                    cb = 64 * c
                    nx = 1 - cur
                    for h in range(4):
                        hp, hh = h // 2, h % 2
                        MM(bCH[cb:cb + 64, h * 64:(h + 1) * 64], WRTm[:, hh, hp, cb:cb + 64], Hb[cur][:, hp, :], True, True,
                           [R_WRTm, R_H[cur]], [R_bCH])
                    for h in range(4):
                        hp, hh = h // 2, h % 2
                        pb = 64 * hh
                        MM(bCH[pb:pb + 64, 256 + hp * 64:256 + (hp + 1) * 64], PTSm[:, hh, c, hp, :], Hb[cur][:, hp, :], True, True,
                           [R_PTSm, R_H[cur]], [R_bCH])
                    TT(Hb[nx][:, :, :], v3(bCH[:, 256:384], 2), QS[:, c, :, :], ALU.add, [R_bCH, R_QS], [R_H[nx]])
                    cur = nx
                YL, R_YL = YLs[pp], R_YLs[pp]
                if d == 0:
                    TT(v3(YL[:, :], 4), v3(bCH[:, 0:256], 4), YV[:, :, :], ALU.add, [R_bCH, R_YV], [R_YL])
                    dma("sp", ys_d[t * 128:(t + 1) * 128, :], YL[:, :], r=[R_YL], w=[R_ysd_t[t]])
                else:
                    TT(v3(YSQ[:, :], 4), v3(bCH[:, 0:256], 4), YV[:, :, :], ALU.add, [R_bCH, R_YV], [R_YSQ])
                    TT(YL[:, :], YL[:, :], YSQ[:, :], ALU.add, [R_YSQ], [R_YL])
                return cur

            def tile_final(t, n_, pp):
                ob = n_ % 2
                YL, R_YL = YLs[pp], R_YLs[pp]
                TOK, R_TOK, SG, R_SG = TOKs[pp], R_TOKs[pp], SGs[pp], R_SGs[pp]
                op("dve", lambda e: e.tensor_reduce(out=STt[:, 0:4], in_=v3(YL[:, :], 4), axis=AXX, op=ALU.add),
                   [R_YL], [R_ST])
                ACT(YSQ[:, :], YL[:, :], AF.Square, [R_YL], [R_YSQ])
                op("dve", lambda e: e.tensor_reduce(out=STt[:, 4:8], in_=v3(YSQ[:, :], 4), axis=AXX, op=ALU.add),
                   [R_YSQ], [R_ST])
                TS(STt[:, 0:4], STt[:, 0:4], 1.0 / 64, ALU.mult, [], [R_ST])
                TT(STt[:, 8:12], STt[:, 0:4], STt[:, 0:4], ALU.mult, [], [R_ST])
                STT(STt[:, 4:8], STt[:, 4:8], 1.0 / 64, STt[:, 8:12], ALU.mult, ALU.subtract, [], [R_ST])
                ACT(STt[:, 4:8], STt[:, 4:8], AF.Sqrt, [R_CF], [R_ST], bias=EPS_T[GN_EPS], scale=1.0)
                RECIP(STt[:, 4:8], STt[:, 4:8], [], [R_ST])
                b64 = lambda a: a.unsqueeze(2).to_broadcast([128, 4, 64])
                TT(v3(YN[:, :], 4), v3(YL[:, :], 4), b64(STt[:, 0:4]), ALU.subtract, [R_YL, R_ST], [R_YN])
                TT(v3(YN[:, :], 4), v3(YN[:, :], 4), b64(STt[:, 4:8]), ALU.mult, [R_ST], [R_YN])
                TT(YN[:, :], YN[:, :], GNG[:, :], ALU.mult, [R_W], [R_YN])
                TT(YN[:, :], YN[:, :], GNB[:, :], ALU.add, [R_W], [R_YN])
                for h in range(4):
                    hp, hh = h // 2, h % 2
                    STT(YN[:, h * 64:(h + 1) * 64], TOK[:, hp, 3, hh * 64:(hh + 1) * 64], BS[:, t, h:h + 1],
                        YN[:, h * 64:(h + 1) * 64], ALU.mult, ALU.add, [R_TOK, R_BS], [R_YN])
                MM(bI2[:, 0:256], SG[:, :], G2w[:, :], True, True, [R_SG, R_W], [R_bI2])
                TT(OC[ob][:, :], YN[:, :], bI2[:, 0:256], ALU.mult, [R_YN, R_bI2], [R_OC[ob]])
                dma("sp", oall_d[t * 128:(t + 1) * 128, 768:1024], OC[ob][:, :], r=[R_OC[ob]], w=[R_oall])

            def run_seq(d, tiles, cur, final, n0):
                n_ = n0
                if not tiles:
                    return cur, n_
                stage_a(d, tiles[0], final, n_ % 2)
                for i, t in enumerate(tiles):
                    pp = n_ % 2
                    if i + 1 < len(tiles):
                        stage_a(d, tiles[i + 1], final, (n_ + 1) % 2)
                    stage_b(d, t, pp)
                    cur = tile_chain(d, t, cur, pp)
                    if final:
                        tile_final(t, n_, pp)
                    n_ += 1
                return cur, n_

            MSET(Hb[0][:, :, :], 0.0, [R_H[0]])
            cur, n_ = run_seq(0, [16, 17] + list(range(16)), 0, False, 0)
            Hflat = lambda i: Hb[i][:, :, :].rearrange("p a b -> p (a b)")
            dma("sp", snd3[:, :], Hflat(cur), r=[R_H[cur]], w=[R_snd3])
            CC(snd3, rcv3, R_snd3, R_rcv3)
            dma("sp", HR[:, :, :], rcv3.rearrange("(r p) c -> p r c", p=128), r=[R_rcv3], w=[R_HR])
            TS(Hflat(0), HR[:, 0, :], pc(l, "sel", 0, 1), ALU.mult, [R_HR, R_PCS], [R_H[0]])
            STT(Hflat(0), HR[:, 1, :], pc(l, "sel", 1, 2), Hflat(0), ALU.mult, ALU.add, [R_HR, R_PCS], [R_H[0]])
            cur, n_ = run_seq(1, list(range(15, -1, -1)), 0, True, n_)
            if ctx_out:
                MSET(Hb[0][:, :, :], 0.0, [R_H[0]])
                cur, n_ = run_seq(1, [17, 16], 0, True, n_)
            fw.barrier()
            fw.flush()

    TBS = [(i * 512, 512, i * 512) for i in range(4)] + [(CTX0, 256, NLAT)]

    for l in range(nlayers):
        ctx_out = l < L - 1
        lam_init = 0.8 - 0.6 * math.exp(-0.3 * l)
        tiles_all = list(range(NT))
        tiles_out = tiles_all if ctx_out else list(range(16))
        if do_ffn:
            for rblk in range(8):
                for cb_ in range(4):
                    dma("pool", wup_bf[rblk * 128:(rblk + 1) * 128, cb_ * 1408:(cb_ + 1) * 1408],
                        ffn_up[l, rblk * 128:(rblk + 1) * 128, cb_ * 1408:(cb_ + 1) * 1408], w=[R_wup])
            for rblk in range(22):
                dma("pool", wdn_bf[rblk * 128:(rblk + 1) * 128, :], ffn_dn[l, rblk * 128:(rblk + 1) * 128, :], w=[R_wdn])

        with ExitStack() as st:
            if not do_p:
                break
            HT = sb(st, "HT", [128, 8, NCOL], BF16)
            R_HT = fw.res()
            XS = [sb(st, "XS%d" % i, [128, D]) for i in range(2)]
            R_XS = [fw.res(), fw.res()]
            RSTD = sb(st, "RSTD", [128, NT])
            R_RSTD = fw.res()
            WB = [sb(st, "WB%d" % i, [128, 8, 512], BF16) for i in range(2)]
            R_WB = [fw.res(), fw.res()]
            ps = [pst(st, "psP%d" % i, [128, 512]) for i in range(8)]
            R_ps = [fw.pres() for _ in range(8)]
            norm_transpose(l, 0, tiles_all, HT, R_HT, ps[0:4], R_ps[0:4], XS, R_XS, RSTD, R_RSTD, XS[1], R_XS[1])
            STAGE(1)
            R_HTh = fw.res()
            halo_exchange(st, HT, R_HT, R_HTh, snd4, rcv4, R_snd4, R_rcv4, l, "a")
            STAGE(2)
            if "hT" in dbg_out and l == dbgl:
                for fc in range(8):
                    for c0_ in range(0, NCOL, 1024):
                        n_ = min(1024, NCOL - c0_)
                        CP(XS[0][:, 0:n_], HT[:, fc, c0_:c0_ + n_], [R_HT], [R_XS[0]])
                        dma("sp", dbg_out["hT"][fc * 128:(fc + 1) * 128, c0_:c0_ + n_], XS[0][:, 0:n_], r=[R_XS[0]])
            wv = w_in[l].rearrange("(kc p) n -> p kc n", p=128)
            nwl = [0]

            def load_w(c0, c1):
                bi = nwl[0] % 2
                nwl[0] += 1
                for kc in range(8):
                    dma("pool", WB[bi][:, kc, 0:c1 - c0], wv[:, kc, c0:c1], w=[R_WB[bi]])
                return bi

            ROPE = [sb(st, "ROPE%d" % i, [128, 2, 512]) for i in range(2)]
            R_ROPE = [fw.res(), fw.res()]
            QF = [sb(st, "QF%d" % i, [128, 512], BF16) for i in range(2)]
            R_QF = [fw.res(), fw.res()]
            T1 = [sb(st, "T1_%d" % i, [128, 512]) for i in range(2)]
            R_T1 = [fw.res(), fw.res()]
            T2 = [sb(st, "T2_%d" % i, [128, 512]) for i in range(2)]
            R_T2 = [fw.res(), fw.res()]
            QO = [sb(st, "QO%d" % i, [128, NTOK], BF16) for i in range(2)]
            R_QO = [fw.res(), fw.res()]
            RAB = sb(st, "RAB", [128, 288], BF16)
            R_RAB = fw.res()
            CP(RAB[:, 0:128], cf("RA"), [R_CF], [R_RAB])
            CP(RAB[:, 128:256], cf("RB"), [R_CF], [R_RAB])
            CP(RAB[:, 256:288], cf("RK"), [R_CF], [R_RAB])
            cnt = [0]

            def proj_fm(bi, wc0, M, c0, n, pbank, extra=()):
                for kc in range(8):
                    MM(ps[pbank][0:M, 0:n], WB[bi][:, kc, wc0:wc0 + M], HT[:, kc, c0:c0 + n], kc == 0, kc == 7,
                       [R_WB[bi], R_HT] + list(extra), [R_ps[pbank]])

            def rope_block(M, n, pbank, p2, rcol, cos_ap, sin_ap, r_tab, dst_ap, rdst, k):
                CP(QF[k][0:M, 0:n], ps[pbank][0:M, 0:n], [R_ps[pbank]], [R_QF[k]], eng="act")
                STAGE(31)
                MM(ps[p2][0:M, 0:n], RAB[0:M, rcol:rcol + M], QF[k][0:M, 0:n], True, True, [R_RAB, R_QF[k]], [R_ps[p2]])
                STAGE(32)
                import os as _os
                _v = _os.environ.get("TTV", "")
                if _v == "a":
                    TT(T1[k][0:M, 0:n], T2[k][0:M, 0:n], cos_ap, ALU.mult, [R_ps[pbank], r_tab], [R_T1[k]])
                elif _v == "c":
                    TT(T1[k][0:M, 0:n], ps[pbank][0:M, 0:n], T2[k][0:M, 0:n], ALU.mult, [R_ps[pbank], r_tab], [R_T1[k]])
                elif _v == "d":
                    TT(T1[k][0:M, 0:n], ps[pbank][0:M, 0:n], cos_ap, ALU.mult, [R_ps[pbank], r_tab, R_QF[k]], [R_T1[k]])
                else:
                    TT(T1[k][0:M, 0:n], ps[pbank][0:M, 0:n], cos_ap, ALU.mult, [R_ps[pbank], r_tab], [R_T1[k]])
                STAGE(33)
                TT(T2[k][0:M, 0:n], ps[p2][0:M, 0:n], sin_ap, ALU.mult, [R_ps[p2], r_tab], [R_T2[k]])
                STAGE(34)
                TT(dst_ap, T1[k][0:M, 0:n], T2[k][0:M, 0:n], ALU.add, [R_T1[k], R_T2[k]], [rdst])

            for grp in range(2):
                bi = load_w(grp * 512, grp * 512 + 512)
                STAGE(21)
                for hh in range(4):
                    ci = grp * 4 + hh
                    qo = ci % 2
                    for (c0, n, t0) in TBS:
                        k = cnt[0] % 2
                        cnt[0] += 1
                        for i in range(2):
                            dma("sp", ROPE[k][:, i, 0:n], ropeA[i, :, c0:c0 + n], w=[R_ROPE[k]])
                        STAGE(22)
                        proj_fm(bi, hh * 128, 128, c0, n, k)
                        STAGE(23)
                        rope_block(128, n, k, 2 + k, 0, ROPE[k][:, 0, 0:n], ROPE[k][:, 1, 0:n], R_ROPE[k],
                                   QO[qo][:, t0:t0 + n], R_QO[qo], k)
                        STAGE(24)
                    STAGE(25)
                    if grp == 0:
                        dma("sp", qA_d[hh * 128:(hh + 1) * 128, :], QO[qo][:, :], r=[R_QO[qo]], w=[R_qA])
                    else:
                        dma("sp", snd1A[hh // 2][(hh % 2) * 128:(hh % 2 + 1) * 128, :], QO[qo][:, 0:NLAT], r=[R_QO[qo]], w=[R_snd1])
                        dma("sp", kctx_d[hh * 128:(hh + 1) * 128, :], QO[qo][:, NLAT:NTOK], r=[R_QO[qo]], w=[R_kctx])
                    if ("qkA" in dbg_out) and l == dbgl:
                        for (c0, n, t0) in TBS:
                            CP(T1[0][:, 0:n], QO[qo][:, t0:t0 + n], [R_QO[qo]], [R_T1[0]])
                            dma("sp", dbg_out["qkA"][ci * 128:(ci + 1) * 128, t0:t0 + n], T1[0][:, 0:n], r=[R_T1[0]])
            STAGE(3)
            VT = [sb(st, "VT%d" % i, [128, 768], BF16) for i in range(2)]
            R_VT = [fw.res(), fw.res()]
            WQ = sb(st, "WQ", [128, 2, 384], BF16)
            WQF = sb(st, "WQF", [128, 2, 384])
            WKV = sb(st, "WKV", [128, 512], BF16)
            WKVF = sb(st, "WKVF", [128, 512])
            R_WQ, R_WKV, R_WQF, R_WKVF = fw.res(), fw.res(), fw.res(), fw.res()
            MSET(WQF[:, :, :], 0.0, [R_WQF])
            dma("sp", WQF[:, 0, :], wq_up[l, 0:128, :], w=[R_WQF])
            dma("sp", WQF[0:64, 1, :], wq_up[l, 128:192, :], w=[R_WQF])
            dma("sp", WKVF[:, :], wkv_up[l, :, :], w=[R_WKVF])
            for c in range(2):
                TS(WQ[:, c, :], WQF[:, c, :], pc(l, "qng", c, c + 1), ALU.mult, [R_WQF, R_PCS], [R_WQ])
            TS(WKV[:, :], WKVF[:, :], pc(l, "kvng", 0, 1), ALU.mult, [R_WKVF, R_PCS], [R_WKV])
            STAGE(4)
            bV = load_w(1024, 1536)
            bB = load_w(1536, 1888)
            CQ = [sb(st, "CQ0", [128, 2, 512], BF16)] * 2
            SQ = [sb(st, "SQ0", [128, 3, 512], BF16)] * 2
            CKV = [sb(st, "CKV0", [128, 512], BF16)] * 2
            RQ = [sb(st, "RQ0", [128, 2, 512])] * 2
            RPB = [sb(st, "RPB0", [128, 2, 512])] * 2
            RPK = [sb(st, "RPK0", [32, 2, 512])] * 2
            RKT = [sb(st, "RKT%d" % i, [128, 4]) for i in range(2)]
            KR = [sb(st, "KR%d" % i, [32, 512], BF16) for i in range(2)]
            KN = [sb(st, "KN%d" % i, [64, 512], BF16) for i in range(2)]
            QBO = [sb(st, "QBO%d" % i, [128, 512], BF16) for i in range(2)]
            R_CQ, R_SQ, R_CKV, R_RQ, R_RPB, R_RPK = [[fw.res()] * 2 for _ in range(6)]
            R_RKT, R_KR, R_KN, R_QBO = [[fw.res(), fw.res()] for _ in range(4)]
            for bix, (c0, n, t0) in enumerate(TBS):
                k = bix % 2
                nt_ = n // 128
                proj_fm(bB, 0, 128, c0, n, 0)
                proj_fm(bB, 128, 64, c0, n, 1)
                proj_fm(bB, 192, 128, c0, n, 2)
                proj_fm(bB, 320, 32, c0, n, 3)
                CP(CQ[k][:, 0, 0:n], ps[0][:, 0:n], [R_ps[0]], [R_CQ[k]], eng="act")
                CP(CQ[k][0:64, 1, 0:n], ps[1][0:64, 0:n], [R_ps[1]], [R_CQ[k]], eng="act")
                CP(CKV[k][:, 0:n], ps[2][:, 0:n], [R_ps[2]], [R_CKV[k]], eng="act")
                ACT(SQ[k][:, 0, 0:n], ps[0][:, 0:n], AF.Square, [R_ps[0]], [R_SQ[k]])
                ACT(SQ[k][0:64, 1, 0:n], ps[1][0:64, 0:n], AF.Square, [R_ps[1]], [R_SQ[k]])
                ACT(SQ[k][:, 2, 0:n], ps[2][:, 0:n], AF.Square, [R_ps[2]], [R_SQ[k]])
                MM(ps[4][:, 0:n], ONB[:, :], SQ[k][:, 0, 0:n], True, False, [R_IDB, R_SQ[k]], [R_ps[4]])
                MM(ps[4][:, 0:n], ONB[0:64, :], SQ[k][0:64, 1, 0:n], False, True, [R_IDB, R_SQ[k]], [R_ps[4]])
                MM(ps[5][:, 0:n], ONB[:, :], SQ[k][:, 2, 0:n], True, True, [R_IDB, R_SQ[k]], [R_ps[5]])
                ACT(RQ[k][:, 0, 0:n], ps[4][:, 0:n], AF.Sqrt, [R_ps[4], R_CF], [R_RQ[k]], bias=EPS_T[NORM_EPS], scale=1.0 / 192)
                ACT(RQ[k][:, 1, 0:n], ps[5][:, 0:n], AF.Sqrt, [R_ps[5], R_CF], [R_RQ[k]], bias=EPS_T[NORM_EPS], scale=1.0 / 128)
                RECIP(RQ[k][:, :, 0:n], RQ[k][:, :, 0:n], [R_RQ[k]], [R_RQ[k]])
                for tt in range(nt_):
                    MM(ps[6][:, tt:tt + 1], SQ[k][:, 2, tt * 128:(tt + 1) * 128], ONB[:, 0:1], True, True,
                       [R_IDB, R_SQ[k]], [R_ps[6]])
                ACT(RKT[k][:, 0:nt_], ps[6][:, 0:nt_], AF.Sqrt, [R_ps[6], R_CF], [R_RKT[k]], bias=EPS_T[NORM_EPS], scale=1.0 / 128)
                RECIP(RKT[k][:, 0:nt_], RKT[k][:, 0:nt_], [R_RKT[k]], [R_RKT[k]])
                for i in range(2):
                    dma("sp", RPB[k][0:96, i, 0:n], ropeB[i, :, c0:c0 + n], w=[R_RPB[k]])
                    dma("sp", RPK[k][0:32, i, 0:n], ropeB[i, 64:96, c0:c0 + n], w=[R_RPK[k]])
                for i in range(2):
                    TT(RPB[k][0:96, i, 0:n], RPB[k][0:96, i, 0:n], RQ[k][0:96, 0, 0:n], ALU.mult, [R_RQ[k]], [R_RPB[k]])
                rope_block(32, n, 3, 7, 256, RPK[k][0:32, 0, 0:n], RPK[k][0:32, 1, 0:n], R_RPK[k],
                           KR[k][0:32, 0:n], R_KR[k], k)
                for hh in range(4):
                    r0 = 512 + hh * 96 + 64
                    rb = (hh % 2) * 96 + 64
                    if t0 < NLAT:
                        dma("sp", snd1B[hh // 2][rb:rb + 32, t0:t0 + n], KR[k][0:32, 0:n], r=[R_KR[k]], w=[R_snd1])
                    else:
                        dma("sp", kctx_d[r0:r0 + 32, :], KR[k][0:32, 0:n], r=[R_KR[k]], w=[R_kctx])
                for hh in range(4):
                    pb_ = hh % 2
                    MM(ps[pb_][0:96, 0:n], WQ[:, 0, hh * 96:(hh + 1) * 96], CQ[k][:, 0, 0:n], True, False,
                       [R_WQ, R_CQ[k]], [R_ps[pb_]])
                    MM(ps[pb_][0:96, 0:n], WQ[0:64, 1, hh * 96:(hh + 1) * 96], CQ[k][0:64, 1, 0:n], False, True,
                       [R_WQ, R_CQ[k]], [R_ps[pb_]])
                    rope_block(96, n, pb_, 2 + pb_, 128, RPB[k][0:96, 0, 0:n], RPB[k][0:96, 1, 0:n], R_RPB[k],
                               QBO[pb_][0:96, 0:n], R_QBO[pb_], pb_)
                    dma("sp", qB_d[hh * 96:(hh + 1) * 96, t0:t0 + n], QBO[pb_][0:96, 0:n], r=[R_QBO[pb_]], w=[R_qB])
                    MM(ps[4 + pb_][0:64, 0:n], WKV[:, hh * 64:(hh + 1) * 64], CKV[k][:, 0:n], True, True,
                       [R_WKV, R_CKV[k]], [R_ps[4 + pb_]])
                    TT(KN[pb_][0:64, 0:n], ps[4 + pb_][0:64, 0:n], RQ[k][0:64, 1, 0:n], ALU.mult,
                       [R_ps[4 + pb_], R_RQ[k]], [R_KN[pb_]])
                    r0 = 512 + hh * 96
                    rb = (hh % 2) * 96
                    if t0 < NLAT:
                        dma("sp", snd1B[hh // 2][rb:rb + 64, t0:t0 + n], KN[pb_][0:64, 0:n], r=[R_KN[pb_]], w=[R_snd1])
                    else:
                        dma("sp", kctx_d[r0:r0 + 64, :], KN[pb_][0:64, 0:n], r=[R_KN[pb_]], w=[R_kctx])
                for tt in range(nt_):
                    vb = tt % 2
                    cc0 = c0 + tt * 128
                    pv = 4 + vb
                    for kc in range(8):
                        MM(ps[pv][:, :], HT[:, kc, cc0:cc0 + 128], WB[bV][:, kc, 0:512], kc == 0, kc == 7,
                           [R_HT, R_WB[bV]], [R_ps[pv]])
                    CP(VT[vb][:, 0:512], ps[pv][:, :], [R_ps[pv]], [R_VT[vb]], eng="act")
                    MM(ps[7][:, 0:256], CKV[k][:, tt * 128:(tt + 1) * 128], WKV[:, 256:512], True, True,
                       [R_CKV[k], R_WKV], [R_ps[7]])
                    TS(VT[vb][:, 512:768], ps[7][:, 0:256], RKT[k][:, tt:tt + 1], ALU.mult, [R_ps[7], R_RKT[k]], [R_VT[vb]])
                    tk = t0 + tt * 128
                    if tk < NLAT:
                        dma("sp", snd2[tk // 512][tk % 512:tk % 512 + 128, :], VT[vb][:, :], r=[R_VT[vb]], w=[R_snd2])
                    else:
                        dma("sp", vctx_d[tk - NLAT:tk - NLAT + 128, :], VT[vb][:, :], r=[R_VT[vb]], w=[R_vctx])
            STAGE(5)
            for i in range(2):
                CC(snd1A[i], rcv1A[i], R_snd1, R_rcv1)
                CC(snd1B[i], rcv1B[i], R_snd1, R_rcv1)
            for i in range(4):
                CC(snd2[i], rcv2[i], R_snd2, R_rcv2)
            if do_rwkv:
                PCO = [sb(st, "PCO%d" % i, [128, 512]) for i in range(2)]
                R_PCO = [fw.res(), fw.res()]
                TBC = [(i * 512, 512) for i in range(4)] + [(HALO, 1), (CTX0, 256)]
                n_ = 0
                for (w0_, w1_) in ((1888, 2400), (2400, 2912), (2912, 3040)):
                    bi = load_w(w0_, w1_)
                    for cc in range((w1_ - w0_) // 128):
                        crow = (w0_ - 1888) + cc * 128
                        for (c0, n) in TBC:
                            k = n_ % 2
                            n_ += 1
                            proj_fm(bi, cc * 128, 128, c0, n, k, extra=[R_HTh] if n == 1 else [])
                            CP(PCO[k][:, 0:n], ps[k][:, 0:n], [R_ps[k]], [R_PCO[k]], eng="act")
                            dma("sp", pcT_d[crow:crow + 128, c0:c0 + n], PCO[k][:, 0:n], r=[R_PCO[k]], w=[R_pcT],
                                **({"allow_slow_non_contiguous": True} if n == 1 else {}))
            fw.muted = False
            fw.barrier()
            fw.flush()

        if do_attn:
            with ExitStack() as st:
                KT = [sb(st, "KT%d" % i, [128, NKEY], BF16) for i in range(2)]
                V1 = [sb(st, "V1_%d" % i, [128, NKT, 129], BF16) for i in range(2)]
                V1B = [sb(st, "V1B%d" % i, [128, NKT, 65], BF16) for i in range(2)]
                QT = [sb(st, "QT%d" % i, [128, NTOK], BF16) for i in range(2)]
                PT = [sb(st, "PT%d" % i, [128, 512], BF16) for i in range(4)]
                ZB = sb(st, "ZB", [128, 512], BF16)
                OAH = [sb(st, "OAH%d" % i, [128, NT, 128], BF16) for i in range(2)]
                O1 = [sb(st, "O1_%d" % i, [128, 128]) for i in range(4)]
                OCP = sb(st, "OCP", [128, 3, 512])
                R_OCP = fw.res()
                JK = sb(st, "JK", [128, 128])
                RCP = [sb(st, "RCP%d" % i, [128, 4]) for i in range(4)]
                LQ = sb(st, "LQ", [128, 256])
                LS = sb(st, "LS", [128, 4])
                GSUB = sb(st, "GSUB", [128, 128])
                psS = [pst(st, "psS%d" % i, [128, 512]) for i in range(4)]
                psO = [pst(st, "psO%d" % i, [128, 512]) for i in range(3)]
                R_KT, R_V1, R_V1B, R_QT, R_OAH = [[fw.res(), fw.res()] for _ in range(5)]
                R_O1, R_RCP = [[fw.res() for _ in range(4)] for _ in range(2)]
                R_PT = [fw.res() for _ in range(4)]
                R_psS = [fw.pres() for _ in range(4)]
                R_psO = [fw.pres() for _ in range(3)]
                R_ZB, R_JK, R_LQ, R_LS, R_GSUB = [fw.res() for _ in range(5)]
                MSET(ZB[:, :], 0.0, [R_ZB])
                for i in range(2):
                    MSET(V1[i][:, :, 128:129], 1.0, [R_V1[i]])
                    MSET(V1B[i][:, :, 64:65], 1.0, [R_V1B[i]])
                dma("sp", LQ[:, :], row_b(l, "lam"), w=[R_LQ])
                dma("sp", GSUB[:, :], row_b(l, "subln"), w=[R_GSUB])
                TT(LQ[:, 0:64], LQ[:, 0:64], LQ[:, 64:128], ALU.mult, [], [R_LQ])
                TT(LQ[:, 128:192], LQ[:, 128:192], LQ[:, 192:256], ALU.mult, [], [R_LQ])
                ACT(JK[:, 0:64], LQ[:, 0:64], AF.Identity, [R_LQ], [R_JK, R_LS], accum_out=LS[:, 0:1])
                ACT(JK[:, 0:64], LQ[:, 128:192], AF.Identity, [R_LQ], [R_JK, R_LS], accum_out=LS[:, 1:2])
                ACT(LS[:, 0:2], LS[:, 0:2], AF.Exp, [], [R_LS])
                TS(LS[:, 2:3], LS[:, 0:1], LS[:, 1:2], ALU.subtract, [], [R_LS], s2=lam_init, op1=ALU.add)
                TS(LS[:, 3:4], LS[:, 2:3], -1.0, ALU.mult, [], [R_LS])
                TS(GSUB[:, :], GSUB[:, :], 1.0 - lam_init, ALU.mult, [], [R_GSUB])
                if "lam" in dbg_out and l == dbgl:
                    dma("sp", dbg_out["lam"], LS[:, :], r=[R_LS])
                vctx_v = vctx_d.rearrange("(t p) c -> p t c", p=128)
                rcv2_v = [r_.rearrange("(t p) c -> p t c", p=128) for r_ in rcv2]
                qgroups = [(i * 512, 512, 0, NKT) for i in range(4)]
                if ctx_out:
                    qgroups.append((NLAT, 256, 0, 2))
                hsets = [("A", h) for h in range(4)] + [("B", h) for h in range(4)]
                for hi, (kind, h) in enumerate(hsets):
                    b = hi % 2
                    if kind == "A":
                        nm, dk, e, scale = 2, 64, 128, 64 ** -0.5
                        kr0, qsrc, Vb, R_Vb, vc0 = h * 128, qA_d[h * 128:(h + 1) * 128, :], V1[b], R_V1[b], h * 128
                        dkt = 128
                    else:
                        nm, dk, e, scale = 1, 96, 64, 96 ** -0.5
                        kr0, qsrc, Vb, R_Vb, vc0 = 512 + h * 96, qB_d[h * 96:(h + 1) * 96, :], V1B[b], R_V1B[b], 512 + h * 64
                        dkt = 96
                    dma("sp", KT[b][0:dkt, 0:NCTX], kctx_d[kr0:kr0 + dkt, :], r=[R_kctx], w=[R_KT[b]])
                    rsrc = rcv1A[h // 2] if kind == "A" else rcv1B[h // 2]
                    nrw = 256 if kind == "A" else 192
                    ro = (h % 2) * dkt
                    dma("sp", KT[b][0:dkt, NCTX:NCTX + NLAT], rsrc[ro:ro + dkt, :], r=[R_rcv1], w=[R_KT[b]])
                    dma("sp", KT[b][0:dkt, NCTX + NLAT:NKEY], rsrc[nrw + ro:nrw + ro + dkt, :], r=[R_rcv1], w=[R_KT[b]])
                    dma("sp", Vb[:, 0:2, 0:e], vctx_v[:, :, vc0:vc0 + e], r=[R_vctx], w=[R_Vb])
                    for rk_ in range(2):
                        for q4 in range(4):
                            kt_ = 2 + rk_ * 16 + q4 * 4
                            dma("sp", Vb[:, kt_:kt_ + 4, 0:e], rcv2_v[q4][:, rk_ * 4:(rk_ + 1) * 4, vc0:vc0 + e],
                                r=[R_rcv2], w=[R_Vb])
                    dma("sp", QT[b][0:dkt, :], qsrc, r=[R_qA, R_qB], w=[R_QT[b]])
                    wacc = e + 1
                    per_bank = 512 // wacc if kind == "A" else 4

                    def acc(qb, m):
                        i = qb * nm + m
                        return i // per_bank, (i % per_bank) * wacc

                    for (q0, nq, kt0, kt1) in qgroups:
                        nqb = nq // 128
                        nbanks = (nqb * nm + per_bank - 1) // per_bank
                        for bk in range(nbanks):
                            MM(psO[bk][:, :], ZB[:, 0:128], ZB[:, :], True, False, [R_ZB], [R_psO[bk]])
                        sidx = [0]

                        def emit_S(kt):
                            lst = []
                            for m in range(nm):
                                si = sidx[0] % 4
                                sidx[0] += 1
                                MM(psS[si][:, 0:nq], KT[b][64 * m:64 * m + dk, kt * 128:(kt + 1) * 128],
                                   QT[b][64 * m:64 * m + dk, q0:q0 + nq], True, True, [R_KT[b], R_QT[b]], [R_psS[si]])
                                lst.append(si)
                            return lst

                        cur = emit_S(kt0)
                        for kt in range(kt0, kt1):
                            nxt = emit_S(kt + 1) if kt + 1 < kt1 else None
                            for m in range(nm):
                                si = cur[m]
                                ACT(PT[si][:, 0:nq], psS[si][:, 0:nq], AF.Exp, [R_psS[si]], [R_PT[si]], scale=scale)
                            for m in range(nm):
                                si = cur[m]
                                for qb in range(nqb):
                                    bk, off = acc(qb, m)
                                    MM(psO[bk][:, off:off + wacc], PT[si][:, qb * 128:(qb + 1) * 128], Vb[:, kt, :],
                                       False, kt == kt1 - 1, [R_PT[si], R_Vb], [R_psO[bk]])
                            cur = nxt
                        for bk in range(nbanks):
                            CP(OCP[:, bk, :], psO[bk][:, :], [R_psO[bk]], [R_OCP], eng="act" if bk % 2 else "dve")
                        tl = [(q0 + qb * 128) // 128 for qb in range(nqb)]
                        if kind == "A":
                            A0 = [acc(qb, 0) for qb in range(nqb)]
                            A1 = [acc(qb, 1) for qb in range(nqb)]
                            for qb in range(nqb):
                                b0, o0 = A0[qb]
                                RECIP(RCP[qb][:, 0:1], OCP[:, b0, o0 + 128:o0 + 129], [R_OCP], [R_RCP[qb]])
                            for qb in range(nqb):
                                b1, o1 = A1[qb]
                                RECIP(RCP[qb][:, 1:2], OCP[:, b1, o1 + 128:o1 + 129], [R_OCP], [R_RCP[qb]])
                            for qb in range(nqb):
                                TT(RCP[qb][:, 1:2], RCP[qb][:, 1:2], LS[:, 3:4], ALU.mult, [R_LS], [R_RCP[qb]])
                            for qb in range(nqb):
                                b0, o0 = A0[qb]
                                TS(O1[qb][:, :], OCP[:, b0, o0:o0 + 128], RCP[qb][:, 0:1], ALU.mult, [R_OCP, R_RCP[qb]], [R_O1[qb]])
                            for qb in range(nqb):
                                b1, o1 = A1[qb]
                                STT(O1[qb][:, :], OCP[:, b1, o1:o1 + 128], RCP[qb][:, 1:2], O1[qb][:, :], ALU.mult, ALU.add,
                                    [R_OCP, R_RCP[qb]], [R_O1[qb]])
                            for qb in range(nqb):
                                ACT(JK[:, :], O1[qb][:, :], AF.Square, [R_O1[qb]], [R_JK, R_RCP[qb]], accum_out=RCP[qb][:, 2:3])
                            for qb in range(nqb):
                                ACT(RCP[qb][:, 2:3], RCP[qb][:, 2:3], AF.Sqrt, [R_CF], [R_RCP[qb]], bias=EPS_T[SUBLN_EPS], scale=1.0 / 128)
                            for qb in range(nqb):
                                RECIP(RCP[qb][:, 2:3], RCP[qb][:, 2:3], [], [R_RCP[qb]])
                            for qb in range(nqb):
                                STT(OAH[b][:, tl[qb], :], O1[qb][:, :], RCP[qb][:, 2:3], GSUB[:, :], ALU.mult, ALU.mult,
                                    [R_O1[qb], R_RCP[qb], R_GSUB], [R_OAH[b]])
                        else:
                            for qb in range(nqb):
                                b0, o0 = acc(qb, 0)
                                RECIP(RCP[qb][:, 0:1], OCP[:, b0, o0 + 64:o0 + 65], [R_OCP], [R_RCP[qb]])
                            for qb in range(nqb):
                                b0, o0 = acc(qb, 0)
                                TS(OAH[b][:, tl[qb], 0:64], OCP[:, b0, o0:o0 + 64], RCP[qb][:, 0:1], ALU.mult,
                                   [R_OCP, R_RCP[qb]], [R_OAH[b]])
                    nto = len(tiles_out)
                    dst = oall_d.rearrange("(t p) c -> p t c", p=128)
                    if kind == "A":
                        dma("sp", dst[:, 0:nto, h * 128:(h + 1) * 128], OAH[b][:, 0:nto, :], r=[R_OAH[b]], w=[R_oall])
                    else:
                        dma("sp", dst[:, 0:nto, 512 + h * 64:512 + (h + 1) * 64], OAH[b][:, 0:nto, 0:64], r=[R_OAH[b]], w=[R_oall])
                fw.barrier()
                fw.flush()

        if do_rwkv:
            RWKV_PHASE(l, ctx_out, tiles_out)

        with ExitStack() as st:
            if not do_merge:
                break
            OT = sb(st, "OT", [128, 8, NTOK], BF16)
            WO = sb(st, "WO", [128, 8, D], BF16)
            OTI = [sb(st, "OTI%d" % i, [128, D], BF16) for i in range(2)]
            G1 = [sb(st, "G1_%d" % i, [128, D]) for i in range(2)]
            TM = [sb(st, "TM%d" % i, [128, D]) for i in range(2)]
            JM = sb(st, "JM", [128, 512])
            SS = [sb(st, "SSm%d" % i, [128, 2]) for i in range(2)]
            psT = [pst(st, "psTb%d" % i, [128, 1024], BF16) for i in range(2)]
            psM = [pst(st, "psM%d" % i, [128, 512]) for i in range(4)]
            R_OT, R_WO, R_JM = fw.res(), fw.res(), fw.res()
            R_OTI, R_G1, R_TM, R_SS = [[fw.res(), fw.res()] for _ in range(4)]
            R_psT = [fw.pres(), fw.pres()]
            R_psM = [fw.pres() for _ in range(4)]
            wov = w_out[l].rearrange("(kc p) n -> p kc n", p=128)
            for kc in range(8):
                dma("pool", WO[:, kc, :], wov[:, kc, :], w=[R_WO])
            for s in range(2):
                gr = (l * 2 + s) * 2 + 0
                dma("sp", G1[s][:, :], grow[gr, :].partition_broadcast(128), r=[R_grow], w=[R_G1[s]])
            for n_, t in enumerate(tiles_out):
                b = n_ % 2
                dma("sp", OTI[b][:, :], oall_d[t * 128:(t + 1) * 128, :], r=[R_oall], w=[R_OTI[b]])
                for fc in range(8):
                    TR(psT[b][:, fc * 128:(fc + 1) * 128], OTI[b][:, fc * 128:(fc + 1) * 128], IDB[:, :],
                       [R_OTI[b], R_IDB], [R_psT[b]])
                if "oall" in dbg_out and l == dbgl:
                    CP(TM[b][:, :], OTI[b][:, :], [R_OTI[b]], [R_TM[b]])
                    dma("sp", dbg_out["oall"][t * 128:(t + 1) * 128, :], TM[b][:, :], r=[R_TM[b]])
                CP(OT[:, :, t * 128:(t + 1) * 128], psT[b][:, :].rearrange("p (a b) -> p a b", a=8), [R_psT[b]], [R_OT],
                   eng="act" if n_ % 2 else "dve")
            for n_, t in enumerate(tiles_out):
                b = n_ % 2
                s = 0 if t < 16 else 1
                for hf in range(2):
                    pm = b * 2 + hf
                    for kc in range(8):
                        MM(psM[pm][:, :], OT[:, kc, t * 128:(t + 1) * 128], WO[:, kc, hf * 512:(hf + 1) * 512], kc == 0, kc == 7,
                           [R_OT, R_WO], [R_psM[pm]])
                    ACT(JM[:, :], psM[pm][:, :], AF.Square, [R_psM[pm]], [R_JM, R_SS[b]], accum_out=SS[b][:, hf:hf + 1])
                    TT(TM[b][:, hf * 512:(hf + 1) * 512], psM[pm][:, :], G1[s][:, hf * 512:(hf + 1) * 512], ALU.mult,
                       [R_psM[pm], R_G1[s]], [R_TM[b]])
                TT(SS[b][:, 0:1], SS[b][:, 0:1], SS[b][:, 1:2], ALU.add, [], [R_SS[b]])
                rstd_from_ss(SS[b][:, 0:1], R_SS[b], D, NORM_EPS)
                STT(X[:, t, :], TM[b][:, :], SS[b][:, 0:1], X[:, t, :], ALU.mult, ALU.add, [R_TM[b], R_SS[b]], [R_X[t]])
            fw.barrier()
            fw.flush()
        if "xm" in dbg_out and l == dbgl:
            for t in range(NT):
                dma("sp", dbg_out["xm"][t * 128:(t + 1) * 128, :], X[:, t, :], r=[R_X[t]])

        if do_ffn:
            with ExitStack() as st:
                H2T = sb(st, "H2T", [128, 8, NCOL], BF16)
                R_H2T = fw.res()
                psF = [pst(st, "psF%d" % i, [128, 512]) for i in range(8)]
                R_psF = [fw.pres() for _ in range(8)]
                with ExitStack() as st2:
                    XS = [sb(st2, "XSf%d" % i, [128, D]) for i in range(2)]
                    R_XS = [fw.res(), fw.res()]
                    RSTD = sb(st2, "RSTDf", [128, NT])
                    R_RSTD = fw.res()
                    norm_transpose(l, 2, tiles_out, H2T, R_H2T, psF[0:4], R_psF[0:4], XS, R_XS, RSTD, R_RSTD, XS[1], R_XS[1])
                    fw.barrier()
                    fw.flush()
                R_H2Th = fw.res()
                halo_exchange(st, H2T, R_H2T, R_H2Th, snd5, rcv5, R_snd5, R_rcv5, l, "f")
                ACTT = sb(st, "ACTTf", [128, NCH, 512], BF16)
                WU = [sb(st, "WU%d" % i, [128, 8, 256], BF16) for i in range(3)]
                WD = [sb(st, "WD%d" % i, [128, D], BF16) for i in range(4)]
                US = [sb(st, "US%d" % i, [128, 2, 514]) for i in range(2)]
                CV = [sb(st, "CV%d" % i, [128, 2, 512]) for i in range(2)]
                SGf = [sb(st, "SGf%d" % i, [128, 512]) for i in range(2)]
                G2 = [sb(st, "G2_%d" % i, [128, D]) for i in range(2)]
                TM = [sb(st, "TMf%d" % i, [128, D]) for i in range(2)]
                JM = sb(st, "JMf", [128, 512])
                SS = [sb(st, "SSf%d" % i, [128, 2]) for i in range(2)]
                R_ACTT, R_JM = fw.res(), fw.res()
                R_WU = [fw.res() for _ in range(3)]
                R_WD = [fw.res() for _ in range(4)]
                R_US, R_CV, R_SGf, R_G2, R_TM, R_SS = [[fw.res(), fw.res()] for _ in range(6)]
                for s in range(2):
                    gr = (l * 2 + s) * 2 + 1
                    dma("sp", G2[s][:, :], grow[gr, :].partition_broadcast(128), r=[R_grow], w=[R_G2[s]])
                wupv = wup_bf.rearrange("(kc p) n -> p kc n", p=128)
                blocks = [(i * 512, 512, 0, NLAT + 1, i * 4) for i in range(4)]
                if ctx_out:
                    blocks.append((CTX0, 256, CTX0, CTX0 + 256, 16))
                nwu = 0
                nwd = 0
                for (c0, n, seq0, seq1, tile0) in blocks:
                    lo = max(c0 - 1, seq0)
                    hi = min(c0 + n + 1, seq1)
                    off0 = lo - (c0 - 1)
                    len1 = min(512, hi - lo)
                    len2 = (hi - lo) - len1
                    for cc in range(NCH):
                        wb = nwu % 3
                        ub = nwu % 2
                        nwu += 1
                        dma("sp", WU[wb][:, :, 0:128], wupv[:, :, cc * 128:(cc + 1) * 128], r=[R_wup], w=[R_WU[wb]])
                        dma("sp", WU[wb][:, :, 128:256], wupv[:, :, (NCH + cc) * 128:(NCH + cc + 1) * 128], r=[R_wup], w=[R_WU[wb]])
                        for gv in range(2):
                            pa = (ub * 2 + gv) * 2
                            pbk = pa + 1
                            hx = [R_H2Th] if (lo <= HALO < hi) else []
                            for kc in range(8):
                                MM(psF[pa][:, 0:len1], WU[wb][:, kc, gv * 128:(gv + 1) * 128], H2T[:, kc, lo:lo + len1],
                                   kc == 0, kc == 7, [R_WU[wb], R_H2T] + hx, [R_psF[pa]])
                            CP(US[ub][:, gv, off0:off0 + len1], psF[pa][:, 0:len1], [R_psF[pa]], [R_US[ub]], eng="act")
                            if len2 > 0:
                                for kc in range(8):
                                    MM(psF[pbk][:, 0:len2], WU[wb][:, kc, gv * 128:(gv + 1) * 128],
                                       H2T[:, kc, lo + len1:lo + len1 + len2], kc == 0, kc == 7, [R_WU[wb], R_H2T] + hx, [R_psF[pbk]])
                                CP(US[ub][:, gv, off0 + len1:off0 + len1 + len2], psF[pbk][:, 0:len2], [R_psF[pbk]], [R_US[ub]],
                                   eng="act")
                            if off0 > 0:
                                MSET(US[ub][:, gv, 0:1], 0.0, [R_US[ub]])
                            if off0 + len1 + len2 < n + 2:
                                MSET(US[ub][:, gv, n + 1:n + 2], 0.0, [R_US[ub]])
                            ci = gv * NCH + cc
                            eng = "dve" if gv == 0 else "pool"
                            TS(CV[ub][:, gv, 0:n], US[ub][:, gv, 1:n + 1], pc(l, "cw1", ci, ci + 1), ALU.mult, [R_US[ub], R_PCS],
                               [R_CV[ub]], s2=pc(l, "cb", ci, ci + 1), op1=ALU.add, eng=eng)
                            STT(CV[ub][:, gv, 0:n], US[ub][:, gv, 0:n], pc(l, "cw0", ci, ci + 1), CV[ub][:, gv, 0:n], ALU.mult, ALU.add,
                                [R_US[ub], R_PCS], [R_CV[ub]], eng=eng)
                            STT(CV[ub][:, gv, 0:n], US[ub][:, gv, 2:n + 2], pc(l, "cw2", ci, ci + 1), CV[ub][:, gv, 0:n], ALU.mult, ALU.add,
                                [R_US[ub], R_PCS], [R_CV[ub]], eng=eng)
                        ACT(SGf[ub][:, 0:n], CV[ub][:, 0, 0:n], AF.Silu, [R_CV[ub]], [R_SGf[ub]])
                        TT(ACTT[:, cc, 0:n], SGf[ub][:, 0:n], CV[ub][:, 1, 0:n], ALU.mult, [R_SGf[ub], R_CV[ub]], [R_ACTT])
                    ntl = n // 128
                    for cc in range(NCH):
                        wd = nwd % 4
                        nwd += 1
                        dma("sp", WD[wd][:, :], wdn_bf[cc * 128:(cc + 1) * 128, :], r=[R_wdn], w=[R_WD[wd]])
                        for tt in range(ntl):
                            for hf in range(2):
                                pm = tt * 2 + hf
                                MM(psF[pm][:, :], ACTT[:, cc, tt * 128:(tt + 1) * 128], WD[wd][:, hf * 512:(hf + 1) * 512],
                                   cc == 0, cc == NCH - 1, [R_ACTT, R_WD[wd]], [R_psF[pm]])
                    for tt in range(ntl):
                        t = tile0 + tt
                        b = tt % 2
                        s = 0 if t < 16 else 1
                        for hf in range(2):
                            pm = tt * 2 + hf
                            ACT(JM[:, :], psF[pm][:, :], AF.Square, [R_psF[pm]], [R_JM, R_SS[b]], accum_out=SS[b][:, hf:hf + 1])
                            TT(TM[b][:, hf * 512:(hf + 1) * 512], psF[pm][:, :], G2[s][:, hf * 512:(hf + 1) * 512], ALU.mult,
                               [R_psF[pm], R_G2[s]], [R_TM[b]])
                        TT(SS[b][:, 0:1], SS[b][:, 0:1], SS[b][:, 1:2], ALU.add, [], [R_SS[b]])
                        rstd_from_ss(SS[b][:, 0:1], R_SS[b], D, NORM_EPS)
                        STT(X[:, t, :], TM[b][:, :], SS[b][:, 0:1], X[:, t, :], ALU.mult, ALU.add, [R_TM[b], R_SS[b]], [R_X[t]])
                fw.barrier()
                fw.flush()
        if "xo" in dbg_out and l == dbgl:
            for t in range(NT):
                dma("sp", dbg_out["xo"][t * 128:(t + 1) * 128, :], X[:, t, :], r=[R_X[t]])

    yv = yout.rearrange("(t p) f -> p t f", p=128)
    for t in range(16):
        dma("sp", yv[:, t, :], X[:, t, :], r=[R_X[t]])
    fw.barrier()
    fw.flush()
    glob.close()
    return nc, fw


def _consts():
    cf = np.zeros((128, NCF), np.float32)
    o = CF_OFF
    cf[:, o["ident"][0]:o["ident"][0] + 128] = np.eye(128, dtype=np.float32)
    cf[:, o["ones"][0]:o["ones"][0] + 128] = 1.0
    blk = np.zeros((128, 128), np.float32)
    blk[0:64, 0:64] = 1.0
    blk[64:128, 64:128] = 1.0
    cf[:, o["blk"][0]:o["blk"][0] + 128] = blk
    RA = np.zeros((128, 128), np.float32)
    for m in range(128):
        if (m % 32) < 16:
            RA[m + 16, m] = -1.0
        else:
            RA[m - 16, m] = 1.0
    cf[:, o["RA"][0]:o["RA"][0] + 128] = RA
    RBm = np.zeros((128, 128), np.float32)
    for m in range(64, 96):
        e = m - 64
        if (e % 16) < 8:
            RBm[m + 8, m] = -1.0
        else:
            RBm[m - 8, m] = 1.0
    cf[:, o["RB"][0]:o["RB"][0] + 128] = RBm
    RK = np.zeros((128, 32), np.float32)
    for m in range(32):
        if (m % 16) < 8:
            RK[m + 8, m] = -1.0
        else:
            RK[m - 8, m] = 1.0
    cf[:, o["RK"][0]:o["RK"][0] + 32] = RK
    cf[0:64, o["hsel"][0]] = 1.0
    cf[64:128, o["hsel"][0] + 1] = 1.0
    cr = np.zeros((128, NCR), np.float32)
    j = np.arange(128)[:, None]
    i = np.arange(128)[None, :]
    same = (j // 64) == (i // 64)
    r = CR_OFF
    cr[:, r["Us"][0]:r["Us"][0] + 128] = (same & (i > j))
    cr[:, r["Ui"][0]:r["Ui"][0] + 128] = (same & (i >= j))
    cr[:, r["Ls"][0]:r["Ls"][0] + 128] = (same & (i < j))
    cr[:, r["Li"][0]:r["Li"][0] + 128] = (same & (i <= j))
    tf = np.concatenate([(same & (j <= i)), (same & (j < i)), (same & (j > i))], axis=1).astype(np.float32) * DECAY_C
    tb = np.concatenate([(same & (j >= i)), (same & (j > i)), (same & (j < i))], axis=1).astype(np.float32) * DECAY_C
    cr[:, r["trif"][0]:r["trif"][0] + 384] = tf
    cr[:, r["trib"][0]:r["trib"][0] + 384] = tb
    return cf, cr


def _rope_tables(s):
    li = np.arange(NLAT)
    tt = li if s == 0 else (4095 - li)
    rows = (tt // 64).astype(np.float64)
    cols = (tt % 64).astype(np.float64)
    ropeA = np.zeros((2, 128, NCOL), np.float32)
    ropeA[0] = 1.0
    invA = 10000.0 ** (-np.arange(16, dtype=np.float32) / 16)
    for p in range(128):
        dd = p % 64
        a, f = dd // 32, dd % 16
        pos = rows if a == 0 else cols
        ang = (pos.astype(np.float32) * invA[f]).astype(np.float32)
        ropeA[0, p, 0:NLAT] = np.cos(ang)
        ropeA[1, p, 0:NLAT] = np.sin(ang)
    ropeB = np.zeros((2, 96, NCOL), np.float32)
    ropeB[0] = 1.0
    invB = 10000.0 ** (-np.arange(8, dtype=np.float32) / 8)
    for e in range(32):
        a, f = e // 16, e % 8
        pos = rows if a == 0 else cols
        ang = (pos.astype(np.float32) * invB[f]).astype(np.float32)
        ropeB[0, 64 + e, 0:NLAT] = np.cos(ang)
        ropeB[1, 64 + e, 0:NLAT] = np.sin(ang)
    return ropeA, ropeB


def _pcol(v, n):
    return np.ascontiguousarray(np.asarray(v, np.float32).reshape(n, 128).T)


def _pack_core(inp, c):
    b, s = c // 2, c % 2
    f32 = np.float32
    xl = inp["x"][b, s * NLAT:(s + 1) * NLAT]
    cx = inp["ctx"][b]
    if s == 1:
        xl = xl[::-1]
        cx = cx[::-1]
    m = {}
    m["xin"] = np.ascontiguousarray(np.concatenate([xl, cx], 0), dtype=f32)
    m["cvec"] = np.ascontiguousarray(np.concatenate([_pcol(inp["c"][b], 8), _pcol(inp["c_ctx"], 8)], 1))
    m["ada_w"] = np.ascontiguousarray(inp["ada_w"], dtype=f32)
    w_in = np.array(inp["w_in"], dtype=f32)
    mup = np.array(inp["c_mu_prev"], dtype=f32)
    mun = np.array(inp["c_mu_next"], dtype=f32)
    dirs = (0, 1) if s == 0 else (1, 0)
    if s == 1:
        perm = np.arange(1152)
        perm[768:832], perm[832:896] = np.arange(832, 896), np.arange(768, 832)
        perm[896:960], perm[960:1024] = np.arange(960, 1024), np.arange(896, 960)
        w_in[:, :, 1888:3040] = w_in[:, :, 1888 + perm]
        mup, mun = mun[:, perm], mup[:, perm]
    m["w_in"] = np.ascontiguousarray(w_in)
    m["w_out"] = np.ascontiguousarray(inp["w_out"], dtype=f32)
    m["ffn_up"] = np.ascontiguousarray(inp["ffn_w_up"], dtype=f32)
    m["ffn_dn"] = np.ascontiguousarray(inp["ffn_w_down"], dtype=f32)
    m["wq_up"] = np.ascontiguousarray(inp["b_w_q_up"], dtype=f32)
    wkv = np.asarray(inp["b_w_kv_up"], f32).reshape(L, 128, 4, 2, 64)
    m["wkv_up"] = np.ascontiguousarray(np.concatenate([wkv[:, :, :, 0, :].reshape(L, 128, 256),
                                                       wkv[:, :, :, 1, :].reshape(L, 128, 256)], axis=2))
    m["cw2"] = np.ascontiguousarray(np.concatenate([inp["c_w2"][:, dirs[0]], inp["c_w2"][:, dirs[1]]], axis=1), dtype=f32)
    m["ca2"] = np.ascontiguousarray(np.concatenate([inp["c_a2"][:, dirs[0]], inp["c_a2"][:, dirs[1]]], axis=1), dtype=f32)
    m["cg2"] = np.ascontiguousarray(inp["c_g2"], dtype=f32)
    pcs = np.zeros((L, 128, NPC), f32)
    rows = np.zeros((L, NROW), f32)
    cw = np.array(inp["ffn_conv_w"], dtype=f32)
    if s == 1:
        cw = cw[:, ::-1, :]
    for l in range(L):
        def put(name, arr):
            o, w = PC_OFF[name]
            assert arr.shape == (128, w), (name, arr.shape)
            pcs[l, :, o:o + w] = arr
        ab = inp["ada_b"][l]
        put("adab", np.concatenate([_pcol(ab[0:1024], 8), _pcol(ab[1024:2048], 8), _pcol(ab[3072:4096], 8),
                                    _pcol(ab[4096:5120], 8)], 1))
        put("preg1", _pcol(inp["mix_pre_g"][l], 8))
        put("preg2", _pcol(inp["ffn_pre_g"][l], 8))
        qn = np.zeros((128, 2), f32)
        qn[:, 0] = inp["b_q_norm_g"][l][0:128]
        qn[0:64, 1] = inp["b_q_norm_g"][l][128:192]
        put("qng", qn)
        put("kvng", np.asarray(inp["b_kv_norm_g"][l], f32).reshape(128, 1))
        put("mup", _pcol(mup[l], 9))
        put("mun", _pcol(mun[l], 9))
        put("a0", np.concatenate([_pcol(inp["c_a0"][l][dirs[0]], 2), _pcol(inp["c_a0"][l][dirs[1]], 2)], 1))
        put("kk", _pcol(inp["c_k_k"][l], 2))
        put("ka", _pcol(inp["c_k_a"][l], 2))
        put("rk", _pcol(np.asarray(inp["c_r_k"][l]).reshape(256), 2))
        put("cw0", _pcol(cw[l, 0], 44))
        put("cw1", _pcol(cw[l, 1], 44))
        put("cw2", _pcol(cw[l, 2], 44))
        put("cb", _pcol(inp["ffn_conv_b"][l], 44))
        sel = np.zeros((128, 2), f32)
        sel[:, 1 - s] = 1.0
        put("sel", sel)

        def putr(name, arr):
            o, w = ROW_OFF[name]
            arr = np.asarray(arr, f32).reshape(-1)
            assert arr.shape[0] == w, (name, arr.shape)
            rows[l, o:o + w] = arr
        putr("adab_g1", ab[2048:3072])
        putr("adab_g2", ab[5120:6144])
        putr("postg1", inp["mix_post_g"][l])
        putr("postg2", inp["ffn_post_g"][l])
        putr("lam", np.concatenate([inp["lam_q1"][l], inp["lam_k1"][l], inp["lam_q2"][l], inp["lam_k2"][l]]))
        putr("subln", inp["a_subln_g"][l])
        putr("w0", np.concatenate([inp["c_w0"][l][dirs[0]], inp["c_w0"][l][dirs[1]]]))
        putr("gng", inp["c_gn_g"][l])
        putr("gnb", inp["c_gn_b"][l])
    m["pc"] = pcs
    m["row"] = rows
    cf_, cr_ = _consts()
    m["cf"] = cf_
    m["cr"] = cr_
    ra, rb = _rope_tables(s)
    m["ropeA"] = ra
    m["ropeB"] = rb
    return m


_CACHE = {}


def kernel(**inputs):
    inp = {k: np.asarray(v) for k, v in inputs.items()}
    if "nc" not in _CACHE:
        _CACHE["nc"] = build_program()[0]
    nc = _CACHE["nc"]
    in_maps = [_pack_core(inp, c) for c in range(8)]
    res = run_bass_kernel_spmd(nc, in_maps, core_ids=list(range(8)))
    out = np.zeros((4, 4096, D), np.float32)
    for c in range(8):
        b, s = c // 2, c % 2
        y = np.asarray(res.results[c]["yout"], dtype=np.float32)
        if s == 1:
            y = y[::-1]
        out[b, s * NLAT:(s + 1) * NLAT] = y
    return out
```
